# Optimizing a Trainium2 kernel written in Bass

```python
import jax, jax.numpy as jnp
from jax import lax
import numpy as np

D_MODEL = 2048
BATCH = 2
SEQ = 4096
DEPTH = 4
DEC_BATCH = 32
DEC_SEQ = 1
PAST_LEN = 16384
PAGE_SIZE = 128

N_MIXERS = 2
N_META = 16
POOL_WINDOWS = (2, 4, 8, 16)
N_POOL_GROUPS = len(POOL_WINDOWS)
POOL_GROUP = D_MODEL // N_POOL_GROUPS
POOL_STATE = max(POOL_WINDOWS) - 1
HEAD_DIM = 64
N_HEADS = D_MODEL // HEAD_DIM
N_KV_HEADS = 4
GROUP = N_HEADS // N_KV_HEADS
WINDOW = 128
BLOCK = 128
ROT_DIM = HEAD_DIM // 4
ROPE_THETA = 500000.0
D_FF = -(-8 * D_MODEL // (3 * 256)) * 256
N_POOL_LAYERS = (DEPTH + N_MIXERS - 1) // N_MIXERS
N_SWA_LAYERS = DEPTH // N_MIXERS
EPS = 1e-6
NEG = -1e30

kernel_name = "pool_swa_sink_hybrid_step"


def rms_norm(x, g):
    xf = x.astype(jnp.float32)
    y = xf * lax.rsqrt(jnp.mean(xf * xf, axis=-1, keepdims=True) + EPS)
    return (y * g.astype(jnp.float32)).astype(x.dtype)


def rope(x, pos):
    half = ROT_DIM // 2
    inv = jnp.float32(ROPE_THETA) ** (-jnp.arange(half, dtype=jnp.float32) * 2.0 / ROT_DIM)
    ang = pos.astype(jnp.float32)[:, None] * inv[None, :]
    cos = jnp.cos(ang)[:, None, :]
    sin = jnp.sin(ang)[:, None, :]
    xr = x[..., :ROT_DIM].astype(jnp.float32)
    x1, x2 = xr[..., :half], xr[..., half:]
    rot = jnp.concatenate([x1 * cos - x2 * sin, x2 * cos + x1 * sin], axis=-1).astype(x.dtype)
    return jnp.concatenate([rot, x[..., ROT_DIM:]], axis=-1)


def pool_mixer(h, prev, w_grp, scale):
    B, T, _ = h.shape
    P = prev.shape[1]
    ext = jnp.concatenate([prev.astype(h.dtype), h], axis=1)
    cs = jnp.cumsum(ext.astype(jnp.float32), axis=1)
    cs = jnp.concatenate([jnp.zeros((B, 1, D_MODEL), jnp.float32), cs], axis=1)
    end = P + 1 + jnp.arange(T)
    outs = []
    for g, w in enumerate(POOL_WINDOWS):
        start = jnp.maximum(end - w, 0)
        c0, c1 = g * POOL_GROUP, (g + 1) * POOL_GROUP
        s = cs[:, end, c0:c1] - cs[:, start, c0:c1]
        cnt = (end - start).astype(jnp.float32)
        outs.append(s / cnt[None, :, None])
    pooled = jnp.stack(outs, axis=2)
    diff = (pooled - h.reshape(B, T, N_POOL_GROUPS, POOL_GROUP).astype(jnp.float32)).astype(h.dtype)
    y = jnp.einsum('btgc,gcd->btgd', diff, w_grp).reshape(B, T, D_MODEL)
    return y * scale, ext[:, -POOL_STATE:]


def qkv_project(h, w_qkv, q_g, k_g, pos):
    B, L, _ = h.shape
    qkv = h @ w_qkv
    q, k, v = jnp.split(qkv, [N_HEADS * HEAD_DIM, (N_HEADS + N_KV_HEADS) * HEAD_DIM], axis=-1)
    q = q.reshape(B, L, N_HEADS, HEAD_DIM)
    k = k.reshape(B, L, N_KV_HEADS, HEAD_DIM)
    v = v.reshape(B, L, N_KV_HEADS, HEAD_DIM)
    q = rope(rms_norm(q, q_g), pos)
    k = rope(rms_norm(k, k_g), pos)
    return q, k, v


def sink_attention(q, k, v, mask, sinks):
    s = jnp.einsum('...qhgd,...khd->...hgqk', q, k).astype(jnp.float32) * (HEAD_DIM ** -0.5)
    s = jnp.where(mask, s, NEG)
    sk = sinks.astype(jnp.float32).reshape(N_KV_HEADS, GROUP, 1, 1)
    m = jnp.maximum(jnp.max(s, axis=-1, keepdims=True), sk)
    p = jnp.exp(s - m)
    p = p / (jnp.sum(p, axis=-1, keepdims=True) + jnp.exp(sk - m))
    return jnp.einsum('...hgqk,...khd->...qhgd', p.astype(v.dtype), v)


def swa_prompt(h, w_qkv, w_o, q_g, k_g, sinks):
    B, L, _ = h.shape
    q, k, v = qkv_project(h, w_qkv, q_g, k_g, jnp.arange(L))
    pad = (-L) % BLOCK
    nb = (L + pad) // BLOCK
    qb = jnp.pad(q, ((0, 0), (pad, 0), (0, 0), (0, 0))).reshape(B, nb, BLOCK, N_KV_HEADS, GROUP, HEAD_DIM)

    def band(t):
        tb = jnp.pad(t, ((0, 0), (pad + BLOCK, 0), (0, 0), (0, 0))).reshape(B, nb + 1, BLOCK, N_KV_HEADS, HEAD_DIM)
        return jnp.concatenate([tb[:, :-1], tb[:, 1:]], axis=2)

    kb, vb = band(k), band(v)
    qpos = jnp.arange(nb)[:, None] * BLOCK + jnp.arange(BLOCK)[None, :] - pad
    kpos = jnp.arange(nb)[:, None] * BLOCK + jnp.arange(2 * BLOCK)[None, :] - BLOCK - pad
    d = qpos[:, :, None] - kpos[:, None, :]
    mask = ((kpos[:, None, :] >= 0) & (d >= 0) & (d < WINDOW))[:, None, None]
    o = sink_attention(qb, kb, vb, mask, sinks).reshape(B, nb * BLOCK, N_HEADS * HEAD_DIM)[:, pad:]
    return o @ w_o, k[:, -WINDOW:], v[:, -WINDOW:]


def swa_sample(h, ck, cv, w_qkv, w_o, q_g, k_g, sinks):
    B, T, _ = h.shape
    pos = PAST_LEN + jnp.arange(T)
    q, k, v = qkv_project(h, w_qkv, q_g, k_g, pos)
    ka = jnp.concatenate([ck.astype(k.dtype), k], axis=1)
    va = jnp.concatenate([cv.astype(v.dtype), v], axis=1)
    kpos = PAST_LEN - WINDOW + jnp.arange(WINDOW + T)
    d = pos[:, None] - kpos[None, :]
    mask = ((kpos[None, :] >= 0) & (d >= 0) & (d < WINDOW))[None, None]
    o = sink_attention(q.reshape(B, T, N_KV_HEADS, GROUP, HEAD_DIM), ka, va, mask, sinks)
    o = o.reshape(B, T, N_HEADS * HEAD_DIM)
    return o @ w_o, ka[:, -WINDOW:], va[:, -WINDOW:]


def swiglu(h, wg, wu, wd):
    return (jax.nn.silu(h @ wg) * (h @ wu)) @ wd


def setup_inputs(seed: int = 0) -> dict:
    key = jax.random.key(seed)
    ks = jax.random.split(key, 20)
    f32 = jnp.float32
    nrm = lambda k, s: jax.random.normal(k, s, f32)
    QKV = (N_HEADS + 2 * N_KV_HEADS) * HEAD_DIM
    return {
        "x_prompt": nrm(ks[0], (BATCH, SEQ, D_MODEL)),
        "x_sample": nrm(ks[1], (DEC_BATCH, DEC_SEQ, D_MODEL)),
        "state_pool": nrm(ks[2], (N_POOL_LAYERS, DEC_BATCH, POOL_STATE, D_MODEL)),
        "cache_k": nrm(ks[3], (N_SWA_LAYERS, DEC_BATCH, WINDOW, N_KV_HEADS, HEAD_DIM)),
        "cache_v": nrm(ks[4], (N_SWA_LAYERS, DEC_BATCH, WINDOW, N_KV_HEADS, HEAD_DIM)),
        "meta_tokens": nrm(ks[5], (N_META, D_MODEL)),
        "norm_mix": 1.0 + 0.05 * nrm(ks[6], (DEPTH, D_MODEL)),
        "norm_ffn": 1.0 + 0.05 * nrm(ks[7], (DEPTH, D_MODEL)),
        "pool_w": nrm(ks[8], (N_POOL_LAYERS, N_POOL_GROUPS, POOL_GROUP, POOL_GROUP)) * POOL_GROUP ** -0.5,
        "pool_scale": 1.0 + 0.1 * nrm(ks[9], (N_POOL_LAYERS, D_MODEL)),
        "w_qkv": nrm(ks[10], (N_SWA_LAYERS, D_MODEL, QKV)) * D_MODEL ** -0.5,
        "w_o": nrm(ks[11], (N_SWA_LAYERS, N_HEADS * HEAD_DIM, D_MODEL)) * (N_HEADS * HEAD_DIM) ** -0.5,
        "q_norm": 1.0 + 0.05 * nrm(ks[12], (N_SWA_LAYERS, HEAD_DIM)),
        "k_norm": 1.0 + 0.05 * nrm(ks[13], (N_SWA_LAYERS, HEAD_DIM)),
        "sinks": 0.5 * nrm(ks[14], (N_SWA_LAYERS, N_HEADS)),
        "w_gate": nrm(ks[15], (DEPTH, D_MODEL, D_FF)) * D_MODEL ** -0.5,
        "w_up": nrm(ks[16], (DEPTH, D_MODEL, D_FF)) * D_MODEL ** -0.5,
        "w_down": nrm(ks[17], (DEPTH, D_FF, D_MODEL)) * D_FF ** -0.5,
    }


def reference(x_prompt, x_sample, state_pool, cache_k, cache_v, meta_tokens, norm_mix, norm_ffn,
              pool_w, pool_scale, w_qkv, w_o, q_norm, k_norm, sinks, w_gate, w_up, w_down):
    B = x_prompt.shape[0]
    meta = jnp.broadcast_to(meta_tokens[None].astype(x_prompt.dtype), (B, N_META, D_MODEL))
    xp = jnp.concatenate([meta, x_prompt], axis=1)
    xs = x_sample
    pool_p, pool_s, kp_l, vp_l, ks_l, vs_l = [], [], [], [], [], []
    for i in range(DEPTH):
        j = i // N_MIXERS
        hp = rms_norm(xp, norm_mix[i])
        hs = rms_norm(xs, norm_mix[i])
        if i % N_MIXERS == 0:
            yp, sp = pool_mixer(hp, hp[:, :0], pool_w[j], pool_scale[j])
            ys, ss = pool_mixer(hs, state_pool[j], pool_w[j], pool_scale[j])
            pool_p.append(sp)
            pool_s.append(ss)
        else:
            yp, kp, vp = swa_prompt(hp, w_qkv[j], w_o[j], q_norm[j], k_norm[j], sinks[j])
            ys, kn, vn = swa_sample(hs, cache_k[j], cache_v[j], w_qkv[j], w_o[j], q_norm[j], k_norm[j], sinks[j])
            kp_l.append(kp)
            vp_l.append(vp)
            ks_l.append(kn)
            vs_l.append(vn)
        xp = xp + yp
        xs = xs + ys
        xp = xp + swiglu(rms_norm(xp, norm_ffn[i]), w_gate[i], w_up[i], w_down[i])
        xs = xs + swiglu(rms_norm(xs, norm_ffn[i]), w_gate[i], w_up[i], w_down[i])
    y_prompt = xp[:, N_META:]
    return (y_prompt, xs, jnp.stack(pool_p), jnp.stack(kp_l), jnp.stack(vp_l),
            jnp.stack(pool_s), jnp.stack(ks_l), jnp.stack(vs_l))
```

```python
import numpy as np
from contextlib import ExitStack
import concourse.bass as bass
import concourse.mybir as mybir
from concourse.bass_utils import run_bass_kernel_spmd

F32 = mybir.dt.float32
BF16 = mybir.dt.bfloat16
AF = mybir.ActivationFunctionType
ALU = mybir.AluOpType

D = 2048
NCH = 16
DFF = 5632
NFC = 44
NOWN = 1028
HALO = 286
TP = NOWN + HALO
NS = 4
T = TP + NS
SEQ_EXT = 4112
PAST = 16384
NEGM = -30000.0
EPS = 1e-6
IN_L = [0, 15, 143, 158]
OUT_L = [15, 143, 158, 286]
RING_SLOTS = 4
RING_EL = 4096
LASTK0 = TP - 128
NPO = 19
FIX0 = HALO


QSTEP = 384
QWMAX = 416


def q_tiles(qs):
    tl = [(a, min(a + QSTEP, TP)) for a in range(qs, TP, QSTEP)]
    if len(tl) > 1 and (tl[-1][1] - tl[-1][0]) + NS <= QWMAX - QSTEP:
        tl[-2] = (tl[-2][0], TP)
        tl.pop()
    return tl


def split_tiles(a, b, maxw=512):
    n = b - a
    k = -(-n // maxw)
    base, rem = divmod(n, k)
    out = []
    s = a
    for i in range(k):
        w = base + (1 if i < rem else 0)
        out.append((s, s + w))
        s += w
    return out


class Slot:
    __slots__ = ("writer", "readers")

    def __init__(self):
        self.writer = None
        self.readers = {}


class KB:
    def __init__(self, nc, st):
        self.nc = nc
        self.eng = {"pe": nc.tensor, "act": nc.scalar, "dve": nc.vector, "pool": nc.gpsimd, "sp": nc.sync}
        self.sem = {k: st.enter_context(nc.semaphore("s_" + k)) for k in self.eng}
        self.cnt = {k: 0 for k in self.eng}
        self.waited = {}
        self.slots = {}
        self.ndsem = 20
        self.dsem = [st.enter_context(nc.semaphore(f"d{i}")) for i in range(self.ndsem)]
        self.dval = [0] * self.ndsem
        self.drr = 0
        self.npsem = 4
        self.psem = [st.enter_context(nc.semaphore(f"pd{i}")) for i in range(self.npsem)]
        self.pval = [0] * self.npsem
        self.prr = 0
        self.rsem = [st.enter_context(nc.semaphore(f"r{i}")) for i in range(RING_SLOTS)]
        self.rval = [0] * RING_SLOTS
        self.live_dma = []

    def slot(self, key):
        s = self.slots.get(key)
        if s is None:
            s = Slot()
            self.slots[key] = s
        return s

    def wait(self, e, tok):
        if tok is None:
            return
        kind, key, val = tok
        if kind == "e":
            if key == e and e in ("pe", "sp"):
                return
            sem = self.sem[key]
        else:
            sem = key
        wk = (e, id(sem))
        if self.waited.get(wk, 0) >= val:
            return
        self.waited[wk] = val
        self.eng[e].wait_ge(sem, val)

    def _deps(self, e, reads, writes):
        for k in reads:
            s = self.slot(k)
            self.wait(e, s.writer)
            if isinstance(k, tuple) and k[0] == "ps":
                for rk, r in s.readers.items():
                    if rk != e:
                        self.wait(e, r)
        for k in writes:
            s = self.slot(k)
            self.wait(e, s.writer)
            for r in s.readers.values():
                self.wait(e, r)

    def _mark(self, tok, reads, writes, rkey):
        for k in reads:
            self.slot(k).readers[rkey] = tok
        for k in writes:
            s = self.slot(k)
            s.writer = tok
            s.readers = {}

    def op(self, e, fn, reads=(), writes=()):
        self._deps(e, reads, writes)
        ins = fn(self.eng[e])
        self.cnt[e] += 1
        ins.then_inc(self.sem[e], 1)
        tok = ("e", e, self.cnt[e])
        self._mark(tok, reads, writes, e)
        return tok

    def mm(self, mms, reads=(), writes=()):
        self._deps("pe", reads, writes)
        ins = None
        for mmi in mms:
            o, l, r, s0, s1 = mmi[:5]
            if len(mmi) > 5 and mmi[5]:
                ins = self.nc.tensor.matmul(o, l, r, start=s0, stop=s1, skip_group_check=True)
            else:
                ins = self.nc.tensor.matmul(o, l, r, start=s0, stop=s1)
        self.cnt["pe"] += 1
        ins.then_inc(self.sem["pe"], 1)
        tok = ("e", "pe", self.cnt["pe"])
        self._mark(tok, reads, writes, "pe")
        return tok

    def dma(self, q, out, in_, reads=(), writes=(), ring=None, track=True):
        self._deps(q, reads, writes)
        if ring is None and q == "pool":
            i = self.prr
            self.prr = (self.prr + 1) % self.npsem
            sem = self.psem[i]
            if self.pval[i] > 0:
                self.wait(q, ("d", sem, self.pval[i]))
            self.pval[i] += 16
            val = self.pval[i]
        elif ring is None:
            i = self.drr
            self.drr = (self.drr + 1) % self.ndsem
            sem = self.dsem[i]
            if self.dval[i] > 0:
                self.wait(q, ("d", sem, self.dval[i]))
            self.dval[i] += 16
            val = self.dval[i]
        else:
            sem = self.rsem[ring]
            self.rval[ring] += 16
            val = self.rval[ring]
        self.eng[q].dma_start(out=out, in_=in_).then_inc(sem, 16)
        tok = ("d", sem, val)
        self._mark(tok, reads, writes, ("dma", id(sem), val))
        if track:
            self.live_dma.append(tok)
        return tok

    def sync_to_barrier(self, e):
        for t in getattr(self, "last_barrier", []):
            self.wait(e, t)

    def barrier(self, engines=("pe", "act", "dve", "sp")):
        toks = [("e", k, self.cnt[k]) for k in ("pe", "act", "dve", "pool") if self.cnt[k] > 0]
        toks += self.live_dma
        self.last_barrier = toks
        for e in engines:
            for t in toks:
                self.wait(e, t)
        self.live_dma = []
        self.slots = {k: v for k, v in self.slots.items() if isinstance(k, tuple) and k[0] == "ring"}


class WStream:
    def __init__(self, kb, ring, plan):
        self.kb = kb
        self.ring = ring
        self.plan = plan
        self.issued = 0
        self.taken = 0
        self.released = [True] * RING_SLOTS
        self.occ = [None] * RING_SLOTS

    def pump(self):
        while self.issued < len(self.plan):
            s = self.issued % RING_SLOTS
            if not self.released[s]:
                break
            kind, parts = self.plan[self.issued]
            for (oview, src) in parts:
                self.kb.dma("pool", oview(self.ring[s]), src, writes=[("ring", s)], ring=s, track=False)
            self.released[s] = False
            self.occ[s] = self.issued
            self.issued += 1

    def next(self, kind):
        self.pump()
        n = self.taken
        assert n < self.issued, f"weight stream stall at block {n} ({kind})"
        assert self.plan[n][0] == kind, (self.plan[n][0], kind)
        self.taken += 1
        s = n % RING_SLOTS
        return s, self.ring[s]

    def release(self, s):
        self.released[s] = True
        self.pump()


class _StopBuild(Exception):
    pass


def build_program(stop=99, swa_stop=99):
    nc = bass.Bass("TRN2", target_bir_lowering=False)
    _uid = [0]
    _orig_sbuf = nc.sbuf_tensor

    def sbuf_tensor(name, shape, dt):
        _uid[0] += 1
        return _orig_sbuf(f"sb{_uid[0]}_{name}", shape, dt)

    def din(name, shape):
        return nc.dram_tensor(name, list(shape), F32, kind="ExternalInput").ap()

    def dout(name, shape):
        return nc.dram_tensor(name, list(shape), F32, kind="ExternalOutput").ap()

    xT_d = din("xT", [128, NCH, T])
    cos_d = din("cosT", [128, T])
    sin_d = din("sinT", [128, T])
    kbias_d = din("kbias", [128, 2, 12])
    rcfix_d = din("rcfix", [128, 4, 16])
    vecs_d = din("vecs", [128, 296])
    cmat_d = din("cmat", [128, 4, 128])
    maskb_d = din("maskb", [128, 256])
    poolst_d = din("poolstT", [2, 128, NCH, NS, 15])
    kcT_d = din("kcT", [2, 128, NS, 2, 127])
    vc_d = din("vc", [2, 127, NS, 256])
    ck_d = din("ck", [2, NS, 128, 256])
    cv_d = din("cv", [2, NS, 128, 256])
    sp_d = din("spool", [2, NS, 15, D])
    pool_w = din("pool_w", [2, 4, 512, 512])
    w_qkv = din("w_qkv", [2, D, 2560])
    w_o = din("w_o", [2, D, D])
    w_gate = din("w_gate", [4, D, DFF])
    w_up = din("w_up", [4, D, DFF])
    w_down = din("w_down", [4, DFF, D])

    yT_o = dout("yT", [128, NCH, NOWN + NS])
    po_o = dout("poolT", [2, 128, NCH, NPO])
    kT_o = dout("knewT", [2, 128, 2, 128 + NS])
    vp_o = dout("vnew_p", [2, 128, 256])
    vs_o = dout("vnew_s", [2, 1, NS, 256])
    ks_o = dout("kold_s", [2, NS, 127, 256])
    vso_o = dout("vold_s", [2, NS, 127, 256])
    pso_o = dout("pold_s", [2, NS, 14, D])

    def v3(a, b):
        return lambda r: r[:, 0:a * b].rearrange("p (a b) -> p a b", a=a)

    plan = []
    for i in range(4):
        j = i // 2
        if i % 2 == 0:
            for g in range(4):
                src = pool_w[j, g].rearrange("(kc p) m -> p kc m", p=128)
                for hf in range(2):
                    plan.append(("pool", [(v3(4, 256), src[:, :, hf * 256:(hf + 1) * 256])]))
        else:
            wq = w_qkv[j].rearrange("(kc p) m -> p kc m", p=128)
            wo = w_o[j].rearrange("(kc p) m -> p kc m", p=128)
            plan.append(("wk", [(v3(16, 256), wq[:, :, 2048:2304])]))
            plan.append(("wv", [(v3(16, 256), wq[:, :, 2304:2560])]))
            ntq = len(q_tiles(OUT_L[i]))
            wo5 = w_o[j].rearrange("(gp u i d) m -> d gp u i m", gp=2, u=2, i=8)
            for tt in range(ntq):
                for gp in range(2):
                    for ip in range(4):
                        parts = []
                        for u in range(2):
                            for hl in range(2):
                                def ov(r, u=u, hl=hl):
                                    return r[:, 0:4096].rearrange("p (kc hl u d) -> p kc hl u d", kc=16, hl=2, u=2)[:, :, hl, u, :]
                                c0 = (16 * gp + 8 * u + 2 * ip + hl) * 64
                                parts.append((ov, wq[:, :, c0:c0 + 64]))
                        plan.append(("wq", parts))
                for mq in range(4):
                    for gp in range(2):
                        parts = []
                        for u in range(2):
                            def ov(r, u=u):
                                return r[u * 64:(u + 1) * 64, 0:4096].rearrange("p (i m) -> p i m", i=8)
                            parts.append((ov, wo5[:, gp, u, :, mq * 512:(mq + 1) * 512]))
                        plan.append(("wo", parts))
        wg = w_gate[i].rearrange("(kc p) m -> p kc m", p=128)
        wu = w_up[i].rearrange("(kc p) m -> p kc m", p=128)
        wd = w_down[i].rearrange("(fc p) m -> p fc m", p=128)
        for qd in range(4):
            f = 0
            while f < 11:
                w = 2 if f + 2 <= 11 else 1
                c0 = (qd * 11 + f) * 128
                plan.append(("wg", [(v3(16, w * 128), wg[:, :, c0:c0 + w * 128])]))
                plan.append(("wu", [(v3(16, w * 128), wu[:, :, c0:c0 + w * 128])]))
                f += w
            for mb in range(8):
                plan.append(("wd", [(v3(11, 256), wd[:, qd * 11:(qd + 1) * 11, mb * 256:(mb + 1) * 256])]))

    with ExitStack() as st:
        E = st.enter_context
        kb = KB(nc, st)
        X = E(sbuf_tensor("X", [128, NCH, T], F32))
        H = E(sbuf_tensor("H", [128, NCH, T + 2], BF16))
        ring = [E(sbuf_tensor(f"ring{s}", [128, RING_EL], BF16)) for s in range(RING_SLOTS)]
        vecs = E(sbuf_tensor("vecs", [128, 296], F32))
        esink = E(sbuf_tensor("esink", [128, 64], F32))
        cmat = E(sbuf_tensor("cmat", [128, 128], F32))
        identb = E(sbuf_tensor("identb", [128, 128], BF16))
        blk64 = E(sbuf_tensor("blk64", [128, 128], BF16))
        onesD = E(sbuf_tensor("onesD", [128, 128], BF16))
        ones1 = E(sbuf_tensor("ones1", [128, 128], BF16))
        maskb = E(sbuf_tensor("maskb", [128, 256], BF16))
        epst = E(sbuf_tensor("epst", [128, 1], F32))
        rstd = E(sbuf_tensor("rstd", [128, 512], F32))
        sq = [E(sbuf_tensor(f"sq{i}", [128, 512], BF16)) for i in range(3)]
        ps = [E(nc.psum_tensor(f"ps{i}", [128, 512], F32)) for i in range(8)]
        psn = [0]

        kb.rot = list(range(8))

        def bank():
            i = kb.rot[psn[0] % len(kb.rot)]
            psn[0] += 1
            return ps[i], ("ps", i)

        RT = cmat[:]
        gmix = lambda i, c: vecs[:, i * 16 + c: i * 16 + c + 1]
        gffn = lambda i, c: vecs[:, 64 + i * 16 + c: 64 + i * 16 + c + 1]
        pscl = lambda j, c: vecs[:, 128 + j * 16 + c: 128 + j * 16 + c + 1]
        qg = lambda j: vecs[:, 160 + j: 161 + j]
        kg = lambda j: vecs[:, 162 + j: 163 + j]

        ws = WStream(kb, ring, plan)

        kb.dma("sp", vecs[:], vecs_d[:, :], writes=["vecs"])
        kb.dma("sp", cmat[:], cmat_d[:, 1, :], writes=["cmat"])
        kb.dma("pool", identb[:], cmat_d[:, 0, :], writes=["identb"])
        kb.dma("pool", blk64[:], cmat_d[:, 2, :], writes=["blk64"])
        kb.dma("pool", maskb[:], maskb_d[:, :], writes=["maskb"])
        for c in range(NCH):
            kb.dma("sp", X[:, c, :], xT_d[:, c, :], writes=[("X", c)])
        ws.pump()
        kb.op("dve", lambda e: e.memset(epst[:], EPS), writes=["eps"])
        kb.op("dve", lambda e: e.memset(H[:, :, T:T + 2], 0.0), writes=[("H", c) for c in range(NCH)])
        kb.op("dve", lambda e: e.memset(onesD[:], 1.0 / D), writes=["onesD"])
        kb.op("dve", lambda e: e.memset(ones1[:], 1.0), writes=["ones1"])
        kb.op("act", lambda e: e.activation(out=esink[:], in_=vecs[:, 164:228], func=AF.Exp), reads=["vecs"], writes=["esink"])
        for j in range(2):
            for s in range(NS):
                kb.dma("sp", ks_o[j, s], ck_d[j, s, 1:128, :])
                kb.dma("sp", vso_o[j, s], cv_d[j, s, 1:128, :])
                kb.dma("sp", pso_o[j, s], sp_d[j, s, 1:15, :])
        kb.barrier()

        def rms_stats(cols, dst, dst_off, key="rstdbuf"):
            a, b = cols
            for (t0, t1) in split_tiles(a, b):
                w = t1 - t0
                pb, pk = bank()
                for c in range(NCH):
                    sb = sq[c % 3]
                    kb.op("act", lambda e, sb=sb, c=c: e.activation(out=sb[:, 0:w], in_=X[:, c, t0:t1], func=AF.Square),
                          reads=[("X", c)], writes=[("sq", c % 3)])
                    kb.mm([(pb[:, 0:w], onesD[:], sb[:, 0:w], c == 0, c == NCH - 1)], reads=[("sq", c % 3), "onesD"], writes=[pk])
                o0 = dst_off + t0 - a
                kb.op("act", lambda e: e.activation(out=dst[:, o0:o0 + w], in_=pb[:, 0:w], func=AF.Ln, bias=epst[:, 0:1], scale=1.0),
                      reads=[pk, "eps"], writes=[key])
                kb.op("act", lambda e: e.activation(out=dst[:, o0:o0 + w], in_=dst[:, o0:o0 + w], func=AF.Exp, scale=-0.5), reads=[key], writes=[key])

        def rmsnorm_to_H(cols, gfn, tilekeys=False):
            a, b = cols
            for (t0, t1) in split_tiles(a, b):
                w = t1 - t0
                rms_stats((t0, t1), rstd, 0)
                for c in range(NCH):
                    kb.op("dve", lambda e, c=c: e.scalar_tensor_tensor(out=H[:, c, t0:t1], in0=X[:, c, t0:t1], scalar=gfn(c),
                                                                        in1=rstd[:, 0:w], op0=ALU.mult, op1=ALU.mult),
                          reads=[("X", c), "rstdbuf", "vecs"], writes=[("H", c, t0) if tilekeys else ("H", c)])

        def ffn(i):
            a = OUT_L[i]
            tiles = split_tiles(a, T)
            rmsnorm_to_H((a, T), lambda c: gffn(i, c), tilekeys=True)
            with ExitStack() as st2:
                act = st2.enter_context(sbuf_tensor("act", [128, 11, T - 15], BF16))
                sg = [st2.enter_context(sbuf_tensor(f"sg{k}", [128, 512], BF16)) for k in range(2)]
                sgn = 0
                for qd in range(4):
                    f = 0
                    while f < 11:
                        wdt = 2 if f + 2 <= 11 else 1
                        sgk, Wg = ws.next("wg")
                        suk, Wu = ws.next("wu")
                        Wg3 = Wg[:, 0:16 * wdt * 128].rearrange("p (a b) -> p a b", a=16)
                        Wu3 = Wu[:, 0:16 * wdt * 128].rearrange("p (a b) -> p a b", a=16)
                        for fl in range(wdt):
                            for (t0, t1) in tiles:
                                w = t1 - t0
                                gb, gk = bank()
                                ub, uk = bank()
                                hreads = [("H", c, t0) for c in range(NCH)]
                                kb.mm([(gb[:, 0:w], Wg3[:, kc, fl * 128:(fl + 1) * 128], H[:, kc, t0:t1], kc == 0, kc == 15) for kc in range(16)],
                                      reads=[("ring", sgk)] + hreads, writes=[gk])
                                kb.mm([(ub[:, 0:w], Wu3[:, kc, fl * 128:(fl + 1) * 128], H[:, kc, t0:t1], kc == 0, kc == 15) for kc in range(16)],
                                      reads=[("ring", suk)] + hreads, writes=[uk])
                                sgt = sg[sgn % 2]
                                sgkey = ("sg", sgn % 2)
                                sgn += 1
                                kb.op("act", lambda e, sgt=sgt, gb=gb: e.activation(out=sgt[:, 0:w], in_=gb[:, 0:w], func=AF.Silu),
                                      reads=[gk], writes=[sgkey])
                                kb.op("dve", lambda e, sgt=sgt, ub=ub, ff=f + fl: e.tensor_tensor(out=act[:, ff, t0 - 15:t1 - 15], in0=ub[:, 0:w],
                                                                                               in1=sgt[:, 0:w], op=ALU.mult),
                                      reads=[uk, sgkey], writes=[("act", f + fl, t0)])
                        ws.release(sgk)
                        ws.release(suk)
                        f += wdt
                    for mb in range(8):
                        sdk, Wd = ws.next("wd")
                        Wd3 = Wd[:, 0:11 * 256].rearrange("p (a b) -> p a b", a=11)
                        for ml in range(2):
                            m = mb * 2 + ml
                            for (t0, t1) in tiles:
                                w = t1 - t0
                                yb, yk = bank()
                                kb.mm([(yb[:, 0:w], Wd3[:, ff, ml * 128:(ml + 1) * 128], act[:, ff, t0 - 15:t1 - 15], ff == 0, ff == 10) for ff in range(11)],
                                      reads=[("ring", sdk)] + [("act", ff, t0) for ff in range(11)], writes=[yk])
                                kb.op("dve", lambda e, yb=yb, m=m: e.tensor_tensor(out=X[:, m, t0:t1], in0=yb[:, 0:w], in1=X[:, m, t0:t1], op=ALU.add),
                                      reads=[yk, ("X", m, t0)], writes=[("X", m, t0)])
                        ws.release(sdk)
                kb.barrier()

        def pool_layer(i):
            j = i // 2
            a_in, a_out = IN_L[i], OUT_L[i]
            n_p = TP - a_in
            EW = 16 + n_p + 16 * NS
            sb0 = 16 + n_p
            with ExitStack() as st2:
                A = st2.enter_context
                rs_all = A(sbuf_tensor("rs_all", [128, T], F32))
                sets = []
                for k_ in range(2):
                    sets.append(tuple(A(sbuf_tensor(f"{nm}{k_}", [128, 16 + TP + 16 * NS], F32)) for nm in ("hf", "bA", "bB")))
                pst = A(sbuf_tensor("pst", [128, NCH, NS, 15], F32))
                PO = A(sbuf_tensor("PO", [128, NCH, NPO], F32))
                rcf = A(sbuf_tensor("rcf", [128, 4, 16], F32))
                fxs = [A(sbuf_tensor(f"fx{k_}", [128, 16], F32)) for k_ in range(2)]
                kb.dma("sp", pst[:], poolst_d[j], writes=["pst"])
                kb.dma("sp", rcf[:], rcfix_d[:, :, :], writes=["rcf"])
                for k_, eng in enumerate(("dve", "dve")):
                    for buf, nm in zip(sets[k_], ("hf", "bA", "bB")):
                        kb.op(eng, lambda e, buf=buf: e.memset(buf[:, 0:16], 0.0), writes=[nm + str(k_)])
                rms_stats((a_in, T), rs_all, a_in, "rsall")
                def chunk_gen(c):
                    g = c // 4
                    wwin = 2 << g
                    k_ = c % 2
                    eng = "dve"
                    hf, bA, bB = sets[k_]
                    fx = fxs[k_]
                    hk, fk = "hf" + str(k_), "fx" + str(k_)
                    hfs = hf[:, sb0:sb0 + 16 * NS].rearrange("p (s k) -> p s k", s=NS)
                    if eng == "dve":
                        kb.op(eng, lambda e, c=c, hf=hf: e.scalar_tensor_tensor(out=hf[:, 16:16 + n_p], in0=X[:, c, a_in:TP], scalar=gmix(i, c),
                                                                               in1=rs_all[:, a_in:TP], op0=ALU.mult, op1=ALU.mult),
                              reads=[("X", c), "vecs", "rsall"], writes=[hk])
                        yield
                        kb.op(eng, lambda e, c=c, hfs=hfs: e.scalar_tensor_tensor(out=hfs[:, :, 15], in0=X[:, c, TP:T], scalar=gmix(i, c),
                                                                                 in1=rs_all[:, TP:T], op0=ALU.mult, op1=ALU.mult),
                              reads=[("X", c), "rsall"], writes=[hk])
                    else:
                        kb.op("act", lambda e, c=c, hf=hf: e.activation(out=hf[:, 16:16 + n_p], in_=X[:, c, a_in:TP], func=AF.Copy, scale=gmix(i, c)),
                              reads=[("X", c), "vecs"], writes=[hk])
                        kb.op("act", lambda e, c=c, hfs=hfs: e.activation(out=hfs[:, :, 15], in_=X[:, c, TP:T], func=AF.Copy, scale=gmix(i, c)),
                              reads=[("X", c), "vecs"], writes=[hk])
                        kb.op(eng, lambda e, hf=hf: e.tensor_tensor(out=hf[:, 16:16 + n_p], in0=hf[:, 16:16 + n_p], in1=rs_all[:, a_in:TP], op=ALU.mult),
                              reads=[hk, "rsall"], writes=[hk])
                        kb.op(eng, lambda e, hfs=hfs: e.tensor_tensor(out=hfs[:, :, 15], in0=hfs[:, :, 15], in1=rs_all[:, TP:T], op=ALU.mult),
                              reads=[hk, "rsall"], writes=[hk])
                    kb.op("act", lambda e, c=c, hfs=hfs: e.copy(out=hfs[:, :, 0:15], in_=pst[:, c, :, :]), reads=["pst"], writes=[hk])
                    kb.op("act", lambda e, c=c, hf=hf: e.copy(out=PO[:, c, 0:15], in_=hf[:, 16 + n_p - 15:16 + n_p]), reads=[hk], writes=["PO"])
                    kb.op("act", lambda e, c=c, hfs=hfs: e.copy(out=PO[:, c, 15:NPO], in_=hfs[:, :, 15]), reads=[hk], writes=["PO"])
                    src, sname = hf, hk
                    bufs = [(bA, "bA" + str(k_)), (bB, "bB" + str(k_))]
                    sh = 1
                    kk = 0
                    while sh < wwin:
                        dst, dname = bufs[kk % 2]
                        kb.op(eng, lambda e, src=src, dst=dst, sh=sh: e.tensor_tensor(out=dst[:, 16:EW], in0=src[:, 16:EW], in1=src[:, 16 - sh:EW - sh], op=ALU.add),
                              reads=[sname], writes=[dname])
                        yield
                        src, sname = dst, dname
                        sh *= 2
                        kk += 1
                    yield
                    srs = src[:, sb0:sb0 + 16 * NS].rearrange("p (s k) -> p s k", s=NS)
                    if eng == "dve":
                        kb.op(eng, lambda e, src=src, c=c, hf=hf: e.scalar_tensor_tensor(out=H[:, c, a_in:TP], in0=src[:, 16:16 + n_p], scalar=1.0 / wwin,
                                                                                        in1=hf[:, 16:16 + n_p], op0=ALU.mult, op1=ALU.subtract),
                              reads=[sname, hk], writes=[("H", c)])
                        kb.op(eng, lambda e, srs=srs, c=c, hfs=hfs: e.scalar_tensor_tensor(out=H[:, c, TP:T], in0=srs[:, :, 15], scalar=1.0 / wwin,
                                                                                          in1=hfs[:, :, 15], op0=ALU.mult, op1=ALU.subtract),
                              reads=[sname, hk], writes=[("H", c)])
                    else:
                        mbuf, mname = bufs[kk % 2]
                        kb.op("act", lambda e, src=src, mbuf=mbuf: e.activation(out=mbuf[:, 16:EW], in_=src[:, 16:EW], func=AF.Copy, scale=1.0 / wwin),
                              reads=[sname], writes=[mname])
                        mbs = mbuf[:, sb0:sb0 + 16 * NS].rearrange("p (s k) -> p s k", s=NS)
                        kb.op(eng, lambda e, mbuf=mbuf, c=c, hf=hf: e.tensor_tensor(out=H[:, c, a_in:TP], in0=mbuf[:, 16:16 + n_p], in1=hf[:, 16:16 + n_p], op=ALU.subtract),
                              reads=[mname, hk], writes=[("H", c)])
                        kb.op(eng, lambda e, mbs=mbs, c=c, hfs=hfs: e.tensor_tensor(out=H[:, c, TP:T], in0=mbs[:, :, 15], in1=hfs[:, :, 15], op=ALU.subtract),
                              reads=[mname, hk], writes=[("H", c)])
                    yield
                    f0 = 16 + FIX0 - a_in
                    kb.op(eng, lambda e, src=src, g=g, fx=fx: e.tensor_tensor(out=fx[:], in0=src[:, f0:f0 + 16], in1=rcf[:, g, :], op=ALU.mult),
                          reads=[sname, "rcf"], writes=[fk])
                    kb.op(eng, lambda e, c=c, fx=fx, hf=hf: e.tensor_tensor(out=H[:, c, FIX0:FIX0 + 16], in0=fx[:], in1=hf[:, f0:f0 + 16], op=ALU.subtract),
                          reads=[fk, hk], writes=[("H", c)])

                for c0_ in range(0, NCH, 2):
                    gens_ = [chunk_gen(c0_), chunk_gen(c0_ + 1)]
                    while gens_:
                        nx_ = []
                        for g_ in gens_:
                            try:
                                next(g_)
                                nx_.append(g_)
                            except StopIteration:
                                pass
                        gens_ = nx_
                kb.dma("sp", po_o[j], PO[:], reads=["PO"])
                tiles = split_tiles(a_out, T)
                for g in range(4):
                    for hfb in range(2):
                        sk, Wp = ws.next("pool")
                        Wp3 = Wp[:, 0:1024].rearrange("p (a b) -> p a b", a=4)
                        for ml in range(2):
                            m = g * 4 + hfb * 2 + ml
                            for (t0, t1) in tiles:
                                w = t1 - t0
                                yb, yk = bank()
                                kb.mm([(yb[:, 0:w], Wp3[:, kc, ml * 128:(ml + 1) * 128], H[:, g * 4 + kc, t0:t1], kc == 0, kc == 3) for kc in range(4)],
                                      reads=[("ring", sk)] + [("H", g * 4 + kc) for kc in range(4)], writes=[yk])
                                kb.op("dve", lambda e, yb=yb, m=m: e.scalar_tensor_tensor(out=X[:, m, t0:t1], in0=yb[:, 0:w], scalar=pscl(j, m),
                                                                                          in1=X[:, m, t0:t1], op0=ALU.mult, op1=ALU.add),
                                      reads=[yk, ("X", m), "vecs"], writes=[("X", m)])
                        ws.release(sk)
                kb.barrier()

        QW = QWMAX

        def swa_layer(i):
            j = i // 2
            ks, qs = IN_L[i], OUT_L[i]
            assert qs == ks + 128
            nkb = -(-(TP - ks) // 128)
            rmsnorm_to_H((ks, T), lambda c: gmix(i, c))
            kb.rot = [0, 1, 2, 3]
            with ExitStack() as st2:
                A = st2.enter_context
                KT = A(sbuf_tensor("KT", [128, 2, T], BF16))
                KTf = A(sbuf_tensor("KTf", [128, 2, 128 + NS], F32))
                V = A(sbuf_tensor("V", [128, nkb, 256], BF16))
                Vs = A(sbuf_tensor("Vs", [128, NS, 256], BF16))
                KTs = A(sbuf_tensor("KTs", [128, NS, 2, 128], BF16))
                VP = A(sbuf_tensor("VP", [128, 256], F32))
                QO = A(sbuf_tensor("QO", [128, NCH, QW], BF16))
                cst = A(sbuf_tensor("cst", [128, 2, QW], F32))
                xn = A(sbuf_tensor("xn", [128, QW], F32))
                rq = A(sbuf_tensor("rq", [128, QW], F32))
                kbs = A(sbuf_tensor("kbs", [128, 12], F32))
                rden = rstd
                kb.dma("sp", kbs[:], kbias_d[:, j, :], writes=["kbs"])
                kb.sync_to_barrier("pool")
                kb.dma("pool", KTs[:, :, :, 1:128], kcT_d[j], writes=["KTs_c"])
                kb.dma("pool", Vs[1:128], vc_d[j], writes=["Vs_c"])
                hreads = [("H", c) for c in range(NCH)]

                def load_tables(t0, t1):
                    w = t1 - t0
                    kb.dma("sp", cst[:, 0, 0:w], cos_d[:, t0:t1], writes=["cst"])
                    kb.dma("sp", cst[:, 1, 0:w], sin_d[:, t0:t1], writes=["cst"])

                xn2 = A(sbuf_tensor("xn2", [128, QW], F32))
                rq2 = A(sbuf_tensor("rq2", [128, QW], F32))
                sqb2 = A(sbuf_tensor("sqb2", [128, QW], BF16))
                qkn = [0]
                ptb = [sq[1], sq[2], A(sbuf_tensor("ptb2", [128, 256], BF16)), A(sbuf_tensor("ptb3", [128, 256], BF16))]

                def qk_post_gen(pb, pk, w, gvec, outs):
                    par = qkn[0] % 2
                    qkn[0] += 1
                    sb, sbk = (sq[0], ("sq", 0)) if par == 0 else (sqb2, "sqb2")
                    xn_, xk = (xn, "xn") if par == 0 else (xn2, "xn2")
                    rq_, rk_ = (rq, "rq") if par == 0 else (rq2, "rq2")
                    kb.op("act", lambda e: e.activation(out=sb[:, 0:w], in_=pb[:, 0:w], func=AF.Square), reads=[pk], writes=[sbk])
                    yield
                    mb_, mk = bank()
                    kb.mm([(mb_[:, 0:w], blk64[:], sb[:, 0:w], True, True)], reads=[sbk, "blk64"], writes=[mk])
                    kb.op("act", lambda e: e.activation(out=rq_[:, 0:w], in_=mb_[:, 0:w], func=AF.Ln, bias=epst[:, 0:1], scale=1.0),
                          reads=[mk, "eps"], writes=[rk_])
                    kb.op("act", lambda e: e.activation(out=rq_[:, 0:w], in_=rq_[:, 0:w], func=AF.Exp, scale=-0.5), reads=[rk_], writes=[rk_])
                    kb.op("dve", lambda e: e.scalar_tensor_tensor(out=xn_[:, 0:w], in0=pb[:, 0:w], scalar=gvec, in1=rq_[:, 0:w], op0=ALU.mult, op1=ALU.mult),
                          reads=[pk, rk_, "vecs"], writes=[xk])
                    yield
                    rb, rk = bank()
                    kb.mm([(rb[:, 0:w], RT, xn_[:, 0:w], True, True)], reads=[xk, "cmat"], writes=[rk])
                    kb.op("dve", lambda e: e.tensor_tensor(out=rq_[:, 0:w], in0=rb[:, 0:w], in1=cst[:, 1, 0:w], op=ALU.mult),
                          reads=[rk, "cst", rk_], writes=[rk_])
                    kb.op("dve", lambda e: e.tensor_tensor(out=xn_[:, 0:w], in0=xn_[:, 0:w], in1=cst[:, 0, 0:w], op=ALU.mult),
                          reads=[xk, "cst"], writes=[xk])
                    for (oap, okey, lo, hi) in outs:
                        kb.op("dve", lambda e, oap=oap, lo=lo, hi=hi: e.tensor_tensor(out=oap, in0=rq_[:, lo:hi], in1=xn_[:, lo:hi], op=ALU.add),
                              reads=[rk_, xk], writes=[okey])

                def run_interleaved(gens):
                    gens = list(gens)
                    while gens:
                        nxt = []
                        for g_ in gens:
                            try:
                                next(g_)
                                nxt.append(g_)
                            except StopIteration:
                                pass
                        gens = nxt

                def qk_post(pb, pk, w, gvec, outs):
                    run_interleaved([qk_post_gen(pb, pk, w, gvec, outs)])

                sk, Wk = ws.next("wk")
                Wk3 = Wk[:, 0:4096].rearrange("p (a b) -> p a b", a=16)
                for (t0, t1) in split_tiles(ks, T, QW):
                    w = t1 - t0
                    load_tables(t0, t1)
                    gens = []
                    for gp in range(2):
                        pb, pk = bank()
                        kb.mm([(pb[:, 0:w], Wk3[:, kc, gp * 128:(gp + 1) * 128], H[:, kc, t0:t1], kc == 0, kc == 15) for kc in range(16)],
                              reads=[("ring", sk)] + hreads, writes=[pk])
                        outs = [(KT[:, gp, t0:t1], "KT", 0, w)]
                        lo = max(t0, LASTK0)
                        if lo < t1:
                            outs.append((KTf[:, gp, lo - LASTK0:t1 - LASTK0], "KTf", lo - t0, w))
                        gens.append(qk_post_gen(pb, pk, w, kg(j), outs))
                    run_interleaved(gens)
                ws.release(sk)
                kb.dma("sp", kT_o[j], KTf[:], reads=["KTf"])
                if swa_stop <= 1:
                    return "stop"
                for s in range(NS):
                    kb.op("dve", lambda e, s=s: e.tensor_copy(out=KTs[:, s, :, 0], in_=KT[:, :, TP + s]), reads=["KT"], writes=["KTs_n"])
                if swa_stop <= 1.2:
                    return "stop"
                sv, Wv = ws.next("wv")
                Wv3 = Wv[:, 0:4096].rearrange("p (a b) -> p a b", a=16)
                for kbi in range(nkb):
                    c0 = ks + 128 * kbi
                    kn = min(128, TP - c0)
                    vb, vk = bank()
                    kb.mm([(vb[0:kn, 0:256], H[:, kc, c0:c0 + kn], Wv3[:, kc, :], kc == 0, kc == 15) for kc in range(16)],
                          reads=[("ring", sv)] + hreads, writes=[vk])
                    kb.op("act", lambda e, vb=vb, kbi=kbi, kn=kn: e.copy(out=V[0:kn, kbi, :], in_=vb[0:kn, 0:256]), reads=[vk], writes=[("V", kbi)])
                if swa_stop <= 1.4:
                    return "stop"
                vb, vk = bank()
                kb.mm([(vb[:, 0:256], H[:, kc, LASTK0:TP], Wv3[:, kc, :], kc == 0, kc == 15) for kc in range(16)],
                      reads=[("ring", sv)] + hreads, writes=[vk])
                kb.op("act", lambda e, vb=vb: e.copy(out=VP[:], in_=vb[:, 0:256]), reads=[vk], writes=["VP"])
                kb.dma("sp", vp_o[j], VP[:], reads=["VP"])
                if swa_stop <= 1.6:
                    return "stop"
                for sp_ in range(2):
                    vb, vk = bank()
                    mms = []
                    for sl in range(2):
                        s = sp_ * 2 + sl
                        mms += [(vb[0:2, sl * 256:(sl + 1) * 256], H[:, kc, TP + s:TP + s + 2], Wv3[:, kc, :], kc == 0, kc == 15) for kc in range(16)]
                    kb.mm(mms, reads=[("ring", sv)] + hreads, writes=[vk])
                    kb.op("act", lambda e, vb=vb: e.copy(out=rden[0:1, 0:512], in_=vb[0:1, 0:512]), reads=[vk], writes=["rstdbuf"])
                    kb.op("dve", lambda e, vb=vb, sp_=sp_: e.tensor_copy(out=Vs[0:1, sp_ * 2:sp_ * 2 + 2, :],
                                                                         in_=vb[0:1, 0:512].rearrange("p (s d) -> p s d", s=2)),
                          reads=[vk], writes=["Vs_n"])
                    kb.dma("sp", vs_o[j, :, sp_ * 2:sp_ * 2 + 2, :], rden[0:1, 0:512].rearrange("p (s d) -> p s d", s=2), reads=["rstdbuf"])
                ws.release(sv)
                if swa_stop <= 2:
                    return "stop"

                qtl = q_tiles(qs)
                qtiles = [(a0_, a1_, n_ == len(qtl) - 1) for n_, (a0_, a1_) in enumerate(qtl)]
                ptn = [0]
                hn = [0]
                for (t0, t1p, has_s) in qtiles:
                    t1 = T if has_s else t1p
                    w = t1 - t0
                    wp = t1p - t0
                    assert w <= QW and wp + NS * (NS + 1) <= 512
                    load_tables(t0, t1)
                    kb0 = (t0 - ks) // 128
                    kbl = (t1p - 1 - ks) // 128
                    for gp in range(2):
                        for ip in range(4):
                            sk, Wq = ws.next("wq")
                            Wq3 = Wq[:, 0:4096].rearrange("p (a b) -> p a b", a=16)
                            gens = []
                            for cl in range(2):
                                c = gp * 8 + ip * 2 + cl
                                pb, pk = bank()
                                kb.mm([(pb[:, 0:w], Wq3[:, kc, cl * 128:(cl + 1) * 128], H[:, kc, t0:t1], kc == 0, kc == 15) for kc in range(16)],
                                      reads=[("ring", sk)] + hreads, writes=[pk])
                                gens.append(qk_post_gen(pb, pk, w, qg(j), [(QO[:, c, 0:w], ("QO", c), 0, w)]))
                            ws.release(sk)
                            run_interleaved(gens)
                            steps = []
                            for cl in range(2):
                                c = gp * 8 + ip * 2 + cl
                                ii = ip * 2 + cl
                                pair = []
                                for hh in range(2):
                                    par = hn[0] % 2
                                    hn[0] += 1
                                    hd = dict(c=c, hh=hh, hq=8 * (2 * gp + hh) + ii, r0=hh * 64, r1=hh * 64 + 64,
                                              ob=ps[4 + par * 2], ok=("ps", 4 + par * 2), db=ps[5 + par * 2], dk=("ps", 5 + par * 2))
                                    hsteps = [("p", hd, kbi) for kbi in range(kb0 - 1, kbl + 1)]
                                    if has_s:
                                        hsteps += [("s", hd, s_) for s_ in range(NS)]
                                    hsteps[-1] = hsteps[-1] + (True,)
                                    pair.append(hsteps)
                                for sa_, sb2_ in zip(pair[0], pair[1]):
                                    steps += [sa_, sb2_]

                            def s_phase(st_):
                                kind, hd, idx = st_[0], st_[1], st_[2]
                                r0, r1, c = hd["r0"], hd["r1"], hd["c"]
                                sb_, sk_ = bank()
                                pi = ptn[0] % 4
                                ptn[0] += 1
                                pt, pkey = ptb[pi], (("sq", pi + 1) if pi < 2 else ("ptb", pi))
                                if kind == "p":
                                    kc0 = ks + 128 * idx
                                    kn = min(128, TP - kc0)
                                    q0 = max(t0, kc0)
                                    q1 = min(t1p, kc0 + 256)
                                    nq = q1 - q0
                                    off = q0 - kc0
                                    kb.mm([(sb_[0:kn, 0:nq], KT[r0:r1, gp, kc0:kc0 + kn], QO[r0:r1, c, q0 - t0:q1 - t0], True, True)],
                                          reads=["KT", ("QO", c)], writes=[sk_])
                                    kb.op("act", lambda e: e.activation(out=pt[0:kn, 0:nq], in_=sb_[0:kn, 0:nq], func=AF.Exp,
                                                                        bias=kbs[0:kn, idx:idx + 1], scale=0.125),
                                          reads=[sk_, "kbs"], writes=[pkey])
                                    kb.op("dve", lambda e: e.tensor_tensor(out=pt[0:kn, 0:nq], in0=pt[0:kn, 0:nq], in1=maskb[0:kn, off:off + nq], op=ALU.mult),
                                          reads=[pkey, "maskb"], writes=[pkey])
                                    return (pt, pkey, kn, q0, q1)
                                kb.mm([(sb_[:, 0:NS], KTs[r0:r1, idx, gp, :], QO[r0:r1, c, wp:wp + NS], True, True)],
                                      reads=["KTs_c", "KTs_n", ("QO", c)], writes=[sk_])
                                kb.op("act", lambda e: e.activation(out=pt[:, 0:NS], in_=sb_[:, 0:NS], func=AF.Exp, scale=0.125),
                                      reads=[sk_], writes=[pkey])
                                return (pt, pkey)

                            def p_phase(st_, sres):
                                kind, hd, idx = st_[0], st_[1], st_[2]
                                ob, ok_, db, dk = hd["ob"], hd["ok"], hd["db"], hd["dk"]
                                if kind == "p":
                                    pt, pkey, kn, q0, q1 = sres
                                    first = not hd.get("opened", False)
                                    hd["opened"] = True
                                    rhs = pt[0:kn, 0:q1 - q0]
                                    kb.mm([(ob[:, q0 - t0:q1 - t0], V[0:kn, idx, gp * 128:(gp + 1) * 128], rhs, first, True, not first)],
                                          reads=[pkey, ("V", idx)], writes=[ok_])
                                    kb.mm([(db[:, q0 - t0:q1 - t0], ones1[0:kn, :], rhs, first, True, not first)], reads=[pkey, "ones1"], writes=[dk])
                                else:
                                    pt, pkey = sres
                                    o0 = wp + NS * idx
                                    kb.mm([(ob[:, o0:o0 + NS], Vs[:, idx, gp * 128:(gp + 1) * 128], pt[:, 0:NS], False, True, True)],
                                          reads=[pkey, "Vs_c", "Vs_n"], writes=[ok_])
                                    kb.mm([(db[:, o0:o0 + NS], ones1[:], pt[:, 0:NS], False, True, True)], reads=[pkey, "ones1"], writes=[dk])
                                if len(st_) > 3:
                                    finish(hd)

                            def finish(hd):
                                r0, r1, c, hq = hd["r0"], hd["r1"], hd["c"], hd["hq"]
                                ob, ok_, db, dk = hd["ob"], hd["ok"], hd["db"], hd["dk"]
                                rkey = ("rden", hd["hh"])
                                esk = esink[r0:r1, j * 32 + hq:j * 32 + hq + 1]
                                parts = []
                                if wp > 0:
                                    parts.append((db[r0:r1, 0:wp], ob[r0:r1, 0:wp], rden[r0:r1, 0:wp], QO[r0:r1, c, 0:wp]))
                                if has_s:
                                    dg = lambda t: t[r0:r1, wp:wp + NS * (NS + 1)].rearrange("p (a b) -> p a b", b=NS + 1)[:, :, 0]
                                    parts.append((dg(db), dg(ob), rden[r0:r1, wp:wp + NS], QO[r0:r1, c, wp:wp + NS]))
                                for (dsrc, osrc, rd, qo) in parts:
                                    kb.op("act", lambda e, dsrc=dsrc, rd=rd: e.activation(out=rd, in_=dsrc, func=AF.Ln, bias=esk, scale=1.0),
                                          reads=[dk, "esink"], writes=[rkey, "rstdbuf"])
                                    kb.op("act", lambda e, rd=rd: e.activation(out=rd, in_=rd, func=AF.Exp, scale=-1.0), reads=[rkey], writes=[rkey])
                                    kb.op("dve", lambda e, osrc=osrc, rd=rd, qo=qo: e.tensor_tensor(out=qo, in0=osrc, in1=rd, op=ALU.mult),
                                          reads=[ok_, rkey], writes=[("QO", c)])

                            LEAD = 3
                            pend = []
                            for st_ in steps:
                                pend.append((st_, s_phase(st_)))
                                if len(pend) > LEAD:
                                    p_phase(*pend.pop(0))
                            while pend:
                                p_phase(*pend.pop(0))
                    if swa_stop <= 3:
                        return "stop"
                    for mq in range(4):
                        wos = []
                        for gp in range(2):
                            sk, Wo = ws.next("wo")
                            wos.append((sk, Wo[:, 0:4096].rearrange("p (i m) -> p i m", i=8)))
                        for ml in range(4):
                            m = mq * 4 + ml
                            yb, yk = bank()
                            kb.mm([(yb[:, 0:w], wos[kc // 8][1][:, kc % 8, ml * 128:(ml + 1) * 128], QO[:, kc, 0:w], kc == 0, kc == 15) for kc in range(16)],
                                  reads=[("ring", wos[0][0]), ("ring", wos[1][0])] + [("QO", c) for c in range(NCH)], writes=[yk])
                            kb.op("dve", lambda e, yb=yb, m=m: e.tensor_tensor(out=X[:, m, t0:t1], in0=yb[:, 0:w], in1=X[:, m, t0:t1], op=ALU.add),
                                  reads=[yk, ("X", m)], writes=[("X", m)])
                        for (sk, _) in wos:
                            ws.release(sk)
                    if swa_stop <= 4:
                        return "stop"
                kb.rot = list(range(8))
                kb.barrier()

        nph = 0
        for i in range(4 if stop > 0 else 0):
            if nph >= stop:
                break
            if i % 2 == 0:
                pool_layer(i)
            else:
                if swa_layer(i) == "stop":
                    kb.rot = list(range(8))
                    kb.barrier()
                    break
            nph += 1
            if nph >= stop:
                break
            ffn(i)
            nph += 1
        assert stop < 99 or ws.taken == len(plan), (ws.taken, len(plan))
        for c in range(NCH):
            kb.dma("sp", yT_o[:, c, :], X[:, c, HALO:T])
        kb.barrier(engines=("sp",))
    return nc


_CACHE = {}


def _const_inputs():
    ident = np.eye(128, dtype=np.float32)
    R = np.zeros((128, 128), np.float32)
    for hb in (0, 64):
        for m in range(8):
            R[hb + m, hb + m + 8] = -1.0
            R[hb + m + 8, hb + m] = 1.0
    blk = np.zeros((128, 128), np.float32)
    blk[:64, :64] = 1.0 / 64
    blk[64:, 64:] = 1.0 / 64
    cmat = np.stack([ident, R.T.copy(), blk, np.zeros((128, 128), np.float32)], axis=1)
    ii = np.arange(128)[:, None]
    cc = np.arange(256)[None, :]
    maskb = np.where((cc >= ii) & (cc < ii + 128), 1.0, 0.0).astype(np.float32)
    return np.ascontiguousarray(cmat), maskb


def kernel(**inputs):
    in_maps = make_in_maps(**inputs)
    if "nc" not in _CACHE:
        _CACHE["nc"] = build_program()
    res = run_bass_kernel_spmd(_CACHE["nc"], in_maps, core_ids=list(range(8)))
    return assemble(res.results)


def make_in_maps(x_prompt, x_sample, state_pool, cache_k, cache_v, meta_tokens, norm_mix, norm_ffn,
                 pool_w, pool_scale, w_qkv, w_o, q_norm, k_norm, sinks, w_gate, w_up, w_down):
    f32 = np.float32
    x_prompt = np.asarray(x_prompt, f32)
    x_sample = np.asarray(x_sample, f32)
    state_pool = np.asarray(state_pool, f32)
    cache_k = np.asarray(cache_k, f32)
    cache_v = np.asarray(cache_v, f32)
    B = x_prompt.shape[0]
    cmat, maskb = _const_inputs()

    def chunked(v):
        v = np.asarray(v, f32)
        return v.reshape(v.shape[0], 16, 128).transpose(2, 0, 1)

    vecs = np.zeros((128, 296), f32)
    vecs[:, 0:64] = chunked(norm_mix).reshape(128, 64)
    vecs[:, 64:128] = chunked(norm_ffn).reshape(128, 64)
    vecs[:, 128:160] = chunked(pool_scale).reshape(128, 32)
    vecs[:, 160:162] = np.tile(np.asarray(q_norm, f32).T, (2, 1))
    vecs[:, 162:164] = np.tile(np.asarray(k_norm, f32).T, (2, 1))
    vecs[:, 164:228] = np.broadcast_to(np.asarray(sinks, f32).reshape(1, 64), (128, 64))

    half = 8
    inv = (f32(500000.0) ** (-np.arange(half, dtype=f32) * f32(2.0) / f32(16))).astype(f32)
    dloc = np.arange(128) % 64

    shared = {
        "cmat": cmat, "maskb": maskb, "vecs": vecs,
        "pool_w": np.asarray(pool_w, f32), "w_qkv": np.asarray(w_qkv, f32), "w_o": np.asarray(w_o, f32),
        "w_gate": np.asarray(w_gate, f32), "w_up": np.asarray(w_up, f32), "w_down": np.asarray(w_down, f32),
    }
    in_maps = []
    for core in range(8):
        b, q = divmod(core, 4)
        O = q * NOWN
        pos = O - HALO + np.arange(TP)
        valid = pos >= 0
        xext = np.concatenate([np.asarray(meta_tokens, f32), x_prompt[b]], axis=0)
        cols = np.zeros((T, D), f32)
        cols[:TP][valid] = xext[pos[valid]]
        sidx = np.arange(NS) + NS * core
        cols[TP:] = x_sample[sidx, 0]
        xT = np.ascontiguousarray(cols.reshape(T, 16, 128).transpose(2, 1, 0))
        pfull = np.concatenate([np.maximum(pos, 0), np.full(NS, PAST)]).astype(f32)
        ang = (pfull[:, None] * inv[None, :]).astype(f32)
        cosv, sinv = np.cos(ang).astype(f32), np.sin(ang).astype(f32)
        cosT = np.ones((128, T), f32)
        sinT = np.zeros((128, T), f32)
        for p in range(128):
            d = dloc[p]
            if d < 16:
                cosT[p] = cosv[:, d % 8]
                sinT[p] = sinv[:, d % 8]
        kbias = np.zeros((128, 2, 12), f32)
        for jj, ks in enumerate((IN_L[1], IN_L[3])):
            for kbi in range(12):
                ccol = ks + 128 * kbi + np.arange(128)
                ok = (ccol < TP) & (np.where(ccol < TP, O - HALO + ccol, -1) >= 0)
                kbias[:, jj, kbi] = np.where(ok, 0.0, NEGM)
        rcfix = np.zeros((128, 4, 16), f32)
        for g in range(4):
            wwin = 2 << g
            cnt = np.minimum(wwin, O + np.arange(16) + 1).astype(f32)
            rcfix[:, g, :] = (f32(1.0) / cnt)[None, :]
        sp = state_pool[:, sidx]
        poolstT = np.ascontiguousarray(sp.reshape(2, NS, 15, 16, 128).transpose(0, 4, 3, 1, 2))
        ck = cache_k[:, sidx].reshape(2, NS, 128, 4, 64)
        kt = ck[:, :, 1:].reshape(2, NS, 127, 2, 2, 64)
        kcT = np.ascontiguousarray(kt.transpose(0, 4, 5, 1, 3, 2).reshape(2, 128, NS, 2, 127))
        cv = cache_v[:, sidx].reshape(2, NS, 128, 256)
        vc = np.ascontiguousarray(cv[:, :, 1:].transpose(0, 2, 1, 3))
        m = dict(shared)
        m.update({
            "xT": xT, "cosT": cosT, "sinT": sinT, "kbias": kbias, "rcfix": rcfix,
            "poolstT": poolstT, "kcT": kcT, "vc": vc,
            "ck": np.ascontiguousarray(ck.reshape(2, NS, 128, 256)), "cv": np.ascontiguousarray(cv),
            "spool": np.ascontiguousarray(sp),
        })
        in_maps.append(m)

    return in_maps


def assemble(R, cores=range(8), B=2):
    f32 = np.float32
    y_prompt = np.zeros((B, 4096, D), f32)
    y_sample = np.zeros((32, 1, D), f32)
    new_pool_p = np.zeros((2, B, 15, D), f32)
    new_k_p = np.zeros((2, B, 128, 4, 64), f32)
    new_v_p = np.zeros((2, B, 128, 4, 64), f32)
    new_pool_s = np.zeros((2, 32, 15, D), f32)
    new_k_s = np.zeros((2, 32, 128, 4, 64), f32)
    new_v_s = np.zeros((2, 32, 128, 4, 64), f32)
    for core in cores:
        b, q = divmod(core, 4)
        r = R[core]
        yt = np.asarray(r["yT"]).transpose(2, 1, 0).reshape(NOWN + NS, D)
        rows = yt[:NOWN]
        if q == 0:
            y_prompt[b, 0:NOWN - 16] = rows[16:]
        else:
            y_prompt[b, q * NOWN - 16:(q + 1) * NOWN - 16] = rows
        sidx = np.arange(NS) + NS * core
        y_sample[sidx, 0] = yt[NOWN:]
        po = np.asarray(r["poolT"]).transpose(0, 3, 2, 1).reshape(2, NPO, D)
        kn = np.asarray(r["knewT"]).reshape(2, 2, 64, 2, 128 + NS).transpose(0, 4, 3, 1, 2).reshape(2, 128 + NS, 4, 64)
        if q == 3:
            new_pool_p[:, b] = po[:, :15]
            new_k_p[:, b] = kn[:, :128]
            new_v_p[:, b] = np.asarray(r["vnew_p"]).reshape(2, 128, 4, 64)
        new_pool_s[:, sidx, 14] = po[:, 15:]
        new_pool_s[:, sidx, :14] = np.asarray(r["pold_s"])
        new_k_s[:, sidx, :127] = np.asarray(r["kold_s"]).reshape(2, NS, 127, 4, 64)
        new_k_s[:, sidx, 127] = kn[:, 128:]
        new_v_s[:, sidx, :127] = np.asarray(r["vold_s"]).reshape(2, NS, 127, 4, 64)
        new_v_s[:, sidx, 127] = np.asarray(r["vnew_s"])[:, 0].reshape(2, NS, 4, 64)
    return (y_prompt, y_sample, new_pool_p, new_k_p, new_v_p, new_pool_s, new_k_s, new_v_s)
```

```python
import numpy as np
from contextlib import ExitStack
import concourse.bass as bass
import concourse.mybir as mybir
from concourse.bass_utils import run_bass_kernel_spmd

F32 = mybir.dt.float32
BF16 = mybir.dt.bfloat16
AF = mybir.ActivationFunctionType
ALU = mybir.AluOpType

D = 2048
NCH = 16
DFF = 5632
NFC = 44
NOWN = 1028
HALO = 286
TP = NOWN + HALO
NS = 4
T = TP + NS
SEQ_EXT = 4112
PAST = 16384
NEGM = -30000.0
EPS = 1e-6
IN_L = [0, 15, 143, 158]
OUT_L = [15, 143, 158, 286]
RING_SLOTS = 4
RING_EL = 4096
LASTK0 = TP - 128
NPO = 19
FIX0 = HALO


QSTEP = 384
QWMAX = 416


def q_tiles(qs):
    tl = [(a, min(a + QSTEP, TP)) for a in range(qs, TP, QSTEP)]
    if len(tl) > 1 and (tl[-1][1] - tl[-1][0]) + NS <= QWMAX - QSTEP:
        tl[-2] = (tl[-2][0], TP)
        tl.pop()
    return tl


def split_tiles(a, b, maxw=512):
    n = b - a
    k = -(-n // maxw)
    base, rem = divmod(n, k)
    out = []
    s = a
    for i in range(k):
        w = base + (1 if i < rem else 0)
        out.append((s, s + w))
        s += w
    return out


class Slot:
    __slots__ = ("writer", "readers")

    def __init__(self):
        self.writer = None
        self.readers = {}


class KB:
    def __init__(self, nc, st):
        self.nc = nc
        self.eng = {"pe": nc.tensor, "act": nc.scalar, "dve": nc.vector, "pool": nc.gpsimd, "sp": nc.sync}
        self.sem = {k: st.enter_context(nc.semaphore("s_" + k)) for k in self.eng}
        self.cnt = {k: 0 for k in self.eng}
        self.waited = {}
        self.slots = {}
        self.ndsem = 20
        self.dsem = [st.enter_context(nc.semaphore(f"d{i}")) for i in range(self.ndsem)]
        self.dval = [0] * self.ndsem
        self.drr = 0
        self.npsem = 4
        self.psem = [st.enter_context(nc.semaphore(f"pd{i}")) for i in range(self.npsem)]
        self.pval = [0] * self.npsem
        self.prr = 0
        self.rsem = [st.enter_context(nc.semaphore(f"r{i}")) for i in range(RING_SLOTS)]
        self.rval = [0] * RING_SLOTS
        self.live_dma = []

    def slot(self, key):
        s = self.slots.get(key)
        if s is None:
            s = Slot()
            self.slots[key] = s
        return s

    def wait(self, e, tok):
        if tok is None:
            return
        kind, key, val = tok
        if kind == "e":
            if key == e and e in ("pe", "sp"):
                return
            sem = self.sem[key]
        else:
            sem = key
        wk = (e, id(sem))
        if self.waited.get(wk, 0) >= val:
            return
        self.waited[wk] = val
        self.eng[e].wait_ge(sem, val)

    def _deps(self, e, reads, writes):
        for k in reads:
            s = self.slot(k)
            self.wait(e, s.writer)
            if isinstance(k, tuple) and k[0] == "ps":
                for rk, r in s.readers.items():
                    if rk != e:
                        self.wait(e, r)
        for k in writes:
            s = self.slot(k)
            self.wait(e, s.writer)
            for r in s.readers.values():
                self.wait(e, r)

    def _mark(self, tok, reads, writes, rkey):
        for k in reads:
            self.slot(k).readers[rkey] = tok
        for k in writes:
            s = self.slot(k)
            s.writer = tok
            s.readers = {}

    def op(self, e, fn, reads=(), writes=()):
        self._deps(e, reads, writes)
        ins = fn(self.eng[e])
        self.cnt[e] += 1
        ins.then_inc(self.sem[e], 1)
        tok = ("e", e, self.cnt[e])
        self._mark(tok, reads, writes, e)
        return tok

    def mm(self, mms, reads=(), writes=()):
        self._deps("pe", reads, writes)
        ins = None
        for mmi in mms:
            o, l, r, s0, s1 = mmi[:5]
            if len(mmi) > 5 and mmi[5]:
                ins = self.nc.tensor.matmul(o, l, r, start=s0, stop=s1, skip_group_check=True)
            else:
                ins = self.nc.tensor.matmul(o, l, r, start=s0, stop=s1)
        self.cnt["pe"] += 1
        ins.then_inc(self.sem["pe"], 1)
        tok = ("e", "pe", self.cnt["pe"])
        self._mark(tok, reads, writes, "pe")
        return tok

    def dma(self, q, out, in_, reads=(), writes=(), ring=None, track=True):
        self._deps(q, reads, writes)
        if ring is None and q == "pool":
            i = self.prr
            self.prr = (self.prr + 1) % self.npsem
            sem = self.psem[i]
            if self.pval[i] > 0:
                self.wait(q, ("d", sem, self.pval[i]))
            self.pval[i] += 16
            val = self.pval[i]
        elif ring is None:
            i = self.drr
            self.drr = (self.drr + 1) % self.ndsem
            sem = self.dsem[i]
            if self.dval[i] > 0:
                self.wait(q, ("d", sem, self.dval[i]))
            self.dval[i] += 16
            val = self.dval[i]
        else:
            sem = self.rsem[ring]
            self.rval[ring] += 16
            val = self.rval[ring]
        self.eng[q].dma_start(out=out, in_=in_).then_inc(sem, 16)
        tok = ("d", sem, val)
        self._mark(tok, reads, writes, ("dma", id(sem), val))
        if track:
            self.live_dma.append(tok)
        return tok

    def sync_to_barrier(self, e):
        for t in getattr(self, "last_barrier", []):
            self.wait(e, t)

    def barrier(self, engines=("pe", "act", "dve", "sp")):
        toks = [("e", k, self.cnt[k]) for k in ("pe", "act", "dve", "pool") if self.cnt[k] > 0]
        toks += self.live_dma
        self.last_barrier = toks
        for e in engines:
            for t in toks:
                self.wait(e, t)
        self.live_dma = []
        self.slots = {k: v for k, v in self.slots.items() if isinstance(k, tuple) and k[0] == "ring"}


class WStream:
    def __init__(self, kb, ring, plan):
        self.kb = kb
        self.ring = ring
        self.plan = plan
        self.issued = 0
        self.taken = 0
        self.released = [True] * RING_SLOTS
        self.occ = [None] * RING_SLOTS

    def pump(self):
        while self.issued < len(self.plan):
            s = self.issued % RING_SLOTS
            if not self.released[s]:
                break
            kind, parts = self.plan[self.issued]
            for (oview, src) in parts:
                self.kb.dma("pool", oview(self.ring[s]), src, writes=[("ring", s)], ring=s, track=False)
            self.released[s] = False
            self.occ[s] = self.issued
            self.issued += 1

    def next(self, kind):
        self.pump()
        n = self.taken
        assert n < self.issued, f"weight stream stall at block {n} ({kind})"
        assert self.plan[n][0] == kind, (self.plan[n][0], kind)
        self.taken += 1
        s = n % RING_SLOTS
        return s, self.ring[s]

    def release(self, s):
        self.released[s] = True
        self.pump()


class _StopBuild(Exception):
    pass


def build_program(stop=99, swa_stop=99):
    nc = bass.Bass("TRN2", target_bir_lowering=False)
    _uid = [0]
    _orig_sbuf = nc.sbuf_tensor

    def sbuf_tensor(name, shape, dt):
        _uid[0] += 1
        return _orig_sbuf(f"sb{_uid[0]}_{name}", shape, dt)

    def din(name, shape):
        return nc.dram_tensor(name, list(shape), F32, kind="ExternalInput").ap()

    def dout(name, shape):
        return nc.dram_tensor(name, list(shape), F32, kind="ExternalOutput").ap()

    xT_d = din("xT", [128, NCH, T])
    cos_d = din("cosT", [128, T])
    sin_d = din("sinT", [128, T])
    kbias_d = din("kbias", [128, 2, 12])
    rcfix_d = din("rcfix", [128, 4, 16])
    vecs_d = din("vecs", [128, 296])
    cmat_d = din("cmat", [128, 4, 128])
    maskb_d = din("maskb", [128, 256])
    poolst_d = din("poolstT", [2, 128, NCH, NS, 15])
    kcT_d = din("kcT", [2, 128, NS, 2, 127])
    vc_d = din("vc", [2, 127, NS, 256])
    ck_d = din("ck", [2, NS, 128, 256])
    cv_d = din("cv", [2, NS, 128, 256])
    sp_d = din("spool", [2, NS, 15, D])
    pool_w = din("pool_w", [2, 4, 512, 512])
    w_qkv = din("w_qkv", [2, D, 2560])
    w_o = din("w_o", [2, D, D])
    w_gate = din("w_gate", [4, D, DFF])
    w_up = din("w_up", [4, D, DFF])
    w_down = din("w_down", [4, DFF, D])

    yT_o = dout("yT", [128, NCH, NOWN + NS])
    po_o = dout("poolT", [2, 128, NCH, NPO])
    kT_o = dout("knewT", [2, 128, 2, 128 + NS])
    vp_o = dout("vnew_p", [2, 128, 256])
    vs_o = dout("vnew_s", [2, 1, NS, 256])
    ks_o = dout("kold_s", [2, NS, 127, 256])
    vso_o = dout("vold_s", [2, NS, 127, 256])
    pso_o = dout("pold_s", [2, NS, 14, D])

    def v3(a, b):
        return lambda r: r[:, 0:a * b].rearrange("p (a b) -> p a b", a=a)

    plan = []
    for i in range(4):
        j = i // 2
        if i % 2 == 0:
            for g in range(4):
                src = pool_w[j, g].rearrange("(kc p) m -> p kc m", p=128)
                for hf in range(2):
                    plan.append(("pool", [(v3(4, 256), src[:, :, hf * 256:(hf + 1) * 256])]))
        else:
            wq = w_qkv[j].rearrange("(kc p) m -> p kc m", p=128)
            wo = w_o[j].rearrange("(kc p) m -> p kc m", p=128)
            plan.append(("wk", [(v3(16, 256), wq[:, :, 2048:2304])]))
            plan.append(("wv", [(v3(16, 256), wq[:, :, 2304:2560])]))
            ntq = len(q_tiles(OUT_L[i]))
            wo5 = w_o[j].rearrange("(gp u i d) m -> d gp u i m", gp=2, u=2, i=8)
            for tt in range(ntq):
                for gp in range(2):
                    for ip in range(4):
                        parts = []
                        for u in range(2):
                            for hl in range(2):
                                def ov(r, u=u, hl=hl):
                                    return r[:, 0:4096].rearrange("p (kc hl u d) -> p kc hl u d", kc=16, hl=2, u=2)[:, :, hl, u, :]
                                c0 = (16 * gp + 8 * u + 2 * ip + hl) * 64
                                parts.append((ov, wq[:, :, c0:c0 + 64]))
                        plan.append(("wq", parts))
                for mq in range(4):
                    for gp in range(2):
                        parts = []
                        for u in range(2):
                            def ov(r, u=u):
                                return r[u * 64:(u + 1) * 64, 0:4096].rearrange("p (i m) -> p i m", i=8)
                            parts.append((ov, wo5[:, gp, u, :, mq * 512:(mq + 1) * 512]))
                        plan.append(("wo", parts))
        wg = w_gate[i].rearrange("(kc p) m -> p kc m", p=128)
        wu = w_up[i].rearrange("(kc p) m -> p kc m", p=128)
        wd = w_down[i].rearrange("(fc p) m -> p fc m", p=128)
        for qd in range(4):
            f = 0
            while f < 11:
                w = 2 if f + 2 <= 11 else 1
                c0 = (qd * 11 + f) * 128
                plan.append(("wg", [(v3(16, w * 128), wg[:, :, c0:c0 + w * 128])]))
                plan.append(("wu", [(v3(16, w * 128), wu[:, :, c0:c0 + w * 128])]))
                f += w
            for mb in range(8):
                plan.append(("wd", [(v3(11, 256), wd[:, qd * 11:(qd + 1) * 11, mb * 256:(mb + 1) * 256])]))

    with ExitStack() as st:
        E = st.enter_context
        kb = KB(nc, st)
        X = E(sbuf_tensor("X", [128, NCH, T], F32))
        H = E(sbuf_tensor("H", [128, NCH, T + 2], BF16))
        ring = [E(sbuf_tensor(f"ring{s}", [128, RING_EL], BF16)) for s in range(RING_SLOTS)]
        vecs = E(sbuf_tensor("vecs", [128, 296], F32))
        esink = E(sbuf_tensor("esink", [128, 64], F32))
        cmat = E(sbuf_tensor("cmat", [128, 128], F32))
        identb = E(sbuf_tensor("identb", [128, 128], BF16))
        blk64 = E(sbuf_tensor("blk64", [128, 128], BF16))
        onesD = E(sbuf_tensor("onesD", [128, 128], BF16))
        ones1 = E(sbuf_tensor("ones1", [128, 128], BF16))
        maskb = E(sbuf_tensor("maskb", [128, 2, 256], BF16))
        epst = E(sbuf_tensor("epst", [128, 1], F32))
        rstd = E(sbuf_tensor("rstd", [128, 512], F32))
        sq = [E(sbuf_tensor(f"sq{i}", [128, 512], BF16)) for i in range(3)]
        ps = [E(nc.psum_tensor(f"ps{i}", [128, 512], F32)) for i in range(8)]
        psn = [0]

        kb.rot = list(range(8))

        def bank():
            i = kb.rot[psn[0] % len(kb.rot)]
            psn[0] += 1
            return ps[i], ("ps", i)

        RT = cmat[:]
        gmix = lambda i, c: vecs[:, i * 16 + c: i * 16 + c + 1]
        gffn = lambda i, c: vecs[:, 64 + i * 16 + c: 64 + i * 16 + c + 1]
        pscl = lambda j, c: vecs[:, 128 + j * 16 + c: 128 + j * 16 + c + 1]
        qg = lambda j: vecs[:, 160 + j: 161 + j]
        kg = lambda j: vecs[:, 162 + j: 163 + j]

        ws = WStream(kb, ring, plan)

        kb.dma("sp", vecs[:], vecs_d[:, :], writes=["vecs"])
        kb.dma("sp", cmat[:], cmat_d[:, 1, :], writes=["cmat"])
        kb.dma("pool", identb[:], cmat_d[:, 0, :], writes=["identb"])
        kb.dma("pool", blk64[:], cmat_d[:, 2, :], writes=["blk64"])
        kb.dma("pool", maskb[:, 0, :], maskb_d[:, :], writes=["maskb"])
        kb.dma("pool", maskb[:, 1, :], maskb_d[:, :], writes=["maskb"])
        for c in range(NCH):
            kb.dma("sp", X[:, c, :], xT_d[:, c, :], writes=[("X", c)])
        ws.pump()
        kb.op("dve", lambda e: e.memset(epst[:], EPS), writes=["eps"])
        kb.op("dve", lambda e: e.memset(H[:, :, T:T + 2], 0.0), writes=[("H", c) for c in range(NCH)])
        kb.op("dve", lambda e: e.memset(onesD[:], 1.0 / D), writes=["onesD"])
        kb.op("dve", lambda e: e.memset(ones1[:], 1.0), writes=["ones1"])
        kb.op("act", lambda e: e.activation(out=esink[:], in_=vecs[:, 164:228], func=AF.Exp), reads=["vecs"], writes=["esink"])
        for j in range(2):
            for s in range(NS):
                kb.dma("sp", ks_o[j, s], ck_d[j, s, 1:128, :])
                kb.dma("sp", vso_o[j, s], cv_d[j, s, 1:128, :])
                kb.dma("sp", pso_o[j, s], sp_d[j, s, 1:15, :])
        kb.barrier()

        def rms_stats(cols, dst, dst_off, key="rstdbuf"):
            a, b = cols
            for (t0, t1) in split_tiles(a, b):
                w = t1 - t0
                pb, pk = bank()
                for c in range(NCH):
                    sb = sq[c % 3]
                    kb.op("act", lambda e, sb=sb, c=c: e.activation(out=sb[:, 0:w], in_=X[:, c, t0:t1], func=AF.Square),
                          reads=[("X", c)], writes=[("sq", c % 3)])
                    kb.mm([(pb[:, 0:w], onesD[:], sb[:, 0:w], c == 0, c == NCH - 1)], reads=[("sq", c % 3), "onesD"], writes=[pk])
                o0 = dst_off + t0 - a
                kb.op("act", lambda e: e.activation(out=dst[:, o0:o0 + w], in_=pb[:, 0:w], func=AF.Ln, bias=epst[:, 0:1], scale=1.0),
                      reads=[pk, "eps"], writes=[key])
                kb.op("act", lambda e: e.activation(out=dst[:, o0:o0 + w], in_=dst[:, o0:o0 + w], func=AF.Exp, scale=-0.5), reads=[key], writes=[key])

        def rmsnorm_to_H(cols, gfn, tilekeys=False):
            a, b = cols
            for (t0, t1) in split_tiles(a, b):
                w = t1 - t0
                rms_stats((t0, t1), rstd, 0)
                for c in range(NCH):
                    kb.op("dve", lambda e, c=c: e.scalar_tensor_tensor(out=H[:, c, t0:t1], in0=X[:, c, t0:t1], scalar=gfn(c),
                                                                        in1=rstd[:, 0:w], op0=ALU.mult, op1=ALU.mult),
                          reads=[("X", c), "rstdbuf", "vecs"], writes=[("H", c, t0) if tilekeys else ("H", c)])

        def ffn(i):
            a = OUT_L[i]
            tiles = split_tiles(a, T)
            rmsnorm_to_H((a, T), lambda c: gffn(i, c), tilekeys=True)
            with ExitStack() as st2:
                act = st2.enter_context(sbuf_tensor("act", [128, 11, T - 15], BF16))
                sg = [st2.enter_context(sbuf_tensor(f"sg{k}", [128, 512], BF16)) for k in range(2)]
                sgn = 0
                for qd in range(4):
                    f = 0
                    while f < 11:
                        wdt = 2 if f + 2 <= 11 else 1
                        sgk, Wg = ws.next("wg")
                        suk, Wu = ws.next("wu")
                        Wg3 = Wg[:, 0:16 * wdt * 128].rearrange("p (a b) -> p a b", a=16)
                        Wu3 = Wu[:, 0:16 * wdt * 128].rearrange("p (a b) -> p a b", a=16)
                        for fl in range(wdt):
                            for (t0, t1) in tiles:
                                w = t1 - t0
                                gb, gk = bank()
                                ub, uk = bank()
                                hreads = [("H", c, t0) for c in range(NCH)]
                                kb.mm([(gb[:, 0:w], Wg3[:, kc, fl * 128:(fl + 1) * 128], H[:, kc, t0:t1], kc == 0, kc == 15) for kc in range(16)],
                                      reads=[("ring", sgk)] + hreads, writes=[gk])
                                kb.mm([(ub[:, 0:w], Wu3[:, kc, fl * 128:(fl + 1) * 128], H[:, kc, t0:t1], kc == 0, kc == 15) for kc in range(16)],
                                      reads=[("ring", suk)] + hreads, writes=[uk])
                                sgt = sg[sgn % 2]
                                sgkey = ("sg", sgn % 2)
                                sgn += 1
                                kb.op("act", lambda e, sgt=sgt, gb=gb: e.activation(out=sgt[:, 0:w], in_=gb[:, 0:w], func=AF.Silu),
                                      reads=[gk], writes=[sgkey])
                                kb.op("dve", lambda e, sgt=sgt, ub=ub, ff=f + fl: e.tensor_tensor(out=act[:, ff, t0 - 15:t1 - 15], in0=ub[:, 0:w],
                                                                                               in1=sgt[:, 0:w], op=ALU.mult),
                                      reads=[uk, sgkey], writes=[("act", f + fl, t0)])
                        ws.release(sgk)
                        ws.release(suk)
                        f += wdt
                    for mb in range(8):
                        sdk, Wd = ws.next("wd")
                        Wd3 = Wd[:, 0:11 * 256].rearrange("p (a b) -> p a b", a=11)
                        for ml in range(2):
                            m = mb * 2 + ml
                            for (t0, t1) in tiles:
                                w = t1 - t0
                                yb, yk = bank()
                                kb.mm([(yb[:, 0:w], Wd3[:, ff, ml * 128:(ml + 1) * 128], act[:, ff, t0 - 15:t1 - 15], ff == 0, ff == 10) for ff in range(11)],
                                      reads=[("ring", sdk)] + [("act", ff, t0) for ff in range(11)], writes=[yk])
                                kb.op("dve", lambda e, yb=yb, m=m: e.tensor_tensor(out=X[:, m, t0:t1], in0=yb[:, 0:w], in1=X[:, m, t0:t1], op=ALU.add),
                                      reads=[yk, ("X", m, t0)], writes=[("X", m, t0)])
                        ws.release(sdk)
                kb.barrier()

        def pool_layer(i):
            j = i // 2
            a_in, a_out = IN_L[i], OUT_L[i]
            n_p = TP - a_in
            EW = 16 + n_p + 16 * NS
            sb0 = 16 + n_p
            with ExitStack() as st2:
                A = st2.enter_context
                rs_all = A(sbuf_tensor("rs_all", [128, T], F32))
                sets = []
                for k_ in range(2):
                    sets.append(tuple(A(sbuf_tensor(f"{nm}{k_}", [128, 16 + TP + 16 * NS], F32)) for nm in ("hf", "bA", "bB")))
                pst = A(sbuf_tensor("pst", [128, NCH, NS, 15], F32))
                PO = A(sbuf_tensor("PO", [128, NCH, NPO], F32))
                rcf = A(sbuf_tensor("rcf", [128, 4, 16], F32))
                fxs = [A(sbuf_tensor(f"fx{k_}", [128, 16], F32)) for k_ in range(2)]
                kb.dma("sp", pst[:], poolst_d[j], writes=["pst"])
                kb.dma("sp", rcf[:], rcfix_d[:, :, :], writes=["rcf"])
                for k_, eng in enumerate(("dve", "dve")):
                    for buf, nm in zip(sets[k_], ("hf", "bA", "bB")):
                        kb.op(eng, lambda e, buf=buf: e.memset(buf[:, 0:16], 0.0), writes=[nm + str(k_)])
                rms_stats((a_in, T), rs_all, a_in, "rsall")
                for c in range(NCH):
                    g = c // 4
                    wwin = 2 << g
                    k_ = c % 2
                    eng = "dve"
                    hf, bA, bB = sets[k_]
                    fx = fxs[k_]
                    hk, fk = "hf" + str(k_), "fx" + str(k_)
                    hfs = hf[:, sb0:sb0 + 16 * NS].rearrange("p (s k) -> p s k", s=NS)
                    if eng == "dve":
                        kb.op(eng, lambda e, c=c, hf=hf: e.scalar_tensor_tensor(out=hf[:, 16:16 + n_p], in0=X[:, c, a_in:TP], scalar=gmix(i, c),
                                                                               in1=rs_all[:, a_in:TP], op0=ALU.mult, op1=ALU.mult),
                              reads=[("X", c), "vecs", "rsall"], writes=[hk])
                        kb.op(eng, lambda e, c=c, hfs=hfs: e.scalar_tensor_tensor(out=hfs[:, :, 15], in0=X[:, c, TP:T], scalar=gmix(i, c),
                                                                                 in1=rs_all[:, TP:T], op0=ALU.mult, op1=ALU.mult),
                              reads=[("X", c), "rsall"], writes=[hk])
                    else:
                        kb.op("act", lambda e, c=c, hf=hf: e.activation(out=hf[:, 16:16 + n_p], in_=X[:, c, a_in:TP], func=AF.Copy, scale=gmix(i, c)),
                              reads=[("X", c), "vecs"], writes=[hk])
                        kb.op("act", lambda e, c=c, hfs=hfs: e.activation(out=hfs[:, :, 15], in_=X[:, c, TP:T], func=AF.Copy, scale=gmix(i, c)),
                              reads=[("X", c), "vecs"], writes=[hk])
                        kb.op(eng, lambda e, hf=hf: e.tensor_tensor(out=hf[:, 16:16 + n_p], in0=hf[:, 16:16 + n_p], in1=rs_all[:, a_in:TP], op=ALU.mult),
                              reads=[hk, "rsall"], writes=[hk])
                        kb.op(eng, lambda e, hfs=hfs: e.tensor_tensor(out=hfs[:, :, 15], in0=hfs[:, :, 15], in1=rs_all[:, TP:T], op=ALU.mult),
                              reads=[hk, "rsall"], writes=[hk])
                    kb.op("act", lambda e, c=c, hfs=hfs: e.copy(out=hfs[:, :, 0:15], in_=pst[:, c, :, :]), reads=["pst"], writes=[hk])
                    kb.op("act", lambda e, c=c, hf=hf: e.copy(out=PO[:, c, 0:15], in_=hf[:, 16 + n_p - 15:16 + n_p]), reads=[hk], writes=["PO"])
                    kb.op("act", lambda e, c=c, hfs=hfs: e.copy(out=PO[:, c, 15:NPO], in_=hfs[:, :, 15]), reads=[hk], writes=["PO"])
                    src, sname = hf, hk
                    bufs = [(bA, "bA" + str(k_)), (bB, "bB" + str(k_))]
                    sh = 1
                    kk = 0
                    while sh < wwin:
                        dst, dname = bufs[kk % 2]
                        kb.op(eng, lambda e, src=src, dst=dst, sh=sh: e.tensor_tensor(out=dst[:, 16:EW], in0=src[:, 16:EW], in1=src[:, 16 - sh:EW - sh], op=ALU.add),
                              reads=[sname], writes=[dname])
                        src, sname = dst, dname
                        sh *= 2
                        kk += 1
                    srs = src[:, sb0:sb0 + 16 * NS].rearrange("p (s k) -> p s k", s=NS)
                    if eng == "dve":
                        kb.op(eng, lambda e, src=src, c=c, hf=hf: e.scalar_tensor_tensor(out=H[:, c, a_in:TP], in0=src[:, 16:16 + n_p], scalar=1.0 / wwin,
                                                                                        in1=hf[:, 16:16 + n_p], op0=ALU.mult, op1=ALU.subtract),
                              reads=[sname, hk], writes=[("H", c)])
                        kb.op(eng, lambda e, srs=srs, c=c, hfs=hfs: e.scalar_tensor_tensor(out=H[:, c, TP:T], in0=srs[:, :, 15], scalar=1.0 / wwin,
                                                                                          in1=hfs[:, :, 15], op0=ALU.mult, op1=ALU.subtract),
                              reads=[sname, hk], writes=[("H", c)])
                    else:
                        mbuf, mname = bufs[kk % 2]
                        kb.op("act", lambda e, src=src, mbuf=mbuf: e.activation(out=mbuf[:, 16:EW], in_=src[:, 16:EW], func=AF.Copy, scale=1.0 / wwin),
                              reads=[sname], writes=[mname])
                        mbs = mbuf[:, sb0:sb0 + 16 * NS].rearrange("p (s k) -> p s k", s=NS)
                        kb.op(eng, lambda e, mbuf=mbuf, c=c, hf=hf: e.tensor_tensor(out=H[:, c, a_in:TP], in0=mbuf[:, 16:16 + n_p], in1=hf[:, 16:16 + n_p], op=ALU.subtract),
                              reads=[mname, hk], writes=[("H", c)])
                        kb.op(eng, lambda e, mbs=mbs, c=c, hfs=hfs: e.tensor_tensor(out=H[:, c, TP:T], in0=mbs[:, :, 15], in1=hfs[:, :, 15], op=ALU.subtract),
                              reads=[mname, hk], writes=[("H", c)])
                    f0 = 16 + FIX0 - a_in
                    kb.op(eng, lambda e, src=src, g=g, fx=fx: e.tensor_tensor(out=fx[:], in0=src[:, f0:f0 + 16], in1=rcf[:, g, :], op=ALU.mult),
                          reads=[sname, "rcf"], writes=[fk])
                    kb.op(eng, lambda e, c=c, fx=fx, hf=hf: e.tensor_tensor(out=H[:, c, FIX0:FIX0 + 16], in0=fx[:], in1=hf[:, f0:f0 + 16], op=ALU.subtract),
                          reads=[fk, hk], writes=[("H", c)])
                kb.dma("sp", po_o[j], PO[:], reads=["PO"])
                tiles = split_tiles(a_out, T)
                for g in range(4):
                    for hfb in range(2):
                        sk, Wp = ws.next("pool")
                        Wp3 = Wp[:, 0:1024].rearrange("p (a b) -> p a b", a=4)
                        for ml in range(2):
                            m = g * 4 + hfb * 2 + ml
                            for (t0, t1) in tiles:
                                w = t1 - t0
                                yb, yk = bank()
                                kb.mm([(yb[:, 0:w], Wp3[:, kc, ml * 128:(ml + 1) * 128], H[:, g * 4 + kc, t0:t1], kc == 0, kc == 3) for kc in range(4)],
                                      reads=[("ring", sk)] + [("H", g * 4 + kc) for kc in range(4)], writes=[yk])
                                kb.op("dve", lambda e, yb=yb, m=m: e.scalar_tensor_tensor(out=X[:, m, t0:t1], in0=yb[:, 0:w], scalar=pscl(j, m),
                                                                                          in1=X[:, m, t0:t1], op0=ALU.mult, op1=ALU.add),
                                      reads=[yk, ("X", m), "vecs"], writes=[("X", m)])
                        ws.release(sk)
                kb.barrier()

        QW = QWMAX

        def swa_layer(i):
            j = i // 2
            ks, qs = IN_L[i], OUT_L[i]
            assert qs == ks + 128
            nkb = -(-(TP - ks) // 128)
            rmsnorm_to_H((ks, T), lambda c: gmix(i, c))
            kb.rot = [0, 1, 2, 3]
            with ExitStack() as st2:
                A = st2.enter_context
                KT = A(sbuf_tensor("KT", [128, 2, T], BF16))
                KTf = A(sbuf_tensor("KTf", [128, 2, 128 + NS], F32))
                V = A(sbuf_tensor("V", [128, nkb, 256], BF16))
                Vs = A(sbuf_tensor("Vs", [128, NS, 256], BF16))
                KTs = A(sbuf_tensor("KTs", [128, NS, 2, 128], BF16))
                VP = A(sbuf_tensor("VP", [128, 256], F32))
                QO = A(sbuf_tensor("QO", [128, NCH, QW], BF16))
                cst = A(sbuf_tensor("cst", [128, 2, QW], F32))
                xn = A(sbuf_tensor("xn", [128, QW], F32))
                rq = A(sbuf_tensor("rq", [128, QW], F32))
                kbs = A(sbuf_tensor("kbs", [128, 12], F32))
                rden = rstd
                kb.dma("sp", kbs[:], kbias_d[:, j, :], writes=["kbs"])
                kb.sync_to_barrier("pool")
                kb.dma("pool", KTs[:, :, :, 1:128], kcT_d[j], writes=["KTs_c"])
                kb.dma("pool", Vs[1:128], vc_d[j], writes=["Vs_c"])
                hreads = [("H", c) for c in range(NCH)]

                def load_tables(t0, t1):
                    w = t1 - t0
                    kb.dma("sp", cst[:, 0, 0:w], cos_d[:, t0:t1], writes=["cst"])
                    kb.dma("sp", cst[:, 1, 0:w], sin_d[:, t0:t1], writes=["cst"])

                xn2 = A(sbuf_tensor("xn2", [128, QW], F32))
                rq2 = A(sbuf_tensor("rq2", [128, QW], F32))
                sqb2 = A(sbuf_tensor("sqb2", [128, QW], BF16))
                qkn = [0]
                ptb = [sq[1], sq[2], A(sbuf_tensor("ptb2", [128, 512], BF16)), A(sbuf_tensor("ptb3", [128, 512], BF16))]

                def qk_post_gen(pb, pk, w, gvec, outs):
                    par = qkn[0] % 2
                    qkn[0] += 1
                    sb, sbk = (sq[0], ("sq", 0)) if par == 0 else (sqb2, "sqb2")
                    xn_, xk = (xn, "xn") if par == 0 else (xn2, "xn2")
                    rq_, rk_ = (rq, "rq") if par == 0 else (rq2, "rq2")
                    kb.op("act", lambda e: e.activation(out=sb[:, 0:w], in_=pb[:, 0:w], func=AF.Square), reads=[pk], writes=[sbk])
                    yield
                    mb_, mk = bank()
                    kb.mm([(mb_[:, 0:w], blk64[:], sb[:, 0:w], True, True)], reads=[sbk, "blk64"], writes=[mk])
                    kb.op("act", lambda e: e.activation(out=rq_[:, 0:w], in_=mb_[:, 0:w], func=AF.Ln, bias=epst[:, 0:1], scale=1.0),
                          reads=[mk, "eps"], writes=[rk_])
                    kb.op("act", lambda e: e.activation(out=rq_[:, 0:w], in_=rq_[:, 0:w], func=AF.Exp, scale=-0.5), reads=[rk_], writes=[rk_])
                    kb.op("dve", lambda e: e.scalar_tensor_tensor(out=xn_[:, 0:w], in0=pb[:, 0:w], scalar=gvec, in1=rq_[:, 0:w], op0=ALU.mult, op1=ALU.mult),
                          reads=[pk, rk_, "vecs"], writes=[xk])
                    yield
                    rb, rk = bank()
                    kb.mm([(rb[:, 0:w], RT, xn_[:, 0:w], True, True)], reads=[xk, "cmat"], writes=[rk])
                    kb.op("dve", lambda e: e.tensor_tensor(out=rq_[:, 0:w], in0=rb[:, 0:w], in1=cst[:, 1, 0:w], op=ALU.mult),
                          reads=[rk, "cst", rk_], writes=[rk_])
                    kb.op("dve", lambda e: e.tensor_tensor(out=xn_[:, 0:w], in0=xn_[:, 0:w], in1=cst[:, 0, 0:w], op=ALU.mult),
                          reads=[xk, "cst"], writes=[xk])
                    for (oap, okey, lo, hi) in outs:
                        kb.op("dve", lambda e, oap=oap, lo=lo, hi=hi: e.tensor_tensor(out=oap, in0=rq_[:, lo:hi], in1=xn_[:, lo:hi], op=ALU.add),
                              reads=[rk_, xk], writes=[okey])

                def run_interleaved(gens):
                    gens = list(gens)
                    while gens:
                        nxt = []
                        for g_ in gens:
                            try:
                                next(g_)
                                nxt.append(g_)
                            except StopIteration:
                                pass
                        gens = nxt

                def qk_post(pb, pk, w, gvec, outs):
                    run_interleaved([qk_post_gen(pb, pk, w, gvec, outs)])

                sk, Wk = ws.next("wk")
                Wk3 = Wk[:, 0:4096].rearrange("p (a b) -> p a b", a=16)
                for (t0, t1) in split_tiles(ks, T, QW):
                    w = t1 - t0
                    load_tables(t0, t1)
                    gens = []
                    for gp in range(2):
                        pb, pk = bank()
                        kb.mm([(pb[:, 0:w], Wk3[:, kc, gp * 128:(gp + 1) * 128], H[:, kc, t0:t1], kc == 0, kc == 15) for kc in range(16)],
                              reads=[("ring", sk)] + hreads, writes=[pk])
                        outs = [(KT[:, gp, t0:t1], "KT", 0, w)]
                        lo = max(t0, LASTK0)
                        if lo < t1:
                            outs.append((KTf[:, gp, lo - LASTK0:t1 - LASTK0], "KTf", lo - t0, w))
                        gens.append(qk_post_gen(pb, pk, w, kg(j), outs))
                    run_interleaved(gens)
                ws.release(sk)
                kb.dma("sp", kT_o[j], KTf[:], reads=["KTf"])
                if swa_stop <= 1:
                    return "stop"
                for s in range(NS):
                    kb.op("dve", lambda e, s=s: e.tensor_copy(out=KTs[:, s, :, 0], in_=KT[:, :, TP + s]), reads=["KT"], writes=["KTs_n"])
                if swa_stop <= 1.2:
                    return "stop"
                sv, Wv = ws.next("wv")
                Wv3 = Wv[:, 0:4096].rearrange("p (a b) -> p a b", a=16)
                for kbi in range(nkb):
                    c0 = ks + 128 * kbi
                    kn = min(128, TP - c0)
                    vb, vk = bank()
                    kb.mm([(vb[0:kn, 0:256], H[:, kc, c0:c0 + kn], Wv3[:, kc, :], kc == 0, kc == 15) for kc in range(16)],
                          reads=[("ring", sv)] + hreads, writes=[vk])
                    kb.op("act", lambda e, vb=vb, kbi=kbi, kn=kn: e.copy(out=V[0:kn, kbi, :], in_=vb[0:kn, 0:256]), reads=[vk], writes=[("V", kbi)])
                if swa_stop <= 1.4:
                    return "stop"
                vb, vk = bank()
                kb.mm([(vb[:, 0:256], H[:, kc, LASTK0:TP], Wv3[:, kc, :], kc == 0, kc == 15) for kc in range(16)],
                      reads=[("ring", sv)] + hreads, writes=[vk])
                kb.op("act", lambda e, vb=vb: e.copy(out=VP[:], in_=vb[:, 0:256]), reads=[vk], writes=["VP"])
                kb.dma("sp", vp_o[j], VP[:], reads=["VP"])
                if swa_stop <= 1.6:
                    return "stop"
                for sp_ in range(2):
                    vb, vk = bank()
                    mms = []
                    for sl in range(2):
                        s = sp_ * 2 + sl
                        mms += [(vb[0:2, sl * 256:(sl + 1) * 256], H[:, kc, TP + s:TP + s + 2], Wv3[:, kc, :], kc == 0, kc == 15) for kc in range(16)]
                    kb.mm(mms, reads=[("ring", sv)] + hreads, writes=[vk])
                    kb.op("act", lambda e, vb=vb: e.copy(out=rden[0:1, 0:512], in_=vb[0:1, 0:512]), reads=[vk], writes=["rstdbuf"])
                    kb.op("dve", lambda e, vb=vb, sp_=sp_: e.tensor_copy(out=Vs[0:1, sp_ * 2:sp_ * 2 + 2, :],
                                                                         in_=vb[0:1, 0:512].rearrange("p (s d) -> p s d", s=2)),
                          reads=[vk], writes=["Vs_n"])
                    kb.dma("sp", vs_o[j, :, sp_ * 2:sp_ * 2 + 2, :], rden[0:1, 0:512].rearrange("p (s d) -> p s d", s=2), reads=["rstdbuf"])
                ws.release(sv)
                if swa_stop <= 2:
                    return "stop"

                qtl = q_tiles(qs)
                qtiles = [(a0_, a1_, n_ == len(qtl) - 1) for n_, (a0_, a1_) in enumerate(qtl)]
                ptn = [0]
                hn = [0]
                for (t0, t1p, has_s) in qtiles:
                    t1 = T if has_s else t1p
                    w = t1 - t0
                    wp = t1p - t0
                    assert w <= QW and wp + NS * (NS + 1) <= 512
                    load_tables(t0, t1)
                    kb0 = (t0 - ks) // 128
                    kbl = (t1p - 1 - ks) // 128
                    for gp in range(2):
                        for ip in range(4):
                            sk, Wq = ws.next("wq")
                            Wq3 = Wq[:, 0:4096].rearrange("p (a b) -> p a b", a=16)
                            gens = []
                            for cl in range(2):
                                c = gp * 8 + ip * 2 + cl
                                pb, pk = bank()
                                kb.mm([(pb[:, 0:w], Wq3[:, kc, cl * 128:(cl + 1) * 128], H[:, kc, t0:t1], kc == 0, kc == 15) for kc in range(16)],
                                      reads=[("ring", sk)] + hreads, writes=[pk])
                                gens.append(qk_post_gen(pb, pk, w, qg(j), [(QO[:, c, 0:w], ("QO", c), 0, w)]))
                            ws.release(sk)
                            run_interleaved(gens)
                            steps = []
                            for cl in range(2):
                                c = gp * 8 + ip * 2 + cl
                                ii = ip * 2 + cl
                                hds = []
                                for hh in range(2):
                                    par = hn[0] % 2
                                    hn[0] += 1
                                    hds.append(dict(c=c, hh=hh, hq=8 * (2 * gp + hh) + ii, r0=hh * 64, r1=hh * 64 + 64,
                                                    ob=ps[4 + par * 2], ok=("ps", 4 + par * 2), db=ps[5 + par * 2], dk=("ps", 5 + par * 2)))
                                csteps = [("p", hds, kbi) for kbi in range(kb0 - 1, kbl + 1)]
                                if has_s:
                                    csteps += [("s", hds, s_) for s_ in range(NS)]
                                csteps[-1] = csteps[-1] + (True,)
                                steps += csteps

                            def s_phase(st_):
                                kind, hds, idx = st_[0], st_[1], st_[2]
                                c = hds[0]["c"]
                                sb_, sk_ = bank()
                                pi = ptn[0] % 4
                                ptn[0] += 1
                                pt, pkey = ptb[pi], (("sq", pi + 1) if pi < 2 else ("ptb", pi))
                                if kind == "p":
                                    kc0 = ks + 128 * idx
                                    kn = min(128, TP - kc0)
                                    q0 = max(t0, kc0)
                                    q1 = min(t1p, kc0 + 256)
                                    nq = q1 - q0
                                    off = q0 - kc0
                                    sc = lambda hh: (sb_[0:kn, hh * 256:hh * 256 + nq], KT[hh * 64:hh * 64 + 64, gp, kc0:kc0 + kn],
                                                     QO[hh * 64:hh * 64 + 64, c, q0 - t0:q1 - t0], True, True)
                                    kb.mm([sc(0), (sb_[0:kn, 256:260], ones1[:, 0:kn], ones1[:, 0:4], True, True), sc(1)],
                                          reads=["KT", ("QO", c), "ones1"], writes=[sk_])
                                    sv_ = sb_[0:kn, 0:512].rearrange("p (h q) -> p h q", h=2)[:, :, 0:nq]
                                    pv_ = pt[0:kn, 0:512].rearrange("p (h q) -> p h q", h=2)[:, :, 0:nq]
                                    kb.op("act", lambda e: e.activation(out=pv_, in_=sv_, func=AF.Exp, bias=kbs[0:kn, idx:idx + 1], scale=0.125),
                                          reads=[sk_, "kbs"], writes=[pkey])
                                    kb.op("dve", lambda e: e.tensor_tensor(out=pv_, in0=pv_, in1=maskb[0:kn, :, off:off + nq], op=ALU.mult),
                                          reads=[pkey, "maskb"], writes=[pkey])
                                    return (pt, pkey, kn, q0, q1)
                                ss = lambda hh: (sb_[:, hh * NS:(hh + 1) * NS], KTs[hh * 64:hh * 64 + 64, idx, gp, :],
                                                 QO[hh * 64:hh * 64 + 64, c, wp:wp + NS], True, True)
                                kb.mm([ss(0), (sb_[:, NS:2 * NS], ones1[:, :], ones1[:, 0:NS], True, True), ss(1)],
                                      reads=["KTs_c", "KTs_n", ("QO", c), "ones1"], writes=[sk_])
                                kb.op("act", lambda e: e.activation(out=pt[:, 0:2 * NS], in_=sb_[:, 0:2 * NS], func=AF.Exp, scale=0.125),
                                      reads=[sk_], writes=[pkey])
                                return (pt, pkey)

                            def p_phase(st_, sres):
                                kind, hds, idx = st_[0], st_[1], st_[2]
                                for hd in hds:
                                    hh = hd["hh"]
                                    ob, ok_, db, dk = hd["ob"], hd["ok"], hd["db"], hd["dk"]
                                    if kind == "p":
                                        pt, pkey, kn, q0, q1 = sres
                                        first = not hd.get("opened", False)
                                        hd["opened"] = True
                                        rhs = pt[0:kn, hh * 256:hh * 256 + q1 - q0]
                                        kb.mm([(ob[:, q0 - t0:q1 - t0], V[0:kn, idx, gp * 128:(gp + 1) * 128], rhs, first, True, not first)],
                                              reads=[pkey, ("V", idx)], writes=[ok_])
                                        kb.mm([(db[:, q0 - t0:q1 - t0], ones1[0:kn, :], rhs, first, True, not first)], reads=[pkey, "ones1"], writes=[dk])
                                    else:
                                        pt, pkey = sres
                                        o0 = wp + NS * idx
                                        rhs = pt[:, hh * NS:(hh + 1) * NS]
                                        kb.mm([(ob[:, o0:o0 + NS], Vs[:, idx, gp * 128:(gp + 1) * 128], rhs, False, True, True)],
                                              reads=[pkey, "Vs_c", "Vs_n"], writes=[ok_])
                                        kb.mm([(db[:, o0:o0 + NS], ones1[:], rhs, False, True, True)], reads=[pkey, "ones1"], writes=[dk])
                                if len(st_) > 3:
                                    for hd in hds:
                                        finish(hd)

                            def finish(hd):
                                r0, r1, c, hq = hd["r0"], hd["r1"], hd["c"], hd["hq"]
                                ob, ok_, db, dk = hd["ob"], hd["ok"], hd["db"], hd["dk"]
                                rkey = ("rden", hd["hh"])
                                esk = esink[r0:r1, j * 32 + hq:j * 32 + hq + 1]
                                parts = []
                                if wp > 0:
                                    parts.append((db[r0:r1, 0:wp], ob[r0:r1, 0:wp], rden[r0:r1, 0:wp], QO[r0:r1, c, 0:wp]))
                                if has_s:
                                    dg = lambda t: t[r0:r1, wp:wp + NS * (NS + 1)].rearrange("p (a b) -> p a b", b=NS + 1)[:, :, 0]
                                    parts.append((dg(db), dg(ob), rden[r0:r1, wp:wp + NS], QO[r0:r1, c, wp:wp + NS]))
                                for (dsrc, osrc, rd, qo) in parts:
                                    kb.op("act", lambda e, dsrc=dsrc, rd=rd: e.activation(out=rd, in_=dsrc, func=AF.Ln, bias=esk, scale=1.0),
                                          reads=[dk, "esink"], writes=[rkey, "rstdbuf"])
                                    kb.op("act", lambda e, rd=rd: e.activation(out=rd, in_=rd, func=AF.Exp, scale=-1.0), reads=[rkey], writes=[rkey])
                                    kb.op("dve", lambda e, osrc=osrc, rd=rd, qo=qo: e.tensor_tensor(out=qo, in0=osrc, in1=rd, op=ALU.mult),
                                          reads=[ok_, rkey], writes=[("QO", c)])

                            LEAD = 3
                            pend = []
                            for st_ in steps:
                                pend.append((st_, s_phase(st_)))
                                if len(pend) > LEAD:
                                    p_phase(*pend.pop(0))
                            while pend:
                                p_phase(*pend.pop(0))
                    if swa_stop <= 3:
                        return "stop"
                    for mq in range(4):
                        wos = []
                        for gp in range(2):
                            sk, Wo = ws.next("wo")
                            wos.append((sk, Wo[:, 0:4096].rearrange("p (i m) -> p i m", i=8)))
                        for ml in range(4):
                            m = mq * 4 + ml
                            yb, yk = bank()
                            kb.mm([(yb[:, 0:w], wos[kc // 8][1][:, kc % 8, ml * 128:(ml + 1) * 128], QO[:, kc, 0:w], kc == 0, kc == 15) for kc in range(16)],
                                  reads=[("ring", wos[0][0]), ("ring", wos[1][0])] + [("QO", c) for c in range(NCH)], writes=[yk])
                            kb.op("dve", lambda e, yb=yb, m=m: e.tensor_tensor(out=X[:, m, t0:t1], in0=yb[:, 0:w], in1=X[:, m, t0:t1], op=ALU.add),
                                  reads=[yk, ("X", m)], writes=[("X", m)])
                        for (sk, _) in wos:
                            ws.release(sk)
                    if swa_stop <= 4:
                        return "stop"
                kb.rot = list(range(8))
                kb.barrier()

        nph = 0
        for i in range(4 if stop > 0 else 0):
            if nph >= stop:
                break
            if i % 2 == 0:
                pool_layer(i)
            else:
                if swa_layer(i) == "stop":
                    kb.rot = list(range(8))
                    kb.barrier()
                    break
            nph += 1
            if nph >= stop:
                break
            ffn(i)
            nph += 1
        assert stop < 99 or ws.taken == len(plan), (ws.taken, len(plan))
        for c in range(NCH):
            kb.dma("sp", yT_o[:, c, :], X[:, c, HALO:T])
        kb.barrier(engines=("sp",))
    return nc


_CACHE = {}


def _const_inputs():
    ident = np.eye(128, dtype=np.float32)
    R = np.zeros((128, 128), np.float32)
    for hb in (0, 64):
        for m in range(8):
            R[hb + m, hb + m + 8] = -1.0
            R[hb + m + 8, hb + m] = 1.0
    blk = np.zeros((128, 128), np.float32)
    blk[:64, :64] = 1.0 / 64
    blk[64:, 64:] = 1.0 / 64
    cmat = np.stack([ident, R.T.copy(), blk, np.zeros((128, 128), np.float32)], axis=1)
    ii = np.arange(128)[:, None]
    cc = np.arange(256)[None, :]
    maskb = np.where((cc >= ii) & (cc < ii + 128), 1.0, 0.0).astype(np.float32)
    return np.ascontiguousarray(cmat), maskb


def kernel(**inputs):
    in_maps = make_in_maps(**inputs)
    if "nc" not in _CACHE:
        _CACHE["nc"] = build_program()
    res = run_bass_kernel_spmd(_CACHE["nc"], in_maps, core_ids=list(range(8)))
    return assemble(res.results)


def make_in_maps(x_prompt, x_sample, state_pool, cache_k, cache_v, meta_tokens, norm_mix, norm_ffn,
                 pool_w, pool_scale, w_qkv, w_o, q_norm, k_norm, sinks, w_gate, w_up, w_down):
    f32 = np.float32
    x_prompt = np.asarray(x_prompt, f32)
    x_sample = np.asarray(x_sample, f32)
    state_pool = np.asarray(state_pool, f32)
    cache_k = np.asarray(cache_k, f32)
    cache_v = np.asarray(cache_v, f32)
    B = x_prompt.shape[0]
    cmat, maskb = _const_inputs()

    def chunked(v):
        v = np.asarray(v, f32)
        return v.reshape(v.shape[0], 16, 128).transpose(2, 0, 1)

    vecs = np.zeros((128, 296), f32)
    vecs[:, 0:64] = chunked(norm_mix).reshape(128, 64)
    vecs[:, 64:128] = chunked(norm_ffn).reshape(128, 64)
    vecs[:, 128:160] = chunked(pool_scale).reshape(128, 32)
    vecs[:, 160:162] = np.tile(np.asarray(q_norm, f32).T, (2, 1))
    vecs[:, 162:164] = np.tile(np.asarray(k_norm, f32).T, (2, 1))
    vecs[:, 164:228] = np.broadcast_to(np.asarray(sinks, f32).reshape(1, 64), (128, 64))

    half = 8
    inv = (f32(500000.0) ** (-np.arange(half, dtype=f32) * f32(2.0) / f32(16))).astype(f32)
    dloc = np.arange(128) % 64

    shared = {
        "cmat": cmat, "maskb": maskb, "vecs": vecs,
        "pool_w": np.asarray(pool_w, f32), "w_qkv": np.asarray(w_qkv, f32), "w_o": np.asarray(w_o, f32),
        "w_gate": np.asarray(w_gate, f32), "w_up": np.asarray(w_up, f32), "w_down": np.asarray(w_down, f32),
    }
    in_maps = []
    for core in range(8):
        b, q = divmod(core, 4)
        O = q * NOWN
        pos = O - HALO + np.arange(TP)
        valid = pos >= 0
        xext = np.concatenate([np.asarray(meta_tokens, f32), x_prompt[b]], axis=0)
        cols = np.zeros((T, D), f32)
        cols[:TP][valid] = xext[pos[valid]]
        sidx = np.arange(NS) + NS * core
        cols[TP:] = x_sample[sidx, 0]
        xT = np.ascontiguousarray(cols.reshape(T, 16, 128).transpose(2, 1, 0))
        pfull = np.concatenate([np.maximum(pos, 0), np.full(NS, PAST)]).astype(f32)
        ang = (pfull[:, None] * inv[None, :]).astype(f32)
        cosv, sinv = np.cos(ang).astype(f32), np.sin(ang).astype(f32)
        cosT = np.ones((128, T), f32)
        sinT = np.zeros((128, T), f32)
        for p in range(128):
            d = dloc[p]
            if d < 16:
                cosT[p] = cosv[:, d % 8]
                sinT[p] = sinv[:, d % 8]
        kbias = np.zeros((128, 2, 12), f32)
        for jj, ks in enumerate((IN_L[1], IN_L[3])):
            for kbi in range(12):
                ccol = ks + 128 * kbi + np.arange(128)
                ok = (ccol < TP) & (np.where(ccol < TP, O - HALO + ccol, -1) >= 0)
                kbias[:, jj, kbi] = np.where(ok, 0.0, NEGM)
        rcfix = np.zeros((128, 4, 16), f32)
        for g in range(4):
            wwin = 2 << g
            cnt = np.minimum(wwin, O + np.arange(16) + 1).astype(f32)
            rcfix[:, g, :] = (f32(1.0) / cnt)[None, :]
        sp = state_pool[:, sidx]
        poolstT = np.ascontiguousarray(sp.reshape(2, NS, 15, 16, 128).transpose(0, 4, 3, 1, 2))
        ck = cache_k[:, sidx].reshape(2, NS, 128, 4, 64)
        kt = ck[:, :, 1:].reshape(2, NS, 127, 2, 2, 64)
        kcT = np.ascontiguousarray(kt.transpose(0, 4, 5, 1, 3, 2).reshape(2, 128, NS, 2, 127))
        cv = cache_v[:, sidx].reshape(2, NS, 128, 256)
        vc = np.ascontiguousarray(cv[:, :, 1:].transpose(0, 2, 1, 3))
        m = dict(shared)
        m.update({
            "xT": xT, "cosT": cosT, "sinT": sinT, "kbias": kbias, "rcfix": rcfix,
            "poolstT": poolstT, "kcT": kcT, "vc": vc,
            "ck": np.ascontiguousarray(ck.reshape(2, NS, 128, 256)), "cv": np.ascontiguousarray(cv),
            "spool": np.ascontiguousarray(sp),
        })
        in_maps.append(m)

    return in_maps


def assemble(R, cores=range(8), B=2):
    f32 = np.float32
    y_prompt = np.zeros((B, 4096, D), f32)
    y_sample = np.zeros((32, 1, D), f32)
    new_pool_p = np.zeros((2, B, 15, D), f32)
    new_k_p = np.zeros((2, B, 128, 4, 64), f32)
    new_v_p = np.zeros((2, B, 128, 4, 64), f32)
    new_pool_s = np.zeros((2, 32, 15, D), f32)
    new_k_s = np.zeros((2, 32, 128, 4, 64), f32)
    new_v_s = np.zeros((2, 32, 128, 4, 64), f32)
    for core in cores:
        b, q = divmod(core, 4)
        r = R[core]
        yt = np.asarray(r["yT"]).transpose(2, 1, 0).reshape(NOWN + NS, D)
        rows = yt[:NOWN]
        if q == 0:
            y_prompt[b, 0:NOWN - 16] = rows[16:]
        else:
            y_prompt[b, q * NOWN - 16:(q + 1) * NOWN - 16] = rows
        sidx = np.arange(NS) + NS * core
        y_sample[sidx, 0] = yt[NOWN:]
        po = np.asarray(r["poolT"]).transpose(0, 3, 2, 1).reshape(2, NPO, D)
        kn = np.asarray(r["knewT"]).reshape(2, 2, 64, 2, 128 + NS).transpose(0, 4, 3, 1, 2).reshape(2, 128 + NS, 4, 64)
        if q == 3:
            new_pool_p[:, b] = po[:, :15]
            new_k_p[:, b] = kn[:, :128]
            new_v_p[:, b] = np.asarray(r["vnew_p"]).reshape(2, 128, 4, 64)
        new_pool_s[:, sidx, 14] = po[:, 15:]
        new_pool_s[:, sidx, :14] = np.asarray(r["pold_s"])
        new_k_s[:, sidx, :127] = np.asarray(r["kold_s"]).reshape(2, NS, 127, 4, 64)
        new_k_s[:, sidx, 127] = kn[:, 128:]
        new_v_s[:, sidx, :127] = np.asarray(r["vold_s"]).reshape(2, NS, 127, 4, 64)
        new_v_s[:, sidx, 127] = np.asarray(r["vnew_s"])[:, 0].reshape(2, NS, 4, 64)
    return (y_prompt, y_sample, new_pool_p, new_k_p, new_v_p, new_pool_s, new_k_s, new_v_s)
```

```python
import numpy as np
from contextlib import ExitStack
import concourse.bass as bass
import concourse.mybir as mybir
from concourse.bass_utils import run_bass_kernel_spmd

F32 = mybir.dt.float32
BF16 = mybir.dt.bfloat16
AF = mybir.ActivationFunctionType
ALU = mybir.AluOpType

D = 2048
NCH = 16
DFF = 5632
NFC = 44
NOWN = 1028
HALO = 286
TP = NOWN + HALO
NS = 4
T = TP + NS
SEQ_EXT = 4112
PAST = 16384
NEGM = -30000.0
EPS = 1e-6
IN_L = [0, 15, 143, 158]
OUT_L = [15, 143, 158, 286]
RING_SLOTS = 4
RING_EL = 4096
LASTK0 = TP - 128
NPO = 19
FIX0 = HALO


QSTEP = 384
QWMAX = 416


def q_tiles(qs):
    tl = [(a, min(a + QSTEP, TP)) for a in range(qs, TP, QSTEP)]
    if len(tl) > 1 and (tl[-1][1] - tl[-1][0]) + NS <= QWMAX - QSTEP:
        tl[-2] = (tl[-2][0], TP)
        tl.pop()
    return tl


def split_tiles(a, b, maxw=512):
    n = b - a
    k = -(-n // maxw)
    base, rem = divmod(n, k)
    out = []
    s = a
    for i in range(k):
        w = base + (1 if i < rem else 0)
        out.append((s, s + w))
        s += w
    return out


class Slot:
    __slots__ = ("writer", "readers")

    def __init__(self):
        self.writer = None
        self.readers = {}


class KB:
    def __init__(self, nc, st):
        self.nc = nc
        self.eng = {"pe": nc.tensor, "act": nc.scalar, "dve": nc.vector, "pool": nc.gpsimd, "sp": nc.sync}
        self.sem = {k: st.enter_context(nc.semaphore("s_" + k)) for k in self.eng}
        self.cnt = {k: 0 for k in self.eng}
        self.waited = {}
        self.slots = {}
        self.ndsem = 20
        self.dsem = [st.enter_context(nc.semaphore(f"d{i}")) for i in range(self.ndsem)]
        self.dval = [0] * self.ndsem
        self.drr = 0
        self.npsem = 4
        self.psem = [st.enter_context(nc.semaphore(f"pd{i}")) for i in range(self.npsem)]
        self.pval = [0] * self.npsem
        self.prr = 0
        self.rsem = [st.enter_context(nc.semaphore(f"r{i}")) for i in range(RING_SLOTS)]
        self.rval = [0] * RING_SLOTS
        self.live_dma = []

    def slot(self, key):
        s = self.slots.get(key)
        if s is None:
            s = Slot()
            self.slots[key] = s
        return s

    def wait(self, e, tok):
        if tok is None:
            return
        kind, key, val = tok
        if kind == "e":
            if key == e and e in ("pe", "sp"):
                return
            sem = self.sem[key]
        else:
            sem = key
        wk = (e, id(sem))
        if self.waited.get(wk, 0) >= val:
            return
        self.waited[wk] = val
        self.eng[e].wait_ge(sem, val)

    def _deps(self, e, reads, writes):
        for k in reads:
            s = self.slot(k)
            self.wait(e, s.writer)
            if isinstance(k, tuple) and k[0] == "ps":
                for rk, r in s.readers.items():
                    if rk != e:
                        self.wait(e, r)
        for k in writes:
            s = self.slot(k)
            self.wait(e, s.writer)
            for r in s.readers.values():
                self.wait(e, r)

    def _mark(self, tok, reads, writes, rkey):
        for k in reads:
            self.slot(k).readers[rkey] = tok
        for k in writes:
            s = self.slot(k)
            s.writer = tok
            s.readers = {}

    def op(self, e, fn, reads=(), writes=()):
        self._deps(e, reads, writes)
        ins = fn(self.eng[e])
        self.cnt[e] += 1
        ins.then_inc(self.sem[e], 1)
        tok = ("e", e, self.cnt[e])
        self._mark(tok, reads, writes, e)
        return tok

    def mm(self, mms, reads=(), writes=()):
        self._deps("pe", reads, writes)
        ins = None
        for mmi in mms:
            o, l, r, s0, s1 = mmi[:5]
            if len(mmi) > 5 and mmi[5]:
                ins = self.nc.tensor.matmul(o, l, r, start=s0, stop=s1, skip_group_check=True)
            else:
                ins = self.nc.tensor.matmul(o, l, r, start=s0, stop=s1)
        self.cnt["pe"] += 1
        ins.then_inc(self.sem["pe"], 1)
        tok = ("e", "pe", self.cnt["pe"])
        self._mark(tok, reads, writes, "pe")
        return tok

    def dma(self, q, out, in_, reads=(), writes=(), ring=None, track=True):
        self._deps(q, reads, writes)
        if ring is None and q == "pool":
            i = self.prr
            self.prr = (self.prr + 1) % self.npsem
            sem = self.psem[i]
            if self.pval[i] > 0:
                self.wait(q, ("d", sem, self.pval[i]))
            self.pval[i] += 16
            val = self.pval[i]
        elif ring is None:
            i = self.drr
            self.drr = (self.drr + 1) % self.ndsem
            sem = self.dsem[i]
            if self.dval[i] > 0:
                self.wait(q, ("d", sem, self.dval[i]))
            self.dval[i] += 16
            val = self.dval[i]
        else:
            sem = self.rsem[ring]
            self.rval[ring] += 16
            val = self.rval[ring]
        self.eng[q].dma_start(out=out, in_=in_).then_inc(sem, 16)
        tok = ("d", sem, val)
        self._mark(tok, reads, writes, ("dma", id(sem), val))
        if track:
            self.live_dma.append(tok)
        return tok

    def sync_to_barrier(self, e):
        for t in getattr(self, "last_barrier", []):
            self.wait(e, t)

    def barrier(self, engines=("pe", "act", "dve", "sp")):
        toks = [("e", k, self.cnt[k]) for k in ("pe", "act", "dve", "pool") if self.cnt[k] > 0]
        toks += self.live_dma
        self.last_barrier = toks
        for e in engines:
            for t in toks:
                self.wait(e, t)
        self.live_dma = []
        self.slots = {k: v for k, v in self.slots.items() if isinstance(k, tuple) and k[0] == "ring"}


class WStream:
    def __init__(self, kb, ring, plan):
        self.kb = kb
        self.ring = ring
        self.plan = plan
        self.issued = 0
        self.taken = 0
        self.released = [True] * RING_SLOTS
        self.occ = [None] * RING_SLOTS

    def pump(self):
        while self.issued < len(self.plan):
            s = self.issued % RING_SLOTS
            if not self.released[s]:
                break
            kind, parts = self.plan[self.issued]
            for (oview, src) in parts:
                self.kb.dma("pool", oview(self.ring[s]), src, writes=[("ring", s)], ring=s, track=False)
            self.released[s] = False
            self.occ[s] = self.issued
            self.issued += 1

    def next(self, kind):
        self.pump()
        n = self.taken
        assert n < self.issued, f"weight stream stall at block {n} ({kind})"
        assert self.plan[n][0] == kind, (self.plan[n][0], kind)
        self.taken += 1
        s = n % RING_SLOTS
        return s, self.ring[s]

    def release(self, s):
        self.released[s] = True
        self.pump()


class _StopBuild(Exception):
    pass


def build_program(stop=99, swa_stop=99):
    nc = bass.Bass("TRN2", target_bir_lowering=False)
    _uid = [0]
    _orig_sbuf = nc.sbuf_tensor

    def sbuf_tensor(name, shape, dt):
        _uid[0] += 1
        return _orig_sbuf(f"sb{_uid[0]}_{name}", shape, dt)

    def din(name, shape):
        return nc.dram_tensor(name, list(shape), F32, kind="ExternalInput").ap()

    def dout(name, shape):
        return nc.dram_tensor(name, list(shape), F32, kind="ExternalOutput").ap()

    xT_d = din("xT", [128, NCH, T])
    cos_d = din("cosT", [128, T])
    sin_d = din("sinT", [128, T])
    kbias_d = din("kbias", [128, 2, 12])
    rcfix_d = din("rcfix", [128, 4, 16])
    vecs_d = din("vecs", [128, 296])
    cmat_d = din("cmat", [128, 4, 128])
    maskb_d = din("maskb", [128, 256])
    poolst_d = din("poolstT", [2, 128, NCH, NS, 15])
    kcT_d = din("kcT", [2, 128, NS, 2, 127])
    vc_d = din("vc", [2, 127, NS, 256])
    ck_d = din("ck", [2, NS, 128, 256])
    cv_d = din("cv", [2, NS, 128, 256])
    sp_d = din("spool", [2, NS, 15, D])
    pool_w = din("pool_w", [2, 4, 512, 512])
    w_qkv = din("w_qkv", [2, D, 2560])
    w_o = din("w_o", [2, D, D])
    w_gate = din("w_gate", [4, D, DFF])
    w_up = din("w_up", [4, D, DFF])
    w_down = din("w_down", [4, DFF, D])

    yT_o = dout("yT", [128, NCH, NOWN + NS])
    po_o = dout("poolT", [2, 128, NCH, NPO])
    kT_o = dout("knewT", [2, 128, 2, 128 + NS])
    vp_o = dout("vnew_p", [2, 128, 256])
    vs_o = dout("vnew_s", [2, 1, NS, 256])
    ks_o = dout("kold_s", [2, NS, 127, 256])
    vso_o = dout("vold_s", [2, NS, 127, 256])
    pso_o = dout("pold_s", [2, NS, 14, D])

    def v3(a, b):
        return lambda r: r[:, 0:a * b].rearrange("p (a b) -> p a b", a=a)

    plan = []
    for i in range(4):
        j = i // 2
        if i % 2 == 0:
            for g in range(4):
                src = pool_w[j, g].rearrange("(kc p) m -> p kc m", p=128)
                for hf in range(2):
                    plan.append(("pool", [(v3(4, 256), src[:, :, hf * 256:(hf + 1) * 256])]))
        else:
            wq = w_qkv[j].rearrange("(kc p) m -> p kc m", p=128)
            wo = w_o[j].rearrange("(kc p) m -> p kc m", p=128)
            plan.append(("wk", [(v3(16, 256), wq[:, :, 2048:2304])]))
            plan.append(("wv", [(v3(16, 256), wq[:, :, 2304:2560])]))
            ntq = len(q_tiles(OUT_L[i]))
            wo5 = w_o[j].rearrange("(gp u i d) m -> d gp u i m", gp=2, u=2, i=8)
            for tt in range(ntq):
                for gp in range(2):
                    for ip in range(4):
                        parts = []
                        for u in range(2):
                            for hl in range(2):
                                def ov(r, u=u, hl=hl):
                                    return r[:, 0:4096].rearrange("p (kc hl u d) -> p kc hl u d", kc=16, hl=2, u=2)[:, :, hl, u, :]
                                c0 = (16 * gp + 8 * u + 2 * ip + hl) * 64
                                parts.append((ov, wq[:, :, c0:c0 + 64]))
                        plan.append(("wq", parts))
                for mq in range(4):
                    for gp in range(2):
                        parts = []
                        for u in range(2):
                            def ov(r, u=u):
                                return r[u * 64:(u + 1) * 64, 0:4096].rearrange("p (i m) -> p i m", i=8)
                            parts.append((ov, wo5[:, gp, u, :, mq * 512:(mq + 1) * 512]))
                        plan.append(("wo", parts))
        wg = w_gate[i].rearrange("(kc p) m -> p kc m", p=128)
        wu = w_up[i].rearrange("(kc p) m -> p kc m", p=128)
        wd = w_down[i].rearrange("(fc p) m -> p fc m", p=128)
        for qd in range(4):
            f = 0
            while f < 11:
                w = 2 if f + 2 <= 11 else 1
                c0 = (qd * 11 + f) * 128
                plan.append(("wg", [(v3(16, w * 128), wg[:, :, c0:c0 + w * 128])]))
                plan.append(("wu", [(v3(16, w * 128), wu[:, :, c0:c0 + w * 128])]))
                f += w
            for mb in range(8):
                plan.append(("wd", [(v3(11, 256), wd[:, qd * 11:(qd + 1) * 11, mb * 256:(mb + 1) * 256])]))

    with ExitStack() as st:
        E = st.enter_context
        kb = KB(nc, st)
        X = E(sbuf_tensor("X", [128, NCH, T], F32))
        H = E(sbuf_tensor("H", [128, NCH, T + 2], BF16))
        ring = [E(sbuf_tensor(f"ring{s}", [128, RING_EL], BF16)) for s in range(RING_SLOTS)]
        vecs = E(sbuf_tensor("vecs", [128, 296], F32))
        esink = E(sbuf_tensor("esink", [128, 64], F32))
        cmat = E(sbuf_tensor("cmat", [128, 128], F32))
        identb = E(sbuf_tensor("identb", [128, 128], BF16))
        blk64 = E(sbuf_tensor("blk64", [128, 128], BF16))
        onesD = E(sbuf_tensor("onesD", [128, 128], BF16))
        ones1 = E(sbuf_tensor("ones1", [128, 128], BF16))
        maskb = E(sbuf_tensor("maskb", [128, 2, 256], BF16))
        epst = E(sbuf_tensor("epst", [128, 1], F32))
        rstd = E(sbuf_tensor("rstd", [128, 512], F32))
        sq = [E(sbuf_tensor(f"sq{i}", [128, 512], BF16)) for i in range(3)]
        ps = [E(nc.psum_tensor(f"ps{i}", [128, 512], F32)) for i in range(8)]
        psn = [0]

        kb.rot = list(range(8))

        def bank():
            i = kb.rot[psn[0] % len(kb.rot)]
            psn[0] += 1
            return ps[i], ("ps", i)

        RT = cmat[:]
        gmix = lambda i, c: vecs[:, i * 16 + c: i * 16 + c + 1]
        gffn = lambda i, c: vecs[:, 64 + i * 16 + c: 64 + i * 16 + c + 1]
        pscl = lambda j, c: vecs[:, 128 + j * 16 + c: 128 + j * 16 + c + 1]
        qg = lambda j: vecs[:, 160 + j: 161 + j]
        kg = lambda j: vecs[:, 162 + j: 163 + j]

        ws = WStream(kb, ring, plan)

        kb.dma("sp", vecs[:], vecs_d[:, :], writes=["vecs"])
        kb.dma("sp", cmat[:], cmat_d[:, 1, :], writes=["cmat"])
        kb.dma("pool", identb[:], cmat_d[:, 0, :], writes=["identb"])
        kb.dma("pool", blk64[:], cmat_d[:, 2, :], writes=["blk64"])
        kb.dma("pool", maskb[:, 0, :], maskb_d[:, :], writes=["maskb"])
        kb.dma("pool", maskb[:, 1, :], maskb_d[:, :], writes=["maskb"])
        for c in range(NCH):
            kb.dma("sp", X[:, c, :], xT_d[:, c, :], writes=[("X", c)])
        ws.pump()
        kb.op("dve", lambda e: e.memset(epst[:], EPS), writes=["eps"])
        kb.op("dve", lambda e: e.memset(H[:, :, T:T + 2], 0.0), writes=[("H", c) for c in range(NCH)])
        kb.op("dve", lambda e: e.memset(onesD[:], 1.0 / D), writes=["onesD"])
        kb.op("dve", lambda e: e.memset(ones1[:], 1.0), writes=["ones1"])
        kb.op("act", lambda e: e.activation(out=esink[:], in_=vecs[:, 164:228], func=AF.Exp), reads=["vecs"], writes=["esink"])
        for j in range(2):
            for s in range(NS):
                kb.dma("sp", ks_o[j, s], ck_d[j, s, 1:128, :])
                kb.dma("sp", vso_o[j, s], cv_d[j, s, 1:128, :])
                kb.dma("sp", pso_o[j, s], sp_d[j, s, 1:15, :])
        kb.barrier()

        def rms_stats(cols, dst, dst_off, key="rstdbuf"):
            a, b = cols
            for (t0, t1) in split_tiles(a, b):
                w = t1 - t0
                pb, pk = bank()
                for c in range(NCH):
                    sb = sq[c % 3]
                    kb.op("act", lambda e, sb=sb, c=c: e.activation(out=sb[:, 0:w], in_=X[:, c, t0:t1], func=AF.Square),
                          reads=[("X", c)], writes=[("sq", c % 3)])
                    kb.mm([(pb[:, 0:w], onesD[:], sb[:, 0:w], c == 0, c == NCH - 1)], reads=[("sq", c % 3), "onesD"], writes=[pk])
                o0 = dst_off + t0 - a
                kb.op("act", lambda e: e.activation(out=dst[:, o0:o0 + w], in_=pb[:, 0:w], func=AF.Ln, bias=epst[:, 0:1], scale=1.0),
                      reads=[pk, "eps"], writes=[key])
                kb.op("act", lambda e: e.activation(out=dst[:, o0:o0 + w], in_=dst[:, o0:o0 + w], func=AF.Exp, scale=-0.5), reads=[key], writes=[key])

        def rmsnorm_to_H(cols, gfn, tilekeys=False):
            a, b = cols
            for (t0, t1) in split_tiles(a, b):
                w = t1 - t0
                rms_stats((t0, t1), rstd, 0)
                for c in range(NCH):
                    kb.op("dve", lambda e, c=c: e.scalar_tensor_tensor(out=H[:, c, t0:t1], in0=X[:, c, t0:t1], scalar=gfn(c),
                                                                        in1=rstd[:, 0:w], op0=ALU.mult, op1=ALU.mult),
                          reads=[("X", c), "rstdbuf", "vecs"], writes=[("H", c, t0) if tilekeys else ("H", c)])

        def ffn(i):
            a = OUT_L[i]
            tiles = split_tiles(a, T)
            rmsnorm_to_H((a, T), lambda c: gffn(i, c), tilekeys=True)
            with ExitStack() as st2:
                act = st2.enter_context(sbuf_tensor("act", [128, 11, T - 15], BF16))
                sg = [st2.enter_context(sbuf_tensor(f"sg{k}", [128, 512], BF16)) for k in range(2)]
                sgn = 0
                for qd in range(4):
                    f = 0
                    while f < 11:
                        wdt = 2 if f + 2 <= 11 else 1
                        sgk, Wg = ws.next("wg")
                        suk, Wu = ws.next("wu")
                        Wg3 = Wg[:, 0:16 * wdt * 128].rearrange("p (a b) -> p a b", a=16)
                        Wu3 = Wu[:, 0:16 * wdt * 128].rearrange("p (a b) -> p a b", a=16)
                        for fl in range(wdt):
                            for (t0, t1) in tiles:
                                w = t1 - t0
                                gb, gk = bank()
                                ub, uk = bank()
                                hreads = [("H", c, t0) for c in range(NCH)]
                                kb.mm([(gb[:, 0:w], Wg3[:, kc, fl * 128:(fl + 1) * 128], H[:, kc, t0:t1], kc == 0, kc == 15) for kc in range(16)],
                                      reads=[("ring", sgk)] + hreads, writes=[gk])
                                kb.mm([(ub[:, 0:w], Wu3[:, kc, fl * 128:(fl + 1) * 128], H[:, kc, t0:t1], kc == 0, kc == 15) for kc in range(16)],
                                      reads=[("ring", suk)] + hreads, writes=[uk])
                                sgt = sg[sgn % 2]
                                sgkey = ("sg", sgn % 2)
                                sgn += 1
                                kb.op("act", lambda e, sgt=sgt, gb=gb: e.activation(out=sgt[:, 0:w], in_=gb[:, 0:w], func=AF.Silu),
                                      reads=[gk], writes=[sgkey])
                                kb.op("dve", lambda e, sgt=sgt, ub=ub, ff=f + fl: e.tensor_tensor(out=act[:, ff, t0 - 15:t1 - 15], in0=ub[:, 0:w],
                                                                                               in1=sgt[:, 0:w], op=ALU.mult),
                                      reads=[uk, sgkey], writes=[("act", f + fl, t0)])
                        ws.release(sgk)
                        ws.release(suk)
                        f += wdt
                    for mb in range(8):
                        sdk, Wd = ws.next("wd")
                        Wd3 = Wd[:, 0:11 * 256].rearrange("p (a b) -> p a b", a=11)
                        for ml in range(2):
                            m = mb * 2 + ml
                            for (t0, t1) in tiles:
                                w = t1 - t0
                                yb, yk = bank()
                                kb.mm([(yb[:, 0:w], Wd3[:, ff, ml * 128:(ml + 1) * 128], act[:, ff, t0 - 15:t1 - 15], ff == 0, ff == 10) for ff in range(11)],
                                      reads=[("ring", sdk)] + [("act", ff, t0) for ff in range(11)], writes=[yk])
                                kb.op("dve", lambda e, yb=yb, m=m: e.tensor_tensor(out=X[:, m, t0:t1], in0=yb[:, 0:w], in1=X[:, m, t0:t1], op=ALU.add),
                                      reads=[yk, ("X", m, t0)], writes=[("X", m, t0)])
                        ws.release(sdk)
                kb.barrier()

        def pool_layer(i):
            j = i // 2
            a_in, a_out = IN_L[i], OUT_L[i]
            n_p = TP - a_in
            EW = 16 + n_p + 16 * NS
            sb0 = 16 + n_p
            with ExitStack() as st2:
                A = st2.enter_context
                rs_all = A(sbuf_tensor("rs_all", [128, T], F32))
                sets = []
                for k_ in range(2):
                    sets.append(tuple(A(sbuf_tensor(f"{nm}{k_}", [128, 16 + TP + 16 * NS], F32)) for nm in ("hf", "bA", "bB")))
                pst = A(sbuf_tensor("pst", [128, NCH, NS, 15], F32))
                PO = A(sbuf_tensor("PO", [128, NCH, NPO], F32))
                rcf = A(sbuf_tensor("rcf", [128, 4, 16], F32))
                fxs = [A(sbuf_tensor(f"fx{k_}", [128, 16], F32)) for k_ in range(2)]
                kb.dma("sp", pst[:], poolst_d[j], writes=["pst"])
                kb.dma("sp", rcf[:], rcfix_d[:, :, :], writes=["rcf"])
                for k_, eng in enumerate(("dve", "dve")):
                    for buf, nm in zip(sets[k_], ("hf", "bA", "bB")):
                        kb.op(eng, lambda e, buf=buf: e.memset(buf[:, 0:16], 0.0), writes=[nm + str(k_)])
                rms_stats((a_in, T), rs_all, a_in, "rsall")
                for c in range(NCH):
                    g = c // 4
                    wwin = 2 << g
                    k_ = c % 2
                    eng = "dve"
                    hf, bA, bB = sets[k_]
                    fx = fxs[k_]
                    hk, fk = "hf" + str(k_), "fx" + str(k_)
                    hfs = hf[:, sb0:sb0 + 16 * NS].rearrange("p (s k) -> p s k", s=NS)
                    if eng == "dve":
                        kb.op(eng, lambda e, c=c, hf=hf: e.scalar_tensor_tensor(out=hf[:, 16:16 + n_p], in0=X[:, c, a_in:TP], scalar=gmix(i, c),
                                                                               in1=rs_all[:, a_in:TP], op0=ALU.mult, op1=ALU.mult),
                              reads=[("X", c), "vecs", "rsall"], writes=[hk])
                        kb.op(eng, lambda e, c=c, hfs=hfs: e.scalar_tensor_tensor(out=hfs[:, :, 15], in0=X[:, c, TP:T], scalar=gmix(i, c),
                                                                                 in1=rs_all[:, TP:T], op0=ALU.mult, op1=ALU.mult),
                              reads=[("X", c), "rsall"], writes=[hk])
                    else:
                        kb.op("act", lambda e, c=c, hf=hf: e.activation(out=hf[:, 16:16 + n_p], in_=X[:, c, a_in:TP], func=AF.Copy, scale=gmix(i, c)),
                              reads=[("X", c), "vecs"], writes=[hk])
                        kb.op("act", lambda e, c=c, hfs=hfs: e.activation(out=hfs[:, :, 15], in_=X[:, c, TP:T], func=AF.Copy, scale=gmix(i, c)),
                              reads=[("X", c), "vecs"], writes=[hk])
                        kb.op(eng, lambda e, hf=hf: e.tensor_tensor(out=hf[:, 16:16 + n_p], in0=hf[:, 16:16 + n_p], in1=rs_all[:, a_in:TP], op=ALU.mult),
                              reads=[hk, "rsall"], writes=[hk])
                        kb.op(eng, lambda e, hfs=hfs: e.tensor_tensor(out=hfs[:, :, 15], in0=hfs[:, :, 15], in1=rs_all[:, TP:T], op=ALU.mult),
                              reads=[hk, "rsall"], writes=[hk])
                    kb.op("act", lambda e, c=c, hfs=hfs: e.copy(out=hfs[:, :, 0:15], in_=pst[:, c, :, :]), reads=["pst"], writes=[hk])
                    kb.op("act", lambda e, c=c, hf=hf: e.copy(out=PO[:, c, 0:15], in_=hf[:, 16 + n_p - 15:16 + n_p]), reads=[hk], writes=["PO"])
                    kb.op("act", lambda e, c=c, hfs=hfs: e.copy(out=PO[:, c, 15:NPO], in_=hfs[:, :, 15]), reads=[hk], writes=["PO"])
                    src, sname = hf, hk
                    bufs = [(bA, "bA" + str(k_)), (bB, "bB" + str(k_))]
                    sh = 1
                    kk = 0
                    while sh < wwin:
                        dst, dname = bufs[kk % 2]
                        kb.op(eng, lambda e, src=src, dst=dst, sh=sh: e.tensor_tensor(out=dst[:, 16:EW], in0=src[:, 16:EW], in1=src[:, 16 - sh:EW - sh], op=ALU.add),
                              reads=[sname], writes=[dname])
                        src, sname = dst, dname
                        sh *= 2
                        kk += 1
                    srs = src[:, sb0:sb0 + 16 * NS].rearrange("p (s k) -> p s k", s=NS)
                    if eng == "dve":
                        kb.op(eng, lambda e, src=src, c=c, hf=hf: e.scalar_tensor_tensor(out=H[:, c, a_in:TP], in0=src[:, 16:16 + n_p], scalar=1.0 / wwin,
                                                                                        in1=hf[:, 16:16 + n_p], op0=ALU.mult, op1=ALU.subtract),
                              reads=[sname, hk], writes=[("H", c)])
                        kb.op(eng, lambda e, srs=srs, c=c, hfs=hfs: e.scalar_tensor_tensor(out=H[:, c, TP:T], in0=srs[:, :, 15], scalar=1.0 / wwin,
                                                                                          in1=hfs[:, :, 15], op0=ALU.mult, op1=ALU.subtract),
                              reads=[sname, hk], writes=[("H", c)])
                    else:
                        mbuf, mname = bufs[kk % 2]
                        kb.op("act", lambda e, src=src, mbuf=mbuf: e.activation(out=mbuf[:, 16:EW], in_=src[:, 16:EW], func=AF.Copy, scale=1.0 / wwin),
                              reads=[sname], writes=[mname])
                        mbs = mbuf[:, sb0:sb0 + 16 * NS].rearrange("p (s k) -> p s k", s=NS)
                        kb.op(eng, lambda e, mbuf=mbuf, c=c, hf=hf: e.tensor_tensor(out=H[:, c, a_in:TP], in0=mbuf[:, 16:16 + n_p], in1=hf[:, 16:16 + n_p], op=ALU.subtract),
                              reads=[mname, hk], writes=[("H", c)])
                        kb.op(eng, lambda e, mbs=mbs, c=c, hfs=hfs: e.tensor_tensor(out=H[:, c, TP:T], in0=mbs[:, :, 15], in1=hfs[:, :, 15], op=ALU.subtract),
                              reads=[mname, hk], writes=[("H", c)])
                    f0 = 16 + FIX0 - a_in
                    kb.op(eng, lambda e, src=src, g=g, fx=fx: e.tensor_tensor(out=fx[:], in0=src[:, f0:f0 + 16], in1=rcf[:, g, :], op=ALU.mult),
                          reads=[sname, "rcf"], writes=[fk])
                    kb.op(eng, lambda e, c=c, fx=fx, hf=hf: e.tensor_tensor(out=H[:, c, FIX0:FIX0 + 16], in0=fx[:], in1=hf[:, f0:f0 + 16], op=ALU.subtract),
                          reads=[fk, hk], writes=[("H", c)])
                kb.dma("sp", po_o[j], PO[:], reads=["PO"])
                tiles = split_tiles(a_out, T)
                for g in range(4):
                    for hfb in range(2):
                        sk, Wp = ws.next("pool")
                        Wp3 = Wp[:, 0:1024].rearrange("p (a b) -> p a b", a=4)
                        for ml in range(2):
                            m = g * 4 + hfb * 2 + ml
                            for (t0, t1) in tiles:
                                w = t1 - t0
                                yb, yk = bank()
                                kb.mm([(yb[:, 0:w], Wp3[:, kc, ml * 128:(ml + 1) * 128], H[:, g * 4 + kc, t0:t1], kc == 0, kc == 3) for kc in range(4)],
                                      reads=[("ring", sk)] + [("H", g * 4 + kc) for kc in range(4)], writes=[yk])
                                kb.op("dve", lambda e, yb=yb, m=m: e.scalar_tensor_tensor(out=X[:, m, t0:t1], in0=yb[:, 0:w], scalar=pscl(j, m),
                                                                                          in1=X[:, m, t0:t1], op0=ALU.mult, op1=ALU.add),
                                      reads=[yk, ("X", m), "vecs"], writes=[("X", m)])
                        ws.release(sk)
                kb.barrier()

        QW = QWMAX

        def swa_layer(i):
            j = i // 2
            ks, qs = IN_L[i], OUT_L[i]
            assert qs == ks + 128
            nkb = -(-(TP - ks) // 128)
            rmsnorm_to_H((ks, T), lambda c: gmix(i, c))
            kb.rot = [0, 1, 2, 3]
            with ExitStack() as st2:
                A = st2.enter_context
                KT = A(sbuf_tensor("KT", [128, 2, T], BF16))
                KTf = A(sbuf_tensor("KTf", [128, 2, 128 + NS], F32))
                V = A(sbuf_tensor("V", [128, nkb, 256], BF16))
                Vs = A(sbuf_tensor("Vs", [128, NS, 256], BF16))
                KTs = A(sbuf_tensor("KTs", [128, NS, 2, 128], BF16))
                VP = A(sbuf_tensor("VP", [128, 256], F32))
                QO = A(sbuf_tensor("QO", [128, NCH, QW], BF16))
                cst = A(sbuf_tensor("cst", [128, 2, QW], F32))
                xn = A(sbuf_tensor("xn", [128, QW], F32))
                rq = A(sbuf_tensor("rq", [128, QW], F32))
                kbs = A(sbuf_tensor("kbs", [128, 12], F32))
                rden = rstd
                kb.dma("sp", kbs[:], kbias_d[:, j, :], writes=["kbs"])
                kb.sync_to_barrier("pool")
                kb.dma("pool", KTs[:, :, :, 1:128], kcT_d[j], writes=["KTs_c"])
                kb.dma("pool", Vs[1:128], vc_d[j], writes=["Vs_c"])
                hreads = [("H", c) for c in range(NCH)]

                def load_tables(t0, t1):
                    w = t1 - t0
                    kb.dma("sp", cst[:, 0, 0:w], cos_d[:, t0:t1], writes=["cst"])
                    kb.dma("sp", cst[:, 1, 0:w], sin_d[:, t0:t1], writes=["cst"])

                xn2 = A(sbuf_tensor("xn2", [128, QW], F32))
                rq2 = A(sbuf_tensor("rq2", [128, QW], F32))
                sqb2 = A(sbuf_tensor("sqb2", [128, QW], BF16))
                qkn = [0]
                ptb = [sq[1], sq[2], A(sbuf_tensor("ptb2", [128, 512], BF16)), A(sbuf_tensor("ptb3", [128, 512], BF16))]

                def qk_post_gen(pb, pk, w, gvec, outs):
                    par = qkn[0] % 2
                    qkn[0] += 1
                    sb, sbk = (sq[0], ("sq", 0)) if par == 0 else (sqb2, "sqb2")
                    xn_, xk = (xn, "xn") if par == 0 else (xn2, "xn2")
                    rq_, rk_ = (rq, "rq") if par == 0 else (rq2, "rq2")
                    kb.op("act", lambda e: e.copy(out=xn_[:, 0:w], in_=pb[:, 0:w]), reads=[pk], writes=[xk])
                    kb.op("act", lambda e: e.activation(out=sb[:, 0:w], in_=xn_[:, 0:w], func=AF.Square), reads=[xk], writes=[sbk])
                    yield
                    mb_, mk = bank()
                    kb.mm([(mb_[:, 0:w], blk64[:], sb[:, 0:w], True, True)], reads=[sbk, "blk64"], writes=[mk])
                    kb.op("act", lambda e: e.activation(out=rq_[:, 0:w], in_=mb_[:, 0:w], func=AF.Ln, bias=epst[:, 0:1], scale=1.0),
                          reads=[mk, "eps"], writes=[rk_])
                    kb.op("act", lambda e: e.activation(out=rq_[:, 0:w], in_=rq_[:, 0:w], func=AF.Exp, scale=-0.5), reads=[rk_], writes=[rk_])
                    kb.op("dve", lambda e: e.scalar_tensor_tensor(out=xn_[:, 0:w], in0=xn_[:, 0:w], scalar=gvec, in1=rq_[:, 0:w], op0=ALU.mult, op1=ALU.mult),
                          reads=[xk, rk_, "vecs"], writes=[xk])
                    yield
                    rb, rk = bank()
                    kb.mm([(rb[:, 0:w], RT, xn_[:, 0:w], True, True)], reads=[xk, "cmat"], writes=[rk])
                    kb.op("dve", lambda e: e.tensor_tensor(out=rq_[:, 0:w], in0=rb[:, 0:w], in1=cst[:, 1, 0:w], op=ALU.mult),
                          reads=[rk, "cst", rk_], writes=[rk_])
                    kb.op("dve", lambda e: e.tensor_tensor(out=xn_[:, 0:w], in0=xn_[:, 0:w], in1=cst[:, 0, 0:w], op=ALU.mult),
                          reads=[xk, "cst"], writes=[xk])
                    for (oap, okey, lo, hi) in outs:
                        kb.op("dve", lambda e, oap=oap, lo=lo, hi=hi: e.tensor_tensor(out=oap, in0=rq_[:, lo:hi], in1=xn_[:, lo:hi], op=ALU.add),
                              reads=[rk_, xk], writes=[okey])

                def run_interleaved(gens):
                    gens = list(gens)
                    while gens:
                        nxt = []
                        for g_ in gens:
                            try:
                                next(g_)
                                nxt.append(g_)
                            except StopIteration:
                                pass
                        gens = nxt

                def qk_post(pb, pk, w, gvec, outs):
                    run_interleaved([qk_post_gen(pb, pk, w, gvec, outs)])

                sk, Wk = ws.next("wk")
                Wk3 = Wk[:, 0:4096].rearrange("p (a b) -> p a b", a=16)
                for (t0, t1) in split_tiles(ks, T, QW):
                    w = t1 - t0
                    load_tables(t0, t1)
                    gens = []
                    for gp in range(2):
                        pb, pk = bank()
                        kb.mm([(pb[:, 0:w], Wk3[:, kc, gp * 128:(gp + 1) * 128], H[:, kc, t0:t1], kc == 0, kc == 15) for kc in range(16)],
                              reads=[("ring", sk)] + hreads, writes=[pk])
                        outs = [(KT[:, gp, t0:t1], "KT", 0, w)]
                        lo = max(t0, LASTK0)
                        if lo < t1:
                            outs.append((KTf[:, gp, lo - LASTK0:t1 - LASTK0], "KTf", lo - t0, w))
                        gens.append(qk_post_gen(pb, pk, w, kg(j), outs))
                    run_interleaved(gens)
                ws.release(sk)
                kb.dma("sp", kT_o[j], KTf[:], reads=["KTf"])
                if swa_stop <= 1:
                    return "stop"
                for s in range(NS):
                    kb.op("dve", lambda e, s=s: e.tensor_copy(out=KTs[:, s, :, 0], in_=KT[:, :, TP + s]), reads=["KT"], writes=["KTs_n"])
                if swa_stop <= 1.2:
                    return "stop"
                sv, Wv = ws.next("wv")
                Wv3 = Wv[:, 0:4096].rearrange("p (a b) -> p a b", a=16)
                for kbi in range(nkb):
                    c0 = ks + 128 * kbi
                    kn = min(128, TP - c0)
                    vb, vk = bank()
                    kb.mm([(vb[0:kn, 0:256], H[:, kc, c0:c0 + kn], Wv3[:, kc, :], kc == 0, kc == 15) for kc in range(16)],
                          reads=[("ring", sv)] + hreads, writes=[vk])
                    kb.op("act", lambda e, vb=vb, kbi=kbi, kn=kn: e.copy(out=V[0:kn, kbi, :], in_=vb[0:kn, 0:256]), reads=[vk], writes=[("V", kbi)])
                if swa_stop <= 1.4:
                    return "stop"
                vb, vk = bank()
                kb.mm([(vb[:, 0:256], H[:, kc, LASTK0:TP], Wv3[:, kc, :], kc == 0, kc == 15) for kc in range(16)],
                      reads=[("ring", sv)] + hreads, writes=[vk])
                kb.op("act", lambda e, vb=vb: e.copy(out=VP[:], in_=vb[:, 0:256]), reads=[vk], writes=["VP"])
                kb.dma("sp", vp_o[j], VP[:], reads=["VP"])
                if swa_stop <= 1.6:
                    return "stop"
                for sp_ in range(2):
                    vb, vk = bank()
                    mms = []
                    for sl in range(2):
                        s = sp_ * 2 + sl
                        mms += [(vb[0:2, sl * 256:(sl + 1) * 256], H[:, kc, TP + s:TP + s + 2], Wv3[:, kc, :], kc == 0, kc == 15) for kc in range(16)]
                    kb.mm(mms, reads=[("ring", sv)] + hreads, writes=[vk])
                    kb.op("act", lambda e, vb=vb: e.copy(out=rden[0:1, 0:512], in_=vb[0:1, 0:512]), reads=[vk], writes=["rstdbuf"])
                    kb.op("dve", lambda e, vb=vb, sp_=sp_: e.tensor_copy(out=Vs[0:1, sp_ * 2:sp_ * 2 + 2, :],
                                                                         in_=vb[0:1, 0:512].rearrange("p (s d) -> p s d", s=2)),
                          reads=[vk], writes=["Vs_n"])
                    kb.dma("sp", vs_o[j, :, sp_ * 2:sp_ * 2 + 2, :], rden[0:1, 0:512].rearrange("p (s d) -> p s d", s=2), reads=["rstdbuf"])
                ws.release(sv)
                if swa_stop <= 2:
                    return "stop"

                qtl = q_tiles(qs)
                qtiles = [(a0_, a1_, n_ == len(qtl) - 1) for n_, (a0_, a1_) in enumerate(qtl)]
                ptn = [0]
                hn = [0]
                for (t0, t1p, has_s) in qtiles:
                    t1 = T if has_s else t1p
                    w = t1 - t0
                    wp = t1p - t0
                    assert w <= QW and wp + NS * (NS + 1) <= 512
                    load_tables(t0, t1)
                    kb0 = (t0 - ks) // 128
                    kbl = (t1p - 1 - ks) // 128
                    blocks = [(gp_, ip_) for gp_ in range(2) for ip_ in range(4)]

                    def start_block(bi):
                        gp_, ip_ = blocks[bi]
                        sk, Wq = ws.next("wq")
                        Wq3 = Wq[:, 0:4096].rearrange("p (a b) -> p a b", a=16)
                        gens = []
                        for cl in range(2):
                            c = gp_ * 8 + ip_ * 2 + cl
                            pb, pk = bank()
                            kb.mm([(pb[:, 0:w], Wq3[:, kc, cl * 128:(cl + 1) * 128], H[:, kc, t0:t1], kc == 0, kc == 15) for kc in range(16)],
                                  reads=[("ring", sk)] + hreads, writes=[pk])
                            g_ = qk_post_gen(pb, pk, w, qg(j), [(QO[:, c, 0:w], ("QO", c), 0, w)])
                            next(g_)
                            gens.append(g_)
                        ws.release(sk)
                        return gens

                    def advance(gens):
                        for g_ in gens:
                            next(g_, None)

                    cur_g = start_block(0)
                    advance(cur_g)
                    advance(cur_g)
                    for bi in range(8):
                        gp, ip = blocks[bi]
                        nxt_g = start_block(bi + 1) if bi + 1 < 8 else None
                        if True:
                            steps = []
                            for cl in range(2):
                                c = gp * 8 + ip * 2 + cl
                                ii = ip * 2 + cl
                                hds = []
                                for hh in range(2):
                                    par = hn[0] % 2
                                    hn[0] += 1
                                    hds.append(dict(c=c, hh=hh, hq=8 * (2 * gp + hh) + ii, r0=hh * 64, r1=hh * 64 + 64,
                                                    ob=ps[4 + par * 2], ok=("ps", 4 + par * 2), db=ps[5 + par * 2], dk=("ps", 5 + par * 2)))
                                csteps = [("p", hds, kbi) for kbi in range(kb0 - 1, kbl + 1)]
                                if has_s:
                                    csteps += [("s", hds, s_) for s_ in range(NS)]
                                csteps[-1] = csteps[-1] + (True,)
                                steps += csteps

                            def s_phase(st_):
                                kind, hds, idx = st_[0], st_[1], st_[2]
                                c = hds[0]["c"]
                                sb_, sk_ = bank()
                                pi = ptn[0] % 4
                                ptn[0] += 1
                                pt, pkey = ptb[pi], (("sq", pi + 1) if pi < 2 else ("ptb", pi))
                                if kind == "p":
                                    kc0 = ks + 128 * idx
                                    kn = min(128, TP - kc0)
                                    q0 = max(t0, kc0)
                                    q1 = min(t1p, kc0 + 256)
                                    nq = q1 - q0
                                    off = q0 - kc0
                                    sc = lambda hh: (sb_[0:kn, hh * 256:hh * 256 + nq], KT[hh * 64:hh * 64 + 64, gp, kc0:kc0 + kn],
                                                     QO[hh * 64:hh * 64 + 64, c, q0 - t0:q1 - t0], True, True)
                                    kb.mm([sc(0), (sb_[0:kn, 256:260], ones1[:, 0:kn], ones1[:, 0:4], True, True), sc(1)],
                                          reads=["KT", ("QO", c), "ones1"], writes=[sk_])
                                    sv_ = sb_[0:kn, 0:512].rearrange("p (h q) -> p h q", h=2)[:, :, 0:nq]
                                    pv_ = pt[0:kn, 0:512].rearrange("p (h q) -> p h q", h=2)[:, :, 0:nq]
                                    kb.op("act", lambda e: e.activation(out=pv_, in_=sv_, func=AF.Exp, bias=kbs[0:kn, idx:idx + 1], scale=0.125),
                                          reads=[sk_, "kbs"], writes=[pkey])
                                    kb.op("dve", lambda e: e.tensor_tensor(out=pv_, in0=pv_, in1=maskb[0:kn, :, off:off + nq], op=ALU.mult),
                                          reads=[pkey, "maskb"], writes=[pkey])
                                    return (pt, pkey, kn, q0, q1)
                                ss = lambda hh: (sb_[:, hh * NS:(hh + 1) * NS], KTs[hh * 64:hh * 64 + 64, idx, gp, :],
                                                 QO[hh * 64:hh * 64 + 64, c, wp:wp + NS], True, True)
                                kb.mm([ss(0), (sb_[:, NS:2 * NS], ones1[:, :], ones1[:, 0:NS], True, True), ss(1)],
                                      reads=["KTs_c", "KTs_n", ("QO", c), "ones1"], writes=[sk_])
                                kb.op("act", lambda e: e.activation(out=pt[:, 0:2 * NS], in_=sb_[:, 0:2 * NS], func=AF.Exp, scale=0.125),
                                      reads=[sk_], writes=[pkey])
                                return (pt, pkey)

                            def p_phase(st_, sres):
                                kind, hds, idx = st_[0], st_[1], st_[2]
                                for hd in hds:
                                    hh = hd["hh"]
                                    ob, ok_, db, dk = hd["ob"], hd["ok"], hd["db"], hd["dk"]
                                    if kind == "p":
                                        pt, pkey, kn, q0, q1 = sres
                                        first = not hd.get("opened", False)
                                        hd["opened"] = True
                                        rhs = pt[0:kn, hh * 256:hh * 256 + q1 - q0]
                                        kb.mm([(ob[:, q0 - t0:q1 - t0], V[0:kn, idx, gp * 128:(gp + 1) * 128], rhs, first, True, not first)],
                                              reads=[pkey, ("V", idx)], writes=[ok_])
                                        kb.mm([(db[:, q0 - t0:q1 - t0], ones1[0:kn, :], rhs, first, True, not first)], reads=[pkey, "ones1"], writes=[dk])
                                    else:
                                        pt, pkey = sres
                                        o0 = wp + NS * idx
                                        rhs = pt[:, hh * NS:(hh + 1) * NS]
                                        kb.mm([(ob[:, o0:o0 + NS], Vs[:, idx, gp * 128:(gp + 1) * 128], rhs, False, True, True)],
                                              reads=[pkey, "Vs_c", "Vs_n"], writes=[ok_])
                                        kb.mm([(db[:, o0:o0 + NS], ones1[:], rhs, False, True, True)], reads=[pkey, "ones1"], writes=[dk])
                                if len(st_) > 3:
                                    for hd in hds:
                                        finish(hd)

                            def finish(hd):
                                r0, r1, c, hq = hd["r0"], hd["r1"], hd["c"], hd["hq"]
                                ob, ok_, db, dk = hd["ob"], hd["ok"], hd["db"], hd["dk"]
                                rkey = ("rden", hd["hh"])
                                esk = esink[r0:r1, j * 32 + hq:j * 32 + hq + 1]
                                parts = []
                                if wp > 0:
                                    parts.append((db[r0:r1, 0:wp], ob[r0:r1, 0:wp], rden[r0:r1, 0:wp], QO[r0:r1, c, 0:wp]))
                                if has_s:
                                    dg = lambda t: t[r0:r1, wp:wp + NS * (NS + 1)].rearrange("p (a b) -> p a b", b=NS + 1)[:, :, 0]
                                    parts.append((dg(db), dg(ob), rden[r0:r1, wp:wp + NS], QO[r0:r1, c, wp:wp + NS]))
                                for (dsrc, osrc, rd, qo) in parts:
                                    kb.op("act", lambda e, dsrc=dsrc, rd=rd: e.activation(out=rd, in_=dsrc, func=AF.Ln, bias=esk, scale=1.0),
                                          reads=[dk, "esink"], writes=[rkey, "rstdbuf"])
                                    kb.op("act", lambda e, rd=rd: e.activation(out=rd, in_=rd, func=AF.Exp, scale=-1.0), reads=[rkey], writes=[rkey])
                                    kb.op("dve", lambda e, osrc=osrc, rd=rd, qo=qo: e.tensor_tensor(out=qo, in0=osrc, in1=rd, op=ALU.mult),
                                          reads=[ok_, rkey], writes=[("QO", c)])

                            LEAD = 3
                            pend = []
                            nst = len(steps)
                            for n_, st_ in enumerate(steps):
                                if nxt_g is not None and n_ in (nst // 3, (2 * nst) // 3):
                                    advance(nxt_g)
                                pend.append((st_, s_phase(st_)))
                                if len(pend) > LEAD:
                                    p_phase(*pend.pop(0))
                            while pend:
                                p_phase(*pend.pop(0))
                    if swa_stop <= 3:
                        return "stop"
                    for mq in range(4):
                        wos = []
                        for gp in range(2):
                            sk, Wo = ws.next("wo")
                            wos.append((sk, Wo[:, 0:4096].rearrange("p (i m) -> p i m", i=8)))
                        for ml in range(4):
                            m = mq * 4 + ml
                            yb, yk = bank()
                            kb.mm([(yb[:, 0:w], wos[kc // 8][1][:, kc % 8, ml * 128:(ml + 1) * 128], QO[:, kc, 0:w], kc == 0, kc == 15) for kc in range(16)],
                                  reads=[("ring", wos[0][0]), ("ring", wos[1][0])] + [("QO", c) for c in range(NCH)], writes=[yk])
                            kb.op("dve", lambda e, yb=yb, m=m: e.tensor_tensor(out=X[:, m, t0:t1], in0=yb[:, 0:w], in1=X[:, m, t0:t1], op=ALU.add),
                                  reads=[yk, ("X", m)], writes=[("X", m)])
                        for (sk, _) in wos:
                            ws.release(sk)
                    if swa_stop <= 4:
                        return "stop"
                kb.rot = list(range(8))
                kb.barrier()

        nph = 0
        for i in range(4 if stop > 0 else 0):
            if nph >= stop:
                break
            if i % 2 == 0:
                pool_layer(i)
            else:
                if swa_layer(i) == "stop":
                    kb.rot = list(range(8))
                    kb.barrier()
                    break
            nph += 1
            if nph >= stop:
                break
            ffn(i)
            nph += 1
        assert stop < 99 or ws.taken == len(plan), (ws.taken, len(plan))
        for c in range(NCH):
            kb.dma("sp", yT_o[:, c, :], X[:, c, HALO:T])
        kb.barrier(engines=("sp",))
    return nc


_CACHE = {}


def _const_inputs():
    ident = np.eye(128, dtype=np.float32)
    R = np.zeros((128, 128), np.float32)
    for hb in (0, 64):
        for m in range(8):
            R[hb + m, hb + m + 8] = -1.0
            R[hb + m + 8, hb + m] = 1.0
    blk = np.zeros((128, 128), np.float32)
    blk[:64, :64] = 1.0 / 64
    blk[64:, 64:] = 1.0 / 64
    cmat = np.stack([ident, R.T.copy(), blk, np.zeros((128, 128), np.float32)], axis=1)
    ii = np.arange(128)[:, None]
    cc = np.arange(256)[None, :]
    maskb = np.where((cc >= ii) & (cc < ii + 128), 1.0, 0.0).astype(np.float32)
    return np.ascontiguousarray(cmat), maskb


def kernel(**inputs):
    in_maps = make_in_maps(**inputs)
    if "nc" not in _CACHE:
        _CACHE["nc"] = build_program()
    res = run_bass_kernel_spmd(_CACHE["nc"], in_maps, core_ids=list(range(8)))
    return assemble(res.results)


def make_in_maps(x_prompt, x_sample, state_pool, cache_k, cache_v, meta_tokens, norm_mix, norm_ffn,
                 pool_w, pool_scale, w_qkv, w_o, q_norm, k_norm, sinks, w_gate, w_up, w_down):
    f32 = np.float32
    x_prompt = np.asarray(x_prompt, f32)
    x_sample = np.asarray(x_sample, f32)
    state_pool = np.asarray(state_pool, f32)
    cache_k = np.asarray(cache_k, f32)
    cache_v = np.asarray(cache_v, f32)
    B = x_prompt.shape[0]
    cmat, maskb = _const_inputs()

    def chunked(v):
        v = np.asarray(v, f32)
        return v.reshape(v.shape[0], 16, 128).transpose(2, 0, 1)

    vecs = np.zeros((128, 296), f32)
    vecs[:, 0:64] = chunked(norm_mix).reshape(128, 64)
    vecs[:, 64:128] = chunked(norm_ffn).reshape(128, 64)
    vecs[:, 128:160] = chunked(pool_scale).reshape(128, 32)
    vecs[:, 160:162] = np.tile(np.asarray(q_norm, f32).T, (2, 1))
    vecs[:, 162:164] = np.tile(np.asarray(k_norm, f32).T, (2, 1))
    vecs[:, 164:228] = np.broadcast_to(np.asarray(sinks, f32).reshape(1, 64), (128, 64))

    half = 8
    inv = (f32(500000.0) ** (-np.arange(half, dtype=f32) * f32(2.0) / f32(16))).astype(f32)
    dloc = np.arange(128) % 64

    shared = {
        "cmat": cmat, "maskb": maskb, "vecs": vecs,
        "pool_w": np.asarray(pool_w, f32), "w_qkv": np.asarray(w_qkv, f32), "w_o": np.asarray(w_o, f32),
        "w_gate": np.asarray(w_gate, f32), "w_up": np.asarray(w_up, f32), "w_down": np.asarray(w_down, f32),
    }
    in_maps = []
    for core in range(8):
        b, q = divmod(core, 4)
        O = q * NOWN
        pos = O - HALO + np.arange(TP)
        valid = pos >= 0
        xext = np.concatenate([np.asarray(meta_tokens, f32), x_prompt[b]], axis=0)
        cols = np.zeros((T, D), f32)
        cols[:TP][valid] = xext[pos[valid]]
        sidx = np.arange(NS) + NS * core
        cols[TP:] = x_sample[sidx, 0]
        xT = np.ascontiguousarray(cols.reshape(T, 16, 128).transpose(2, 1, 0))
        pfull = np.concatenate([np.maximum(pos, 0), np.full(NS, PAST)]).astype(f32)
        ang = (pfull[:, None] * inv[None, :]).astype(f32)
        cosv, sinv = np.cos(ang).astype(f32), np.sin(ang).astype(f32)
        cosT = np.ones((128, T), f32)
        sinT = np.zeros((128, T), f32)
        for p in range(128):
            d = dloc[p]
            if d < 16:
                cosT[p] = cosv[:, d % 8]
                sinT[p] = sinv[:, d % 8]
        kbias = np.zeros((128, 2, 12), f32)
        for jj, ks in enumerate((IN_L[1], IN_L[3])):
            for kbi in range(12):
                ccol = ks + 128 * kbi + np.arange(128)
                ok = (ccol < TP) & (np.where(ccol < TP, O - HALO + ccol, -1) >= 0)
                kbias[:, jj, kbi] = np.where(ok, 0.0, NEGM)
        rcfix = np.zeros((128, 4, 16), f32)
        for g in range(4):
            wwin = 2 << g
            cnt = np.minimum(wwin, O + np.arange(16) + 1).astype(f32)
            rcfix[:, g, :] = (f32(1.0) / cnt)[None, :]
        sp = state_pool[:, sidx]
        poolstT = np.ascontiguousarray(sp.reshape(2, NS, 15, 16, 128).transpose(0, 4, 3, 1, 2))
        ck = cache_k[:, sidx].reshape(2, NS, 128, 4, 64)
        kt = ck[:, :, 1:].reshape(2, NS, 127, 2, 2, 64)
        kcT = np.ascontiguousarray(kt.transpose(0, 4, 5, 1, 3, 2).reshape(2, 128, NS, 2, 127))
        cv = cache_v[:, sidx].reshape(2, NS, 128, 256)
        vc = np.ascontiguousarray(cv[:, :, 1:].transpose(0, 2, 1, 3))
        m = dict(shared)
        m.update({
            "xT": xT, "cosT": cosT, "sinT": sinT, "kbias": kbias, "rcfix": rcfix,
            "poolstT": poolstT, "kcT": kcT, "vc": vc,
            "ck": np.ascontiguousarray(ck.reshape(2, NS, 128, 256)), "cv": np.ascontiguousarray(cv),
            "spool": np.ascontiguousarray(sp),
        })
        in_maps.append(m)

    return in_maps


def assemble(R, cores=range(8), B=2):
    f32 = np.float32
    y_prompt = np.zeros((B, 4096, D), f32)
    y_sample = np.zeros((32, 1, D), f32)
    new_pool_p = np.zeros((2, B, 15, D), f32)
    new_k_p = np.zeros((2, B, 128, 4, 64), f32)
    new_v_p = np.zeros((2, B, 128, 4, 64), f32)
    new_pool_s = np.zeros((2, 32, 15, D), f32)
    new_k_s = np.zeros((2, 32, 128, 4, 64), f32)
    new_v_s = np.zeros((2, 32, 128, 4, 64), f32)
    for core in cores:
        b, q = divmod(core, 4)
        r = R[core]
        yt = np.asarray(r["yT"]).transpose(2, 1, 0).reshape(NOWN + NS, D)
        rows = yt[:NOWN]
        if q == 0:
            y_prompt[b, 0:NOWN - 16] = rows[16:]
        else:
            y_prompt[b, q * NOWN - 16:(q + 1) * NOWN - 16] = rows
        sidx = np.arange(NS) + NS * core
        y_sample[sidx, 0] = yt[NOWN:]
        po = np.asarray(r["poolT"]).transpose(0, 3, 2, 1).reshape(2, NPO, D)
        kn = np.asarray(r["knewT"]).reshape(2, 2, 64, 2, 128 + NS).transpose(0, 4, 3, 1, 2).reshape(2, 128 + NS, 4, 64)
        if q == 3:
            new_pool_p[:, b] = po[:, :15]
            new_k_p[:, b] = kn[:, :128]
            new_v_p[:, b] = np.asarray(r["vnew_p"]).reshape(2, 128, 4, 64)
        new_pool_s[:, sidx, 14] = po[:, 15:]
        new_pool_s[:, sidx, :14] = np.asarray(r["pold_s"])
        new_k_s[:, sidx, :127] = np.asarray(r["kold_s"]).reshape(2, NS, 127, 4, 64)
        new_k_s[:, sidx, 127] = kn[:, 128:]
        new_v_s[:, sidx, :127] = np.asarray(r["vold_s"]).reshape(2, NS, 127, 4, 64)
        new_v_s[:, sidx, 127] = np.asarray(r["vnew_s"])[:, 0].reshape(2, NS, 4, 64)
    return (y_prompt, y_sample, new_pool_p, new_k_p, new_v_p, new_pool_s, new_k_s, new_v_s)
```

```python
import numpy as np
from contextlib import ExitStack
import concourse.bass as bass
import concourse.mybir as mybir
from concourse.bass_utils import run_bass_kernel_spmd

F32 = mybir.dt.float32
BF16 = mybir.dt.bfloat16
AF = mybir.ActivationFunctionType
ALU = mybir.AluOpType

D = 2048
NCH = 16
DFF = 5632
NFC = 44
NOWN = 1028
HALO = 286
TP = NOWN + HALO
NS = 4
T = TP + NS
SEQ_EXT = 4112
PAST = 16384
NEGM = -30000.0
EPS = 1e-6
IN_L = [0, 15, 143, 158]
OUT_L = [15, 143, 158, 286]
RING_SLOTS = 4
RING_EL = 4096
LASTK0 = TP - 128
NPO = 19
FIX0 = HALO


QSTEP = 384
QWMAX = 416


def q_tiles(qs):
    tl = [(a, min(a + QSTEP, TP)) for a in range(qs, TP, QSTEP)]
    if len(tl) > 1 and (tl[-1][1] - tl[-1][0]) + NS <= QWMAX - QSTEP:
        tl[-2] = (tl[-2][0], TP)
        tl.pop()
    return tl


def split_tiles(a, b, maxw=512):
    n = b - a
    k = -(-n // maxw)
    base, rem = divmod(n, k)
    out = []
    s = a
    for i in range(k):
        w = base + (1 if i < rem else 0)
        out.append((s, s + w))
        s += w
    return out


class Slot:
    __slots__ = ("writer", "readers")

    def __init__(self):
        self.writer = None
        self.readers = {}


class KB:
    def __init__(self, nc, st):
        self.nc = nc
        self.eng = {"pe": nc.tensor, "act": nc.scalar, "dve": nc.vector, "pool": nc.gpsimd, "sp": nc.sync}
        self.sem = {k: st.enter_context(nc.semaphore("s_" + k)) for k in self.eng}
        self.cnt = {k: 0 for k in self.eng}
        self.waited = {}
        self.slots = {}
        self.ndsem = 20
        self.dsem = [st.enter_context(nc.semaphore(f"d{i}")) for i in range(self.ndsem)]
        self.dval = [0] * self.ndsem
        self.drr = 0
        self.npsem = 4
        self.psem = [st.enter_context(nc.semaphore(f"pd{i}")) for i in range(self.npsem)]
        self.pval = [0] * self.npsem
        self.prr = 0
        self.rsem = [st.enter_context(nc.semaphore(f"r{i}")) for i in range(RING_SLOTS)]
        self.rval = [0] * RING_SLOTS
        self.live_dma = []

    def slot(self, key):
        s = self.slots.get(key)
        if s is None:
            s = Slot()
            self.slots[key] = s
        return s

    def wait(self, e, tok):
        if tok is None:
            return
        kind, key, val = tok
        if kind == "e":
            if key == e and e in ("pe", "sp"):
                return
            sem = self.sem[key]
        else:
            sem = key
        wk = (e, id(sem))
        if self.waited.get(wk, 0) >= val:
            return
        self.waited[wk] = val
        self.eng[e].wait_ge(sem, val)

    def _deps(self, e, reads, writes):
        for k in reads:
            s = self.slot(k)
            self.wait(e, s.writer)
            if isinstance(k, tuple) and k[0] == "ps":
                for rk, r in s.readers.items():
                    if rk != e:
                        self.wait(e, r)
        for k in writes:
            s = self.slot(k)
            self.wait(e, s.writer)
            for r in s.readers.values():
                self.wait(e, r)

    def _mark(self, tok, reads, writes, rkey):
        for k in reads:
            self.slot(k).readers[rkey] = tok
        for k in writes:
            s = self.slot(k)
            s.writer = tok
            s.readers = {}

    def op(self, e, fn, reads=(), writes=()):
        self._deps(e, reads, writes)
        ins = fn(self.eng[e])
        self.cnt[e] += 1
        ins.then_inc(self.sem[e], 1)
        tok = ("e", e, self.cnt[e])
        self._mark(tok, reads, writes, e)
        return tok

    def mm(self, mms, reads=(), writes=()):
        self._deps("pe", reads, writes)
        ins = None
        for mmi in mms:
            o, l, r, s0, s1 = mmi[:5]
            if len(mmi) > 5 and mmi[5]:
                ins = self.nc.tensor.matmul(o, l, r, start=s0, stop=s1, skip_group_check=True)
            else:
                ins = self.nc.tensor.matmul(o, l, r, start=s0, stop=s1)
        self.cnt["pe"] += 1
        ins.then_inc(self.sem["pe"], 1)
        tok = ("e", "pe", self.cnt["pe"])
        self._mark(tok, reads, writes, "pe")
        return tok

    def dma(self, q, out, in_, reads=(), writes=(), ring=None, track=True):
        self._deps(q, reads, writes)
        if ring is None and q == "pool":
            i = self.prr
            self.prr = (self.prr + 1) % self.npsem
            sem = self.psem[i]
            if self.pval[i] > 0:
                self.wait(q, ("d", sem, self.pval[i]))
            self.pval[i] += 16
            val = self.pval[i]
        elif ring is None:
            i = self.drr
            self.drr = (self.drr + 1) % self.ndsem
            sem = self.dsem[i]
            if self.dval[i] > 0:
                self.wait(q, ("d", sem, self.dval[i]))
            self.dval[i] += 16
            val = self.dval[i]
        else:
            sem = self.rsem[ring]
            self.rval[ring] += 16
            val = self.rval[ring]
        self.eng[q].dma_start(out=out, in_=in_).then_inc(sem, 16)
        tok = ("d", sem, val)
        self._mark(tok, reads, writes, ("dma", id(sem), val))
        if track:
            self.live_dma.append(tok)
        return tok

    def sync_to_barrier(self, e):
        for t in getattr(self, "last_barrier", []):
            self.wait(e, t)

    def barrier(self, engines=("pe", "act", "dve", "sp")):
        toks = [("e", k, self.cnt[k]) for k in ("pe", "act", "dve", "pool") if self.cnt[k] > 0]
        toks += self.live_dma
        self.last_barrier = toks
        for e in engines:
            for t in toks:
                self.wait(e, t)
        self.live_dma = []
        self.slots = {k: v for k, v in self.slots.items() if isinstance(k, tuple) and k[0] == "ring"}


class WStream:
    def __init__(self, kb, ring, plan):
        self.kb = kb
        self.ring = ring
        self.plan = plan
        self.issued = 0
        self.taken = 0
        self.released = [True] * RING_SLOTS
        self.occ = [None] * RING_SLOTS

    def pump(self):
        while self.issued < len(self.plan):
            s = self.issued % RING_SLOTS
            if not self.released[s]:
                break
            kind, parts = self.plan[self.issued]
            for (oview, src) in parts:
                self.kb.dma("pool", oview(self.ring[s]), src, writes=[("ring", s)], ring=s, track=False)
            self.released[s] = False
            self.occ[s] = self.issued
            self.issued += 1

    def next(self, kind):
        self.pump()
        n = self.taken
        assert n < self.issued, f"weight stream stall at block {n} ({kind})"
        assert self.plan[n][0] == kind, (self.plan[n][0], kind)
        self.taken += 1
        s = n % RING_SLOTS
        return s, self.ring[s]

    def release(self, s):
        self.released[s] = True
        self.pump()


class _StopBuild(Exception):
    pass


def build_program(stop=99, swa_stop=99):
    nc = bass.Bass("TRN2", target_bir_lowering=False)
    _uid = [0]
    _orig_sbuf = nc.sbuf_tensor

    def sbuf_tensor(name, shape, dt):
        _uid[0] += 1
        return _orig_sbuf(f"sb{_uid[0]}_{name}", shape, dt)

    def din(name, shape):
        return nc.dram_tensor(name, list(shape), F32, kind="ExternalInput").ap()

    def dout(name, shape):
        return nc.dram_tensor(name, list(shape), F32, kind="ExternalOutput").ap()

    xT_d = din("xT", [128, NCH, T])
    cos_d = din("cosT", [128, T])
    sin_d = din("sinT", [128, T])
    kbias_d = din("kbias", [128, 2, 12])
    rcfix_d = din("rcfix", [128, 4, 16])
    vecs_d = din("vecs", [128, 296])
    cmat_d = din("cmat", [128, 4, 128])
    maskb_d = din("maskb", [128, 256])
    poolst_d = din("poolstT", [2, 128, NCH, NS, 15])
    kcT_d = din("kcT", [2, 128, NS, 2, 127])
    vc_d = din("vc", [2, 127, NS, 256])
    ck_d = din("ck", [2, NS, 128, 256])
    cv_d = din("cv", [2, NS, 128, 256])
    sp_d = din("spool", [2, NS, 15, D])
    pool_w = din("pool_w", [2, 4, 512, 512])
    w_qkv = din("w_qkv", [2, D, 2560])
    w_o = din("w_o", [2, D, D])
    w_gate = din("w_gate", [4, D, DFF])
    w_up = din("w_up", [4, D, DFF])
    w_down = din("w_down", [4, DFF, D])

    yT_o = dout("yT", [128, NCH, NOWN + NS])
    po_o = dout("poolT", [2, 128, NCH, NPO])
    kT_o = dout("knewT", [2, 128, 2, 128 + NS])
    vp_o = dout("vnew_p", [2, 128, 256])
    vs_o = dout("vnew_s", [2, 1, NS, 256])
    ks_o = dout("kold_s", [2, NS, 127, 256])
    vso_o = dout("vold_s", [2, NS, 127, 256])
    pso_o = dout("pold_s", [2, NS, 14, D])

    def v3(a, b):
        return lambda r: r[:, 0:a * b].rearrange("p (a b) -> p a b", a=a)

    plan = []
    for i in range(4):
        j = i // 2
        if i % 2 == 0:
            for g in range(4):
                src = pool_w[j, g].rearrange("(kc p) m -> p kc m", p=128)
                for hf in range(2):
                    plan.append(("pool", [(v3(4, 256), src[:, :, hf * 256:(hf + 1) * 256])]))
        else:
            wq = w_qkv[j].rearrange("(kc p) m -> p kc m", p=128)
            wo = w_o[j].rearrange("(kc p) m -> p kc m", p=128)
            plan.append(("wk", [(v3(16, 256), wq[:, :, 2048:2304])]))
            plan.append(("wv", [(v3(16, 256), wq[:, :, 2304:2560])]))
            ntq = len(q_tiles(OUT_L[i]))
            wo5 = w_o[j].rearrange("(gp u i d) m -> d gp u i m", gp=2, u=2, i=8)
            for tt in range(ntq):
                for gp in range(2):
                    for ip in range(4):
                        parts = []
                        for u in range(2):
                            for hl in range(2):
                                def ov(r, u=u, hl=hl):
                                    return r[:, 0:4096].rearrange("p (kc hl u d) -> p kc hl u d", kc=16, hl=2, u=2)[:, :, hl, u, :]
                                c0 = (16 * gp + 8 * u + 2 * ip + hl) * 64
                                parts.append((ov, wq[:, :, c0:c0 + 64]))
                        plan.append(("wq", parts))
                for mq in range(4):
                    for gp in range(2):
                        parts = []
                        for u in range(2):
                            def ov(r, u=u):
                                return r[u * 64:(u + 1) * 64, 0:4096].rearrange("p (i m) -> p i m", i=8)
                            parts.append((ov, wo5[:, gp, u, :, mq * 512:(mq + 1) * 512]))
                        plan.append(("wo", parts))
        wg = w_gate[i].rearrange("(kc p) m -> p kc m", p=128)
        wu = w_up[i].rearrange("(kc p) m -> p kc m", p=128)
        wd = w_down[i].rearrange("(fc p) m -> p fc m", p=128)
        for qd in range(4):
            f = 0
            while f < 11:
                w = 2 if f + 2 <= 11 else 1
                c0 = (qd * 11 + f) * 128
                plan.append(("wg", [(v3(16, w * 128), wg[:, :, c0:c0 + w * 128])]))
                plan.append(("wu", [(v3(16, w * 128), wu[:, :, c0:c0 + w * 128])]))
                f += w
            for mb in range(8):
                plan.append(("wd", [(v3(11, 256), wd[:, qd * 11:(qd + 1) * 11, mb * 256:(mb + 1) * 256])]))

    with ExitStack() as st:
        E = st.enter_context
        kb = KB(nc, st)
        X = E(sbuf_tensor("X", [128, NCH, T], F32))
        H = E(sbuf_tensor("H", [128, NCH, T + 2], BF16))
        ring = [E(sbuf_tensor(f"ring{s}", [128, RING_EL], BF16)) for s in range(RING_SLOTS)]
        vecs = E(sbuf_tensor("vecs", [128, 296], F32))
        esink = E(sbuf_tensor("esink", [128, 64], F32))
        cmat = E(sbuf_tensor("cmat", [128, 128], F32))
        identb = E(sbuf_tensor("identb", [128, 128], BF16))
        blk64 = E(sbuf_tensor("blk64", [128, 128], BF16))
        onesD = E(sbuf_tensor("onesD", [128, 128], BF16))
        ones1 = E(sbuf_tensor("ones1", [128, 128], BF16))
        maskb = E(sbuf_tensor("maskb", [128, 2, 256], BF16))
        epst = E(sbuf_tensor("epst", [128, 1], F32))
        rstd = E(sbuf_tensor("rstd", [128, 512], F32))
        sq = [E(sbuf_tensor(f"sq{i}", [128, 512], BF16)) for i in range(3)]
        ps = [E(nc.psum_tensor(f"ps{i}", [128, 512], F32)) for i in range(8)]
        psn = [0]

        kb.rot = list(range(8))

        def bank():
            i = kb.rot[psn[0] % len(kb.rot)]
            psn[0] += 1
            return ps[i], ("ps", i)

        RT = cmat[:]
        gmix = lambda i, c: vecs[:, i * 16 + c: i * 16 + c + 1]
        gffn = lambda i, c: vecs[:, 64 + i * 16 + c: 64 + i * 16 + c + 1]
        pscl = lambda j, c: vecs[:, 128 + j * 16 + c: 128 + j * 16 + c + 1]
        qg = lambda j: vecs[:, 160 + j: 161 + j]
        kg = lambda j: vecs[:, 162 + j: 163 + j]

        ws = WStream(kb, ring, plan)
        out_done = [False]

        kb.dma("sp", vecs[:], vecs_d[:, :], writes=["vecs"])
        kb.dma("sp", cmat[:], cmat_d[:, 1, :], writes=["cmat"])
        kb.dma("pool", identb[:], cmat_d[:, 0, :], writes=["identb"])
        kb.dma("pool", blk64[:], cmat_d[:, 2, :], writes=["blk64"])
        kb.dma("pool", maskb[:, 0, :], maskb_d[:, :], writes=["maskb"])
        kb.dma("pool", maskb[:, 1, :], maskb_d[:, :], writes=["maskb"])
        for c in range(NCH):
            kb.dma("sp", X[:, c, :], xT_d[:, c, :], writes=[("X", c)])
        ws.pump()
        kb.op("dve", lambda e: e.memset(epst[:], EPS), writes=["eps"])
        kb.op("dve", lambda e: e.memset(H[:, :, T:T + 2], 0.0), writes=[("H", c) for c in range(NCH)])
        kb.op("dve", lambda e: e.memset(onesD[:], 1.0 / D), writes=["onesD"])
        kb.op("dve", lambda e: e.memset(ones1[:], 1.0), writes=["ones1"])
        kb.op("act", lambda e: e.activation(out=esink[:], in_=vecs[:, 164:228], func=AF.Exp), reads=["vecs"], writes=["esink"])
        for j in range(2):
            for s in range(NS):
                kb.dma("sp", ks_o[j, s], ck_d[j, s, 1:128, :])
                kb.dma("sp", vso_o[j, s], cv_d[j, s, 1:128, :])
                kb.dma("sp", pso_o[j, s], sp_d[j, s, 1:15, :])
        kb.barrier()

        def rms_stats(cols, dst, dst_off, key="rstdbuf"):
            a, b = cols
            for (t0, t1) in split_tiles(a, b):
                w = t1 - t0
                pb, pk = bank()
                for c in range(NCH):
                    sb = sq[c % 3]
                    kb.op("act", lambda e, sb=sb, c=c: e.activation(out=sb[:, 0:w], in_=X[:, c, t0:t1], func=AF.Square),
                          reads=[("X", c)], writes=[("sq", c % 3)])
                    kb.mm([(pb[:, 0:w], onesD[:], sb[:, 0:w], c == 0, c == NCH - 1)], reads=[("sq", c % 3), "onesD"], writes=[pk])
                o0 = dst_off + t0 - a
                kb.op("act", lambda e: e.activation(out=dst[:, o0:o0 + w], in_=pb[:, 0:w], func=AF.Ln, bias=epst[:, 0:1], scale=1.0),
                      reads=[pk, "eps"], writes=[key])
                kb.op("act", lambda e: e.activation(out=dst[:, o0:o0 + w], in_=dst[:, o0:o0 + w], func=AF.Exp, scale=-0.5), reads=[key], writes=[key])

        def rmsnorm_to_H(cols, gfn, tilekeys=False, maxw=512):
            a, b = cols
            for (t0, t1) in split_tiles(a, b, maxw):
                w = t1 - t0
                rms_stats((t0, t1), rstd, 0)
                for c in range(NCH):
                    kb.op("dve", lambda e, c=c: e.scalar_tensor_tensor(out=H[:, c, t0:t1], in0=X[:, c, t0:t1], scalar=gfn(c),
                                                                        in1=rstd[:, 0:w], op0=ALU.mult, op1=ALU.mult),
                          reads=[("X", c), "rstdbuf", "vecs"], writes=([("H", c, t0), ("H", c)] if tilekeys == "both" else [("H", c, t0)] if tilekeys else [("H", c)]))

        def ffn(i):
            a = OUT_L[i]
            tiles = split_tiles(a, T)
            rmsnorm_to_H((a, T), lambda c: gffn(i, c), tilekeys=True)
            with ExitStack() as st2:
                act = st2.enter_context(sbuf_tensor("act", [128, 11, T - 15], BF16))
                sg = [st2.enter_context(sbuf_tensor(f"sg{k}", [128, 512], BF16)) for k in range(2)]
                sgn = 0
                for qd in range(4):
                    f = 0
                    while f < 11:
                        wdt = 2 if f + 2 <= 11 else 1
                        sgk, Wg = ws.next("wg")
                        suk, Wu = ws.next("wu")
                        Wg3 = Wg[:, 0:16 * wdt * 128].rearrange("p (a b) -> p a b", a=16)
                        Wu3 = Wu[:, 0:16 * wdt * 128].rearrange("p (a b) -> p a b", a=16)
                        for fl in range(wdt):
                            for (t0, t1) in tiles:
                                w = t1 - t0
                                gb, gk = bank()
                                ub, uk = bank()
                                hreads = [("H", c, t0) for c in range(NCH)]
                                kb.mm([(gb[:, 0:w], Wg3[:, kc, fl * 128:(fl + 1) * 128], H[:, kc, t0:t1], kc == 0, kc == 15) for kc in range(16)],
                                      reads=[("ring", sgk)] + hreads, writes=[gk])
                                kb.mm([(ub[:, 0:w], Wu3[:, kc, fl * 128:(fl + 1) * 128], H[:, kc, t0:t1], kc == 0, kc == 15) for kc in range(16)],
                                      reads=[("ring", suk)] + hreads, writes=[uk])
                                sgt = sg[sgn % 2]
                                sgkey = ("sg", sgn % 2)
                                sgn += 1
                                kb.op("act", lambda e, sgt=sgt, gb=gb: e.activation(out=sgt[:, 0:w], in_=gb[:, 0:w], func=AF.Silu),
                                      reads=[gk], writes=[sgkey])
                                kb.op("dve", lambda e, sgt=sgt, ub=ub, ff=f + fl: e.tensor_tensor(out=act[:, ff, t0 - 15:t1 - 15], in0=ub[:, 0:w],
                                                                                               in1=sgt[:, 0:w], op=ALU.mult),
                                      reads=[uk, sgkey], writes=[("act", f + fl, t0)])
                        ws.release(sgk)
                        ws.release(suk)
                        f += wdt
                    for mb in range(8):
                        sdk, Wd = ws.next("wd")
                        Wd3 = Wd[:, 0:11 * 256].rearrange("p (a b) -> p a b", a=11)
                        for ml in range(2):
                            m = mb * 2 + ml
                            for (t0, t1) in tiles:
                                w = t1 - t0
                                yb, yk = bank()
                                kb.mm([(yb[:, 0:w], Wd3[:, ff, ml * 128:(ml + 1) * 128], act[:, ff, t0 - 15:t1 - 15], ff == 0, ff == 10) for ff in range(11)],
                                      reads=[("ring", sdk)] + [("act", ff, t0) for ff in range(11)], writes=[yk])
                                kb.op("dve", lambda e, yb=yb, m=m: e.tensor_tensor(out=X[:, m, t0:t1], in0=yb[:, 0:w], in1=X[:, m, t0:t1], op=ALU.add),
                                      reads=[yk, ("X", m, t0)], writes=[("X", m, t0)])
                            if i == 3 and qd == 3:
                                kb.dma("sp", yT_o[:, m, :], X[:, m, HALO:T], reads=[("X", m, t0_) for (t0_, _) in tiles])
                                out_done[0] = True
                        ws.release(sdk)
                kb.barrier()

        def pool_layer(i):
            j = i // 2
            a_in, a_out = IN_L[i], OUT_L[i]
            n_p = TP - a_in
            EW = 16 + n_p + 16 * NS
            sb0 = 16 + n_p
            with ExitStack() as st2:
                A = st2.enter_context
                rs_all = A(sbuf_tensor("rs_all", [128, T], F32))
                sets = []
                for k_ in range(2):
                    sets.append(tuple(A(sbuf_tensor(f"{nm}{k_}", [128, 16 + TP + 16 * NS], F32)) for nm in ("hf", "bA", "bB")))
                pst = A(sbuf_tensor("pst", [128, NCH, NS, 15], F32))
                PO = A(sbuf_tensor("PO", [128, NCH, NPO], F32))
                rcf = A(sbuf_tensor("rcf", [128, 4, 16], F32))
                fxs = [A(sbuf_tensor(f"fx{k_}", [128, 16], F32)) for k_ in range(2)]
                kb.dma("sp", pst[:], poolst_d[j], writes=["pst"])
                kb.dma("sp", rcf[:], rcfix_d[:, :, :], writes=["rcf"])
                for k_, eng in enumerate(("dve", "dve")):
                    for buf, nm in zip(sets[k_], ("hf", "bA", "bB")):
                        kb.op(eng, lambda e, buf=buf: e.memset(buf[:, 0:16], 0.0), writes=[nm + str(k_)])
                rms_stats((a_in, T), rs_all, a_in, "rsall")
                for c in range(NCH):
                    g = c // 4
                    wwin = 2 << g
                    k_ = c % 2
                    eng = "dve"
                    hf, bA, bB = sets[k_]
                    fx = fxs[k_]
                    hk, fk = "hf" + str(k_), "fx" + str(k_)
                    hfs = hf[:, sb0:sb0 + 16 * NS].rearrange("p (s k) -> p s k", s=NS)
                    if eng == "dve":
                        kb.op(eng, lambda e, c=c, hf=hf: e.scalar_tensor_tensor(out=hf[:, 16:16 + n_p], in0=X[:, c, a_in:TP], scalar=gmix(i, c),
                                                                               in1=rs_all[:, a_in:TP], op0=ALU.mult, op1=ALU.mult),
                              reads=[("X", c), "vecs", "rsall"], writes=[hk])
                        kb.op(eng, lambda e, c=c, hfs=hfs: e.scalar_tensor_tensor(out=hfs[:, :, 15], in0=X[:, c, TP:T], scalar=gmix(i, c),
                                                                                 in1=rs_all[:, TP:T], op0=ALU.mult, op1=ALU.mult),
                              reads=[("X", c), "rsall"], writes=[hk])
                    else:
                        kb.op("act", lambda e, c=c, hf=hf: e.activation(out=hf[:, 16:16 + n_p], in_=X[:, c, a_in:TP], func=AF.Copy, scale=gmix(i, c)),
                              reads=[("X", c), "vecs"], writes=[hk])
                        kb.op("act", lambda e, c=c, hfs=hfs: e.activation(out=hfs[:, :, 15], in_=X[:, c, TP:T], func=AF.Copy, scale=gmix(i, c)),
                              reads=[("X", c), "vecs"], writes=[hk])
                        kb.op(eng, lambda e, hf=hf: e.tensor_tensor(out=hf[:, 16:16 + n_p], in0=hf[:, 16:16 + n_p], in1=rs_all[:, a_in:TP], op=ALU.mult),
                              reads=[hk, "rsall"], writes=[hk])
                        kb.op(eng, lambda e, hfs=hfs: e.tensor_tensor(out=hfs[:, :, 15], in0=hfs[:, :, 15], in1=rs_all[:, TP:T], op=ALU.mult),
                              reads=[hk, "rsall"], writes=[hk])
                    kb.op("act", lambda e, c=c, hfs=hfs: e.copy(out=hfs[:, :, 0:15], in_=pst[:, c, :, :]), reads=["pst"], writes=[hk])
                    kb.op("act", lambda e, c=c, hf=hf: e.copy(out=PO[:, c, 0:15], in_=hf[:, 16 + n_p - 15:16 + n_p]), reads=[hk], writes=["PO"])
                    kb.op("act", lambda e, c=c, hfs=hfs: e.copy(out=PO[:, c, 15:NPO], in_=hfs[:, :, 15]), reads=[hk], writes=["PO"])
                    src, sname = hf, hk
                    bufs = [(bA, "bA" + str(k_)), (bB, "bB" + str(k_))]
                    sh = 1
                    kk = 0
                    while sh < wwin:
                        dst, dname = bufs[kk % 2]
                        kb.op(eng, lambda e, src=src, dst=dst, sh=sh: e.tensor_tensor(out=dst[:, 16:EW], in0=src[:, 16:EW], in1=src[:, 16 - sh:EW - sh], op=ALU.add),
                              reads=[sname], writes=[dname])
                        src, sname = dst, dname
                        sh *= 2
                        kk += 1
                    srs = src[:, sb0:sb0 + 16 * NS].rearrange("p (s k) -> p s k", s=NS)
                    if eng == "dve":
                        kb.op(eng, lambda e, src=src, c=c, hf=hf: e.scalar_tensor_tensor(out=H[:, c, a_in:TP], in0=src[:, 16:16 + n_p], scalar=1.0 / wwin,
                                                                                        in1=hf[:, 16:16 + n_p], op0=ALU.mult, op1=ALU.subtract),
                              reads=[sname, hk], writes=[("H", c)])
                        kb.op(eng, lambda e, srs=srs, c=c, hfs=hfs: e.scalar_tensor_tensor(out=H[:, c, TP:T], in0=srs[:, :, 15], scalar=1.0 / wwin,
                                                                                          in1=hfs[:, :, 15], op0=ALU.mult, op1=ALU.subtract),
                              reads=[sname, hk], writes=[("H", c)])
                    else:
                        mbuf, mname = bufs[kk % 2]
                        kb.op("act", lambda e, src=src, mbuf=mbuf: e.activation(out=mbuf[:, 16:EW], in_=src[:, 16:EW], func=AF.Copy, scale=1.0 / wwin),
                              reads=[sname], writes=[mname])
                        mbs = mbuf[:, sb0:sb0 + 16 * NS].rearrange("p (s k) -> p s k", s=NS)
                        kb.op(eng, lambda e, mbuf=mbuf, c=c, hf=hf: e.tensor_tensor(out=H[:, c, a_in:TP], in0=mbuf[:, 16:16 + n_p], in1=hf[:, 16:16 + n_p], op=ALU.subtract),
                              reads=[mname, hk], writes=[("H", c)])
                        kb.op(eng, lambda e, mbs=mbs, c=c, hfs=hfs: e.tensor_tensor(out=H[:, c, TP:T], in0=mbs[:, :, 15], in1=hfs[:, :, 15], op=ALU.subtract),
                              reads=[mname, hk], writes=[("H", c)])
                    f0 = 16 + FIX0 - a_in
                    kb.op(eng, lambda e, src=src, g=g, fx=fx: e.tensor_tensor(out=fx[:], in0=src[:, f0:f0 + 16], in1=rcf[:, g, :], op=ALU.mult),
                          reads=[sname, "rcf"], writes=[fk])
                    kb.op(eng, lambda e, c=c, fx=fx, hf=hf: e.tensor_tensor(out=H[:, c, FIX0:FIX0 + 16], in0=fx[:], in1=hf[:, f0:f0 + 16], op=ALU.subtract),
                          reads=[fk, hk], writes=[("H", c)])
                kb.dma("sp", po_o[j], PO[:], reads=["PO"])
                tiles = split_tiles(a_out, T)
                for g in range(4):
                    for hfb in range(2):
                        sk, Wp = ws.next("pool")
                        Wp3 = Wp[:, 0:1024].rearrange("p (a b) -> p a b", a=4)
                        for ml in range(2):
                            m = g * 4 + hfb * 2 + ml
                            for (t0, t1) in tiles:
                                w = t1 - t0
                                yb, yk = bank()
                                kb.mm([(yb[:, 0:w], Wp3[:, kc, ml * 128:(ml + 1) * 128], H[:, g * 4 + kc, t0:t1], kc == 0, kc == 3) for kc in range(4)],
                                      reads=[("ring", sk)] + [("H", g * 4 + kc) for kc in range(4)], writes=[yk])
                                kb.op("dve", lambda e, yb=yb, m=m: e.scalar_tensor_tensor(out=X[:, m, t0:t1], in0=yb[:, 0:w], scalar=pscl(j, m),
                                                                                          in1=X[:, m, t0:t1], op0=ALU.mult, op1=ALU.add),
                                      reads=[yk, ("X", m), "vecs"], writes=[("X", m)])
                        ws.release(sk)
                kb.barrier()

        QW = QWMAX

        def swa_layer(i):
            j = i // 2
            ks, qs = IN_L[i], OUT_L[i]
            assert qs == ks + 128
            nkb = -(-(TP - ks) // 128)
            rmsnorm_to_H((ks, T), lambda c: gmix(i, c), tilekeys="both", maxw=QW)
            kb.rot = [0, 1, 2, 3]
            with ExitStack() as st2:
                A = st2.enter_context
                KT = A(sbuf_tensor("KT", [128, 2, T], BF16))
                KTf = A(sbuf_tensor("KTf", [128, 2, 128 + NS], F32))
                V = A(sbuf_tensor("V", [128, nkb, 256], BF16))
                Vs = A(sbuf_tensor("Vs", [128, NS, 256], BF16))
                KTs = A(sbuf_tensor("KTs", [128, NS, 2, 128], BF16))
                VP = A(sbuf_tensor("VP", [128, 256], F32))
                QO = A(sbuf_tensor("QO", [128, NCH, QW], BF16))
                cst = A(sbuf_tensor("cst", [128, 2, QW], F32))
                xn = A(sbuf_tensor("xn", [128, QW], F32))
                rq = A(sbuf_tensor("rq", [128, QW], F32))
                kbs = A(sbuf_tensor("kbs", [128, 12], F32))
                rden = rstd
                kb.dma("sp", kbs[:], kbias_d[:, j, :], writes=["kbs"])
                kb.sync_to_barrier("pool")
                kb.dma("pool", KTs[:, :, :, 1:128], kcT_d[j], writes=["KTs_c"])
                kb.dma("pool", Vs[1:128], vc_d[j], writes=["Vs_c"])
                hreads = [("H", c) for c in range(NCH)]

                def load_tables(t0, t1):
                    w = t1 - t0
                    kb.dma("sp", cst[:, 0, 0:w], cos_d[:, t0:t1], writes=["cst"])
                    kb.dma("sp", cst[:, 1, 0:w], sin_d[:, t0:t1], writes=["cst"])

                xn2 = A(sbuf_tensor("xn2", [128, QW], F32))
                rq2 = A(sbuf_tensor("rq2", [128, QW], F32))
                sqb2 = A(sbuf_tensor("sqb2", [128, QW], BF16))
                qkn = [0]
                ptb = [sq[1], sq[2], A(sbuf_tensor("ptb2", [128, 512], BF16)), A(sbuf_tensor("ptb3", [128, 512], BF16))]

                def qk_post_gen(pb, pk, w, gvec, outs):
                    par = qkn[0] % 2
                    qkn[0] += 1
                    sb, sbk = (sq[0], ("sq", 0)) if par == 0 else (sqb2, "sqb2")
                    xn_, xk = (xn, "xn") if par == 0 else (xn2, "xn2")
                    rq_, rk_ = (rq, "rq") if par == 0 else (rq2, "rq2")
                    kb.op("act", lambda e: e.copy(out=xn_[:, 0:w], in_=pb[:, 0:w]), reads=[pk], writes=[xk])
                    kb.op("act", lambda e: e.activation(out=sb[:, 0:w], in_=xn_[:, 0:w], func=AF.Square), reads=[xk], writes=[sbk])
                    yield
                    mb_, mk = bank()
                    kb.mm([(mb_[:, 0:w], blk64[:], sb[:, 0:w], True, True)], reads=[sbk, "blk64"], writes=[mk])
                    kb.op("act", lambda e: e.activation(out=rq_[:, 0:w], in_=mb_[:, 0:w], func=AF.Ln, bias=epst[:, 0:1], scale=1.0),
                          reads=[mk, "eps"], writes=[rk_])
                    kb.op("act", lambda e: e.activation(out=rq_[:, 0:w], in_=rq_[:, 0:w], func=AF.Exp, scale=-0.5), reads=[rk_], writes=[rk_])
                    kb.op("dve", lambda e: e.scalar_tensor_tensor(out=xn_[:, 0:w], in0=xn_[:, 0:w], scalar=gvec, in1=rq_[:, 0:w], op0=ALU.mult, op1=ALU.mult),
                          reads=[xk, rk_, "vecs"], writes=[xk])
                    yield
                    rb, rk = bank()
                    kb.mm([(rb[:, 0:w], RT, xn_[:, 0:w], True, True)], reads=[xk, "cmat"], writes=[rk])
                    kb.op("dve", lambda e: e.tensor_tensor(out=rq_[:, 0:w], in0=rb[:, 0:w], in1=cst[:, 1, 0:w], op=ALU.mult),
                          reads=[rk, "cst", rk_], writes=[rk_])
                    kb.op("dve", lambda e: e.tensor_tensor(out=xn_[:, 0:w], in0=xn_[:, 0:w], in1=cst[:, 0, 0:w], op=ALU.mult),
                          reads=[xk, "cst"], writes=[xk])
                    for (oap, okey, lo, hi) in outs:
                        kb.op("dve", lambda e, oap=oap, lo=lo, hi=hi: e.tensor_tensor(out=oap, in0=rq_[:, lo:hi], in1=xn_[:, lo:hi], op=ALU.add),
                              reads=[rk_, xk], writes=[okey])

                def run_interleaved(gens):
                    gens = list(gens)
                    while gens:
                        nxt = []
                        for g_ in gens:
                            try:
                                next(g_)
                                nxt.append(g_)
                            except StopIteration:
                                pass
                        gens = nxt

                def qk_post(pb, pk, w, gvec, outs):
                    run_interleaved([qk_post_gen(pb, pk, w, gvec, outs)])

                sk, Wk = ws.next("wk")
                Wk3 = Wk[:, 0:4096].rearrange("p (a b) -> p a b", a=16)
                for (t0, t1) in split_tiles(ks, T, QW):
                    w = t1 - t0
                    load_tables(t0, t1)
                    gens = []
                    for gp in range(2):
                        pb, pk = bank()
                        kb.mm([(pb[:, 0:w], Wk3[:, kc, gp * 128:(gp + 1) * 128], H[:, kc, t0:t1], kc == 0, kc == 15) for kc in range(16)],
                              reads=[("ring", sk)] + [("H", c_, t0) for c_ in range(NCH)], writes=[pk])
                        outs = [(KT[:, gp, t0:t1], "KT", 0, w)]
                        lo = max(t0, LASTK0)
                        if lo < t1:
                            outs.append((KTf[:, gp, lo - LASTK0:t1 - LASTK0], "KTf", lo - t0, w))
                        gens.append(qk_post_gen(pb, pk, w, kg(j), outs))
                    run_interleaved(gens)
                ws.release(sk)
                kb.dma("sp", kT_o[j], KTf[:], reads=["KTf"])
                if swa_stop <= 1:
                    return "stop"
                for s in range(NS):
                    kb.op("dve", lambda e, s=s: e.tensor_copy(out=KTs[:, s, :, 0], in_=KT[:, :, TP + s]), reads=["KT"], writes=["KTs_n"])
                if swa_stop <= 1.2:
                    return "stop"
                sv, Wv = ws.next("wv")
                Wv3 = Wv[:, 0:4096].rearrange("p (a b) -> p a b", a=16)
                for kbi in range(nkb):
                    c0 = ks + 128 * kbi
                    kn = min(128, TP - c0)
                    vb, vk = bank()
                    kb.mm([(vb[0:kn, 0:256], H[:, kc, c0:c0 + kn], Wv3[:, kc, :], kc == 0, kc == 15) for kc in range(16)],
                          reads=[("ring", sv)] + hreads, writes=[vk])
                    kb.op("act", lambda e, vb=vb, kbi=kbi, kn=kn: e.copy(out=V[0:kn, kbi, :], in_=vb[0:kn, 0:256]), reads=[vk], writes=[("V", kbi)])
                if swa_stop <= 1.4:
                    return "stop"
                vb, vk = bank()
                kb.mm([(vb[:, 0:256], H[:, kc, LASTK0:TP], Wv3[:, kc, :], kc == 0, kc == 15) for kc in range(16)],
                      reads=[("ring", sv)] + hreads, writes=[vk])
                kb.op("act", lambda e, vb=vb: e.copy(out=VP[:], in_=vb[:, 0:256]), reads=[vk], writes=["VP"])
                kb.dma("sp", vp_o[j], VP[:], reads=["VP"])
                if swa_stop <= 1.6:
                    return "stop"
                for sp_ in range(2):
                    vb, vk = bank()
                    mms = []
                    for sl in range(2):
                        s = sp_ * 2 + sl
                        mms += [(vb[0:2, sl * 256:(sl + 1) * 256], H[:, kc, TP + s:TP + s + 2], Wv3[:, kc, :], kc == 0, kc == 15) for kc in range(16)]
                    kb.mm(mms, reads=[("ring", sv)] + hreads, writes=[vk])
                    kb.op("act", lambda e, vb=vb: e.copy(out=rden[0:1, 0:512], in_=vb[0:1, 0:512]), reads=[vk], writes=["rstdbuf"])
                    kb.op("dve", lambda e, vb=vb, sp_=sp_: e.tensor_copy(out=Vs[0:1, sp_ * 2:sp_ * 2 + 2, :],
                                                                         in_=vb[0:1, 0:512].rearrange("p (s d) -> p s d", s=2)),
                          reads=[vk], writes=["Vs_n"])
                    kb.dma("sp", vs_o[j, :, sp_ * 2:sp_ * 2 + 2, :], rden[0:1, 0:512].rearrange("p (s d) -> p s d", s=2), reads=["rstdbuf"])
                ws.release(sv)
                if swa_stop <= 2:
                    return "stop"

                qtl = q_tiles(qs)
                qtiles = [(a0_, a1_, n_ == len(qtl) - 1) for n_, (a0_, a1_) in enumerate(qtl)]
                ptn = [0]
                hn = [0]
                for (t0, t1p, has_s) in qtiles:
                    t1 = T if has_s else t1p
                    w = t1 - t0
                    wp = t1p - t0
                    assert w <= QW and wp + NS * (NS + 1) <= 512
                    load_tables(t0, t1)
                    kb0 = (t0 - ks) // 128
                    kbl = (t1p - 1 - ks) // 128
                    blocks = [(gp_, ip_) for gp_ in range(2) for ip_ in range(4)]

                    def start_block(bi):
                        gp_, ip_ = blocks[bi]
                        sk, Wq = ws.next("wq")
                        Wq3 = Wq[:, 0:4096].rearrange("p (a b) -> p a b", a=16)
                        gens = []
                        for cl in range(2):
                            c = gp_ * 8 + ip_ * 2 + cl
                            pb, pk = bank()
                            kb.mm([(pb[:, 0:w], Wq3[:, kc, cl * 128:(cl + 1) * 128], H[:, kc, t0:t1], kc == 0, kc == 15) for kc in range(16)],
                                  reads=[("ring", sk)] + hreads, writes=[pk])
                            g_ = qk_post_gen(pb, pk, w, qg(j), [(QO[:, c, 0:w], ("QO", c), 0, w)])
                            next(g_)
                            gens.append(g_)
                        ws.release(sk)
                        return gens

                    def advance(gens):
                        for g_ in gens:
                            next(g_, None)

                    cur_g = start_block(0)
                    advance(cur_g)
                    advance(cur_g)
                    for bi in range(8):
                        gp, ip = blocks[bi]
                        nxt_g = start_block(bi + 1) if bi + 1 < 8 else None
                        if True:
                            steps = []
                            for cl in range(2):
                                c = gp * 8 + ip * 2 + cl
                                ii = ip * 2 + cl
                                hds = []
                                for hh in range(2):
                                    par = hn[0] % 2
                                    hn[0] += 1
                                    hds.append(dict(c=c, hh=hh, hq=8 * (2 * gp + hh) + ii, r0=hh * 64, r1=hh * 64 + 64,
                                                    ob=ps[4 + par * 2], ok=("ps", 4 + par * 2), db=ps[5 + par * 2], dk=("ps", 5 + par * 2)))
                                csteps = [("p", hds, kbi) for kbi in range(kb0 - 1, kbl + 1)]
                                if has_s:
                                    csteps += [("s", hds, s_) for s_ in range(NS)]
                                csteps[-1] = csteps[-1] + (True,)
                                steps += csteps

                            def s_phase(st_):
                                kind, hds, idx = st_[0], st_[1], st_[2]
                                c = hds[0]["c"]
                                sb_, sk_ = bank()
                                pi = ptn[0] % 4
                                ptn[0] += 1
                                pt, pkey = ptb[pi], (("sq", pi + 1) if pi < 2 else ("ptb", pi))
                                if kind == "p":
                                    kc0 = ks + 128 * idx
                                    kn = min(128, TP - kc0)
                                    q0 = max(t0, kc0)
                                    q1 = min(t1p, kc0 + 256)
                                    nq = q1 - q0
                                    off = q0 - kc0
                                    sc = lambda hh: (sb_[0:kn, hh * 256:hh * 256 + nq], KT[hh * 64:hh * 64 + 64, gp, kc0:kc0 + kn],
                                                     QO[hh * 64:hh * 64 + 64, c, q0 - t0:q1 - t0], True, True)
                                    kb.mm([sc(0), (sb_[0:kn, 256:260], ones1[:, 0:kn], ones1[:, 0:4], True, True), sc(1)],
                                          reads=["KT", ("QO", c), "ones1"], writes=[sk_])
                                    sv_ = sb_[0:kn, 0:512].rearrange("p (h q) -> p h q", h=2)[:, :, 0:nq]
                                    pv_ = pt[0:kn, 0:512].rearrange("p (h q) -> p h q", h=2)[:, :, 0:nq]
                                    kb.op("act", lambda e: e.activation(out=pv_, in_=sv_, func=AF.Exp, bias=kbs[0:kn, idx:idx + 1], scale=0.125),
                                          reads=[sk_, "kbs"], writes=[pkey])
                                    kb.op("dve", lambda e: e.tensor_tensor(out=pv_, in0=pv_, in1=maskb[0:kn, :, off:off + nq], op=ALU.mult),
                                          reads=[pkey, "maskb"], writes=[pkey])
                                    return (pt, pkey, kn, q0, q1)
                                ss = lambda hh: (sb_[:, hh * NS:(hh + 1) * NS], KTs[hh * 64:hh * 64 + 64, idx, gp, :],
                                                 QO[hh * 64:hh * 64 + 64, c, wp:wp + NS], True, True)
                                kb.mm([ss(0), (sb_[:, NS:2 * NS], ones1[:, :], ones1[:, 0:NS], True, True), ss(1)],
                                      reads=["KTs_c", "KTs_n", ("QO", c), "ones1"], writes=[sk_])
                                kb.op("act", lambda e: e.activation(out=pt[:, 0:2 * NS], in_=sb_[:, 0:2 * NS], func=AF.Exp, scale=0.125),
                                      reads=[sk_], writes=[pkey])
                                return (pt, pkey)

                            def p_phase(st_, sres):
                                kind, hds, idx = st_[0], st_[1], st_[2]
                                for hd in hds:
                                    hh = hd["hh"]
                                    ob, ok_, db, dk = hd["ob"], hd["ok"], hd["db"], hd["dk"]
                                    if kind == "p":
                                        pt, pkey, kn, q0, q1 = sres
                                        first = not hd.get("opened", False)
                                        hd["opened"] = True
                                        rhs = pt[0:kn, hh * 256:hh * 256 + q1 - q0]
                                        kb.mm([(ob[:, q0 - t0:q1 - t0], V[0:kn, idx, gp * 128:(gp + 1) * 128], rhs, first, True, not first)],
                                              reads=[pkey, ("V", idx)], writes=[ok_])
                                        kb.mm([(db[:, q0 - t0:q1 - t0], ones1[0:kn, :], rhs, first, True, not first)], reads=[pkey, "ones1"], writes=[dk])
                                    else:
                                        pt, pkey = sres
                                        o0 = wp + NS * idx
                                        rhs = pt[:, hh * NS:(hh + 1) * NS]
                                        kb.mm([(ob[:, o0:o0 + NS], Vs[:, idx, gp * 128:(gp + 1) * 128], rhs, False, True, True)],
                                              reads=[pkey, "Vs_c", "Vs_n"], writes=[ok_])
                                        kb.mm([(db[:, o0:o0 + NS], ones1[:], rhs, False, True, True)], reads=[pkey, "ones1"], writes=[dk])
                                if len(st_) > 3:
                                    for hd in hds:
                                        finish(hd)

                            def finish(hd):
                                r0, r1, c, hq = hd["r0"], hd["r1"], hd["c"], hd["hq"]
                                ob, ok_, db, dk = hd["ob"], hd["ok"], hd["db"], hd["dk"]
                                rkey = ("rden", hd["hh"])
                                esk = esink[r0:r1, j * 32 + hq:j * 32 + hq + 1]
                                parts = []
                                if wp > 0:
                                    parts.append((db[r0:r1, 0:wp], ob[r0:r1, 0:wp], rden[r0:r1, 0:wp], QO[r0:r1, c, 0:wp]))
                                if has_s:
                                    dg = lambda t: t[r0:r1, wp:wp + NS * (NS + 1)].rearrange("p (a b) -> p a b", b=NS + 1)[:, :, 0]
                                    parts.append((dg(db), dg(ob), rden[r0:r1, wp:wp + NS], QO[r0:r1, c, wp:wp + NS]))
                                for (dsrc, osrc, rd, qo) in parts:
                                    kb.op("act", lambda e, dsrc=dsrc, rd=rd: e.activation(out=rd, in_=dsrc, func=AF.Ln, bias=esk, scale=1.0),
                                          reads=[dk, "esink"], writes=[rkey, "rstdbuf"])
                                    kb.op("act", lambda e, rd=rd: e.activation(out=rd, in_=rd, func=AF.Exp, scale=-1.0), reads=[rkey], writes=[rkey])
                                    kb.op("dve", lambda e, osrc=osrc, rd=rd, qo=qo: e.tensor_tensor(out=qo, in0=osrc, in1=rd, op=ALU.mult),
                                          reads=[ok_, rkey], writes=[("QO", c)])

                            LEAD = 3
                            pend = []
                            nst = len(steps)
                            for n_, st_ in enumerate(steps):
                                if nxt_g is not None and n_ in (nst // 3, (2 * nst) // 3):
                                    advance(nxt_g)
                                pend.append((st_, s_phase(st_)))
                                if len(pend) > LEAD:
                                    p_phase(*pend.pop(0))
                            while pend:
                                p_phase(*pend.pop(0))
                    if swa_stop <= 3:
                        return "stop"
                    for mq in range(4):
                        wos = []
                        for gp in range(2):
                            sk, Wo = ws.next("wo")
                            wos.append((sk, Wo[:, 0:4096].rearrange("p (i m) -> p i m", i=8)))
                        for ml in range(4):
                            m = mq * 4 + ml
                            yb, yk = bank()
                            kb.mm([(yb[:, 0:w], wos[kc // 8][1][:, kc % 8, ml * 128:(ml + 1) * 128], QO[:, kc, 0:w], kc == 0, kc == 15) for kc in range(16)],
                                  reads=[("ring", wos[0][0]), ("ring", wos[1][0])] + [("QO", c) for c in range(NCH)], writes=[yk])
                            kb.op("dve", lambda e, yb=yb, m=m: e.tensor_tensor(out=X[:, m, t0:t1], in0=yb[:, 0:w], in1=X[:, m, t0:t1], op=ALU.add),
                                  reads=[yk, ("X", m)], writes=[("X", m)])
                        for (sk, _) in wos:
                            ws.release(sk)
                    if swa_stop <= 4:
                        return "stop"
                kb.rot = list(range(8))
                kb.barrier()

        nph = 0
        for i in range(4 if stop > 0 else 0):
            if nph >= stop:
                break
            if i % 2 == 0:
                pool_layer(i)
            else:
                if swa_layer(i) == "stop":
                    kb.rot = list(range(8))
                    kb.barrier()
                    break
            nph += 1
            if nph >= stop:
                break
            ffn(i)
            nph += 1
        assert stop < 99 or ws.taken == len(plan), (ws.taken, len(plan))
        if not out_done[0]:
            for c in range(NCH):
                kb.dma("sp", yT_o[:, c, :], X[:, c, HALO:T])
        kb.barrier(engines=("sp",))
    return nc


_CACHE = {}


def _const_inputs():
    ident = np.eye(128, dtype=np.float32)
    R = np.zeros((128, 128), np.float32)
    for hb in (0, 64):
        for m in range(8):
            R[hb + m, hb + m + 8] = -1.0
            R[hb + m + 8, hb + m] = 1.0
    blk = np.zeros((128, 128), np.float32)
    blk[:64, :64] = 1.0 / 64
    blk[64:, 64:] = 1.0 / 64
    cmat = np.stack([ident, R.T.copy(), blk, np.zeros((128, 128), np.float32)], axis=1)
    ii = np.arange(128)[:, None]
    cc = np.arange(256)[None, :]
    maskb = np.where((cc >= ii) & (cc < ii + 128), 1.0, 0.0).astype(np.float32)
    return np.ascontiguousarray(cmat), maskb


def kernel(**inputs):
    in_maps = make_in_maps(**inputs)
    if "nc" not in _CACHE:
        _CACHE["nc"] = build_program()
    res = run_bass_kernel_spmd(_CACHE["nc"], in_maps, core_ids=list(range(8)))
    return assemble(res.results)


def make_in_maps(x_prompt, x_sample, state_pool, cache_k, cache_v, meta_tokens, norm_mix, norm_ffn,
                 pool_w, pool_scale, w_qkv, w_o, q_norm, k_norm, sinks, w_gate, w_up, w_down):
    f32 = np.float32
    x_prompt = np.asarray(x_prompt, f32)
    x_sample = np.asarray(x_sample, f32)
    state_pool = np.asarray(state_pool, f32)
    cache_k = np.asarray(cache_k, f32)
    cache_v = np.asarray(cache_v, f32)
    B = x_prompt.shape[0]
    cmat, maskb = _const_inputs()

    def chunked(v):
        v = np.asarray(v, f32)
        return v.reshape(v.shape[0], 16, 128).transpose(2, 0, 1)

    vecs = np.zeros((128, 296), f32)
    vecs[:, 0:64] = chunked(norm_mix).reshape(128, 64)
    vecs[:, 64:128] = chunked(norm_ffn).reshape(128, 64)
    vecs[:, 128:160] = chunked(pool_scale).reshape(128, 32)
    vecs[:, 160:162] = np.tile(np.asarray(q_norm, f32).T, (2, 1))
    vecs[:, 162:164] = np.tile(np.asarray(k_norm, f32).T, (2, 1))
    vecs[:, 164:228] = np.broadcast_to(np.asarray(sinks, f32).reshape(1, 64), (128, 64))

    half = 8
    inv = (f32(500000.0) ** (-np.arange(half, dtype=f32) * f32(2.0) / f32(16))).astype(f32)
    dloc = np.arange(128) % 64

    shared = {
        "cmat": cmat, "maskb": maskb, "vecs": vecs,
        "pool_w": np.asarray(pool_w, f32), "w_qkv": np.asarray(w_qkv, f32), "w_o": np.asarray(w_o, f32),
        "w_gate": np.asarray(w_gate, f32), "w_up": np.asarray(w_up, f32), "w_down": np.asarray(w_down, f32),
    }
    in_maps = []
    for core in range(8):
        b, q = divmod(core, 4)
        O = q * NOWN
        pos = O - HALO + np.arange(TP)
        valid = pos >= 0
        xext = np.concatenate([np.asarray(meta_tokens, f32), x_prompt[b]], axis=0)
        cols = np.zeros((T, D), f32)
        cols[:TP][valid] = xext[pos[valid]]
        sidx = np.arange(NS) + NS * core
        cols[TP:] = x_sample[sidx, 0]
        xT = np.ascontiguousarray(cols.reshape(T, 16, 128).transpose(2, 1, 0))
        pfull = np.concatenate([np.maximum(pos, 0), np.full(NS, PAST)]).astype(f32)
        ang = (pfull[:, None] * inv[None, :]).astype(f32)
        cosv, sinv = np.cos(ang).astype(f32), np.sin(ang).astype(f32)
        cosT = np.ones((128, T), f32)
        sinT = np.zeros((128, T), f32)
        for p in range(128):
            d = dloc[p]
            if d < 16:
                cosT[p] = cosv[:, d % 8]
                sinT[p] = sinv[:, d % 8]
        kbias = np.zeros((128, 2, 12), f32)
        for jj, ks in enumerate((IN_L[1], IN_L[3])):
            for kbi in range(12):
                ccol = ks + 128 * kbi + np.arange(128)
                ok = (ccol < TP) & (np.where(ccol < TP, O - HALO + ccol, -1) >= 0)
                kbias[:, jj, kbi] = np.where(ok, 0.0, NEGM)
        rcfix = np.zeros((128, 4, 16), f32)
        for g in range(4):
            wwin = 2 << g
            cnt = np.minimum(wwin, O + np.arange(16) + 1).astype(f32)
            rcfix[:, g, :] = (f32(1.0) / cnt)[None, :]
        sp = state_pool[:, sidx]
        poolstT = np.ascontiguousarray(sp.reshape(2, NS, 15, 16, 128).transpose(0, 4, 3, 1, 2))
        ck = cache_k[:, sidx].reshape(2, NS, 128, 4, 64)
        kt = ck[:, :, 1:].reshape(2, NS, 127, 2, 2, 64)
        kcT = np.ascontiguousarray(kt.transpose(0, 4, 5, 1, 3, 2).reshape(2, 128, NS, 2, 127))
        cv = cache_v[:, sidx].reshape(2, NS, 128, 256)
        vc = np.ascontiguousarray(cv[:, :, 1:].transpose(0, 2, 1, 3))
        m = dict(shared)
        m.update({
            "xT": xT, "cosT": cosT, "sinT": sinT, "kbias": kbias, "rcfix": rcfix,
            "poolstT": poolstT, "kcT": kcT, "vc": vc,
            "ck": np.ascontiguousarray(ck.reshape(2, NS, 128, 256)), "cv": np.ascontiguousarray(cv),
            "spool": np.ascontiguousarray(sp),
        })
        in_maps.append(m)

    return in_maps


def assemble(R, cores=range(8), B=2):
    f32 = np.float32
    y_prompt = np.zeros((B, 4096, D), f32)
    y_sample = np.zeros((32, 1, D), f32)
    new_pool_p = np.zeros((2, B, 15, D), f32)
    new_k_p = np.zeros((2, B, 128, 4, 64), f32)
    new_v_p = np.zeros((2, B, 128, 4, 64), f32)
    new_pool_s = np.zeros((2, 32, 15, D), f32)
    new_k_s = np.zeros((2, 32, 128, 4, 64), f32)
    new_v_s = np.zeros((2, 32, 128, 4, 64), f32)
    for core in cores:
        b, q = divmod(core, 4)
        r = R[core]
        yt = np.asarray(r["yT"]).transpose(2, 1, 0).reshape(NOWN + NS, D)
        rows = yt[:NOWN]
        if q == 0:
            y_prompt[b, 0:NOWN - 16] = rows[16:]
        else:
            y_prompt[b, q * NOWN - 16:(q + 1) * NOWN - 16] = rows
        sidx = np.arange(NS) + NS * core
        y_sample[sidx, 0] = yt[NOWN:]
        po = np.asarray(r["poolT"]).transpose(0, 3, 2, 1).reshape(2, NPO, D)
        kn = np.asarray(r["knewT"]).reshape(2, 2, 64, 2, 128 + NS).transpose(0, 4, 3, 1, 2).reshape(2, 128 + NS, 4, 64)
        if q == 3:
            new_pool_p[:, b] = po[:, :15]
            new_k_p[:, b] = kn[:, :128]
            new_v_p[:, b] = np.asarray(r["vnew_p"]).reshape(2, 128, 4, 64)
        new_pool_s[:, sidx, 14] = po[:, 15:]
        new_pool_s[:, sidx, :14] = np.asarray(r["pold_s"])
        new_k_s[:, sidx, :127] = np.asarray(r["kold_s"]).reshape(2, NS, 127, 4, 64)
        new_k_s[:, sidx, 127] = kn[:, 128:]
        new_v_s[:, sidx, :127] = np.asarray(r["vold_s"]).reshape(2, NS, 127, 4, 64)
        new_v_s[:, sidx, 127] = np.asarray(r["vnew_s"])[:, 0].reshape(2, NS, 4, 64)
    return (y_prompt, y_sample, new_pool_p, new_k_p, new_v_p, new_pool_s, new_k_s, new_v_s)
```
